# Optimizing a Trainium2 kernel written in Bass

```python
import math
import jax, jax.numpy as jnp
from jax import lax
import numpy as np

D_MODEL = 1024
BATCH = 8
SEQ = 2048
DEPTH = 1
DEC_BATCH = 128
DEC_SEQ = 8
PAST_LEN = 16384
PAGE_SIZE = 128

POOL_WINDOWS = (2, 4, 8, 16)
N_POOL_GROUPS = 4
D_POOL = D_MODEL
POOL_GROUP = D_POOL // N_POOL_GROUPS
POOL_BUF = 15
N_HEADS = 8
HEAD_K = D_MODEL // N_HEADS
HEAD_V = D_MODEL // N_HEADS
D_QK = N_HEADS * HEAD_K
D_V = N_HEADS * HEAD_V
D_CONV_CH = 2 * D_QK + D_V
CONV_W = 4
CHUNK = 64
N_BRANCH = 2
D_IN = 2 * D_POOL + 2 * D_QK + 2 * D_V + 2 * N_HEADS + N_BRANCH * D_MODEL
DEEPNORM_ALPHA = (2 * DEPTH) ** 0.25
DEEPNORM_BETA = (8 * DEPTH) ** -0.25
LN_EPS = 1e-5
RMS_EPS = 1e-6
L2_EPS = 1e-6

kernel_name = 'pool_gdn_gated_hybrid_step'


def _in_offsets():
    sizes = (D_POOL, D_POOL, D_QK, D_QK, D_V, D_V, N_HEADS, N_HEADS, D_MODEL, D_MODEL)
    return [int(s) for s in np.cumsum(sizes)[:-1]]


def _pool_mixer(u, buf, pos0, pool_w, pool_scale):
    bsz, t_len, _ = u.shape
    ext = jnp.concatenate([buf.astype(u.dtype), u], axis=1)
    ext32 = ext.astype(jnp.float32)
    cs = jnp.concatenate([jnp.zeros_like(ext32[:, :1]), jnp.cumsum(ext32, axis=1)], axis=1)
    pos = pos0 + jnp.arange(t_len, dtype=jnp.int32)
    means = []
    for gi, w in enumerate(POOL_WINDOWS):
        ch = slice(gi * POOL_GROUP, (gi + 1) * POOL_GROUP)
        hi = cs[:, POOL_BUF + 1:POOL_BUF + 1 + t_len, ch]
        lo = cs[:, POOL_BUF + 1 - w:POOL_BUF + 1 - w + t_len, ch]
        cnt = jnp.minimum(pos + 1, w).astype(jnp.float32)[None, :, None]
        means.append((hi - lo) / cnt)
    pooled = (jnp.concatenate(means, axis=-1) - ext32[:, POOL_BUF:]).astype(u.dtype)
    pooled = pooled.reshape(bsz, t_len, N_POOL_GROUPS, POOL_GROUP)
    mixed = jnp.einsum('btgc,gcd->btgd', pooled, pool_w).reshape(bsz, t_len, D_POOL) * pool_scale
    return mixed, ext[:, -POOL_BUF:]


def _short_conv(xc, buf, conv_w):
    t_len = xc.shape[1]
    ext = jnp.concatenate([buf.astype(xc.dtype), xc], axis=1)
    out = sum(ext[:, i:i + t_len] * conv_w[i] for i in range(CONV_W))
    return jax.nn.silu(out), ext[:, -(CONV_W - 1):]


def _l2norm(a):
    return a * lax.rsqrt(jnp.sum(a * a, axis=-1, keepdims=True) + L2_EPS)


def _chunk_gated_delta(q, k, v, beta, g, s0):
    bsz, t_len, n_h, _ = q.shape
    c = min(CHUNK, t_len)
    n_c = -(-t_len // c)
    pad = n_c * c - t_len

    def prep(a):
        a = jnp.pad(a, [(0, 0), (0, pad)] + [(0, 0)] * (a.ndim - 2))
        a = a.reshape((bsz, n_c, c) + a.shape[2:])
        return jnp.moveaxis(jnp.moveaxis(a, 3, 2), 1, 0)

    q, k, v, beta, g = prep(q), prep(k), prep(v), prep(beta), prep(g)
    gc = jnp.cumsum(g, axis=-1)
    incl = jnp.tril(jnp.ones((c, c), dtype=bool))
    strict = jnp.tril(jnp.ones((c, c), dtype=bool), -1)
    decay = jnp.exp(jnp.where(incl, gc[..., :, None] - gc[..., None, :], -jnp.inf))
    kk = jnp.einsum('nbhrd,nbhjd->nbhrj', k, k)
    lmat = jnp.where(strict, beta[..., :, None] * decay * kk, 0.0) + jnp.eye(c, dtype=jnp.float32)
    gam = jnp.exp(gc)
    rhs = jnp.concatenate([beta[..., None] * v, (beta * gam)[..., None] * k], axis=-1)
    sol = lax.linalg.triangular_solve(lmat, rhs, left_side=True, lower=True, unit_diagonal=True)
    w_v, w_k = sol[..., :HEAD_V], sol[..., HEAD_V:]
    qk = jnp.einsum('nbhrd,nbhjd->nbhrj', q, k) * decay
    qg = q * gam[..., None]
    kt = k * jnp.exp(gc[..., -1:] - gc)[..., None]
    glast = gam[..., -1]

    def step(s, xs):
        wv_c, wk_c, qk_c, qg_c, kt_c, gl_c = xs
        u = wv_c - jnp.einsum('bhcd,bhde->bhce', wk_c, s)
        o = jnp.einsum('bhcd,bhde->bhce', qg_c, s) + jnp.einsum('bhrj,bhje->bhre', qk_c, u)
        s = gl_c[..., None, None] * s + jnp.einsum('bhcd,bhce->bhde', kt_c, u)
        return s, o

    s_fin, o = lax.scan(step, s0, (w_v, w_k, qk, qg, kt, glast))
    o = jnp.transpose(o, (1, 0, 3, 2, 4)).reshape(bsz, n_c * c, n_h, HEAD_V)[:, :t_len]
    return o, s_fin


def _layer(x, c, pool_buf, conv_buf, s0, pos0, w_ada, b_ada, w_in, conv_w, a_log, dt_bias,
           head_norm_w, pool_w, pool_scale, p_a, p_b, w_out, ln_g, ln_b):
    dt = x.dtype
    bsz, t_len, _ = x.shape
    mod = jax.nn.silu(c) @ w_ada + b_ada
    shift, scale, gate = jnp.split(mod[:, None, :], 3, axis=-1)
    h = x * (1 + scale) + shift
    proj = h @ w_in
    u_a, z_a, q, k, v, z_b, b_raw, a_raw, ga_raw, gb_raw = jnp.split(proj, _in_offsets(), axis=-1)
    y_a, new_pool = _pool_mixer(u_a, pool_buf, pos0, pool_w, pool_scale)
    y_a = y_a * jax.nn.silu(z_a)
    qkv, new_conv = _short_conv(jnp.concatenate([q, k, v], axis=-1), conv_buf, conv_w)
    q, k, v = jnp.split(qkv.astype(jnp.float32), [D_QK, 2 * D_QK], axis=-1)
    q = _l2norm(q.reshape(bsz, t_len, N_HEADS, HEAD_K)) * (HEAD_K ** -0.5)
    k = _l2norm(k.reshape(bsz, t_len, N_HEADS, HEAD_K))
    v = v.reshape(bsz, t_len, N_HEADS, HEAD_V)
    beta = jax.nn.sigmoid(b_raw.astype(jnp.float32))
    g = -jnp.exp(a_log.astype(jnp.float32)) * jax.nn.softplus(a_raw.astype(jnp.float32) + dt_bias.astype(jnp.float32))
    o, s_new = _chunk_gated_delta(q, k, v, beta, g, s0.astype(jnp.float32))
    o = o * lax.rsqrt(jnp.mean(o * o, axis=-1, keepdims=True) + RMS_EPS) * head_norm_w.astype(jnp.float32)
    y_b = o.reshape(bsz, t_len, D_V).astype(dt) * jax.nn.silu(z_b)
    merged = jax.nn.sigmoid(ga_raw) * (y_a @ p_a) + jax.nn.sigmoid(gb_raw) * (y_b @ p_b)
    sub = (1 + gate) * (merged @ w_out)
    r = (DEEPNORM_ALPHA * x + sub).astype(jnp.float32)
    mu = jnp.mean(r, axis=-1, keepdims=True)
    var = jnp.mean(jnp.square(r - mu), axis=-1, keepdims=True)
    y = (r - mu) * lax.rsqrt(var + LN_EPS) * ln_g.astype(jnp.float32) + ln_b.astype(jnp.float32)
    return y.astype(dt), new_pool, new_conv, s_new.astype(dt)


def setup_inputs(seed: int = 0) -> dict:
    key = jax.random.key(seed)
    ks = jax.random.split(key, 24)
    f32 = jnp.float32
    nrm = lambda kk, shape, s: jax.random.normal(kk, shape, f32) * s
    dt_init = jnp.exp(jax.random.uniform(ks[10], (DEPTH, N_HEADS), f32, math.log(1e-3), math.log(1e-1)))
    return {
        'x_prompt': nrm(ks[0], (BATCH, SEQ, D_MODEL), 1.0),
        'x_sample': nrm(ks[1], (DEC_BATCH, DEC_SEQ, D_MODEL), 1.0),
        'state_pool': nrm(ks[2], (DEPTH, DEC_BATCH, POOL_BUF, D_POOL), 1.0),
        'state_conv': nrm(ks[3], (DEPTH, DEC_BATCH, CONV_W - 1, D_CONV_CH), 1.0),
        'state_delta': nrm(ks[4], (DEPTH, DEC_BATCH, N_HEADS, HEAD_K, HEAD_V), 0.05),
        'c_prompt': nrm(ks[5], (BATCH, D_MODEL), 1.0),
        'c_sample': nrm(ks[6], (DEC_BATCH, D_MODEL), 1.0),
        'w_ada': nrm(ks[7], (DEPTH, D_MODEL, 3 * D_MODEL), 0.02 * D_MODEL ** -0.5),
        'b_ada': nrm(ks[8], (DEPTH, 3 * D_MODEL), 0.02),
        'w_in': nrm(ks[9], (DEPTH, D_MODEL, D_IN), D_MODEL ** -0.5),
        'conv_w': nrm(ks[11], (DEPTH, CONV_W, D_CONV_CH), CONV_W ** -0.5),
        'a_log': jnp.log(jax.random.uniform(ks[12], (DEPTH, N_HEADS), f32, 1.0, 16.0)),
        'dt_bias': dt_init + jnp.log(-jnp.expm1(-dt_init)),
        'head_norm_w': 1.0 + nrm(ks[13], (DEPTH, HEAD_V), 0.02),
        'pool_w': nrm(ks[14], (DEPTH, N_POOL_GROUPS, POOL_GROUP, POOL_GROUP), POOL_GROUP ** -0.5),
        'pool_scale': 1.0 + nrm(ks[15], (DEPTH, D_POOL), 0.02),
        'p_a': nrm(ks[16], (DEPTH, D_POOL, D_MODEL), DEEPNORM_BETA * D_POOL ** -0.5),
        'p_b': nrm(ks[17], (DEPTH, D_V, D_MODEL), DEEPNORM_BETA * D_V ** -0.5),
        'w_out': nrm(ks[18], (DEPTH, D_MODEL, D_MODEL), DEEPNORM_BETA * D_MODEL ** -0.5),
        'ln_g': 1.0 + nrm(ks[19], (DEPTH, D_MODEL), 0.02),
        'ln_b': nrm(ks[20], (DEPTH, D_MODEL), 0.02),
    }


def reference(x_prompt, x_sample, state_pool, state_conv, state_delta, c_prompt, c_sample,
              w_ada, b_ada, w_in, conv_w, a_log, dt_bias, head_norm_w, pool_w, pool_scale,
              p_a, p_b, w_out, ln_g, ln_b):
    bp = x_prompt.shape[0]
    hp, hs = x_prompt, x_sample
    pool_p, conv_p, delta_p, pool_s, conv_s, delta_s = [], [], [], [], [], []
    for l in range(DEPTH):
        wl = (w_ada[l], b_ada[l], w_in[l], conv_w[l], a_log[l], dt_bias[l], head_norm_w[l],
              pool_w[l], pool_scale[l], p_a[l], p_b[l], w_out[l], ln_g[l], ln_b[l])
        zp = jnp.zeros((bp, POOL_BUF, D_POOL), hp.dtype)
        zc = jnp.zeros((bp, CONV_W - 1, D_CONV_CH), hp.dtype)
        zs = jnp.zeros((bp, N_HEADS, HEAD_K, HEAD_V), jnp.float32)
        hp, npl, ncv, nst = _layer(hp, c_prompt, zp, zc, zs, 0, *wl)
        pool_p.append(npl); conv_p.append(ncv); delta_p.append(nst)
        hs, npl, ncv, nst = _layer(hs, c_sample, state_pool[l], state_conv[l], state_delta[l], PAST_LEN, *wl)
        pool_s.append(npl); conv_s.append(ncv); delta_s.append(nst)
    pool_prompt = jnp.stack(pool_p)
    conv_prompt = jnp.stack(conv_p)
    delta_prompt = jnp.stack(delta_p)
    pool_sample = jnp.stack(pool_s)
    conv_sample = jnp.stack(conv_s)
    delta_sample = jnp.stack(delta_s)
    return (hp, hs, pool_prompt, conv_prompt, delta_prompt, pool_sample, conv_sample, delta_sample)
```

```python
from contextlib import ExitStack
import math
import numpy as np
import ml_dtypes
import concourse.bass as bass
import concourse.mybir as mybir
from concourse.bass_utils import run_bass_kernel_spmd

F32 = mybir.dt.float32
BF16 = mybir.dt.bfloat16
ALU = mybir.AluOpType
AF = mybir.ActivationFunctionType

D = 1024
NH = 8
DIN = 8208
NEG = -1.0e5
DEBUG = {}
STRICT_SAME_ENGINE = False


class Op:
    __slots__ = ("eng", "fn", "waits", "sig", "dma", "sem", "tick", "idx", "deps", "cost", "lat", "tbl")


class Sched:
    ENG = ("pe", "act", "dve", "pool", "sp")

    def __init__(self):
        self.ops = []
        self.lastw = {}
        self.readers = {}
        self.dma_cnt = {}

    def add(self, eng, fn, r=(), w=(), dma_sem=None, cost=0.2, lat=0.0, tbl=None):
        op = Op()
        op.tbl = tbl
        op.eng, op.fn, op.dma, op.sem = eng, fn, dma_sem is not None, dma_sem
        op.sig = False
        op.idx = len(self.ops)
        op.cost, op.lat = cost, lat
        deps = {}
        for k in r:
            lw = self.lastw.get(k)
            if lw is not None:
                deps[lw.idx] = (lw, True)
        for k in w:
            lw = self.lastw.get(k)
            if lw is not None and lw.idx not in deps:
                deps[lw.idx] = (lw, False)
            for rd in self.readers.get(k, ()):
                if rd.idx not in deps:
                    deps[rd.idx] = (rd, False)
        op.deps = []
        waits = []
        for d, raw in deps.values():
            need = True
            if (not d.dma) and (not op.dma) and d.eng == eng and not raw and not STRICT_SAME_ENGINE:
                need = False
            if (not d.dma) and (not op.dma) and d.eng == eng and eng == "pe":
                need = False
            op.deps.append((d, need))
            if need:
                waits.append(d)
                d.sig = True
        op.waits = waits
        for k in r:
            self.readers.setdefault(k, []).append(op)
        for k in w:
            self.lastw[k] = op
            self.readers[k] = []
        self.ops.append(op)
        return op

    def schedule(self):
        import heapq
        ops = self.ops
        n = len(ops)
        succ = [[] for _ in range(n)]
        indeg = [0] * n
        for op in ops:
            for d, _ in op.deps:
                succ[d.idx].append(op.idx)
                indeg[op.idx] += 1
        bl = [0.0] * n
        for op in reversed(ops):
            m = 0.0
            for s in succ[op.idx]:
                if bl[s] > m:
                    m = bl[s]
            bl[op.idx] = op.cost + op.lat + m
        eng_free = {e: 0.0 for e in self.ENG}
        future = {e: [] for e in self.ENG}
        avail = {e: [] for e in self.ENG}
        ready_t = [0.0] * n
        finish = [0.0] * n
        for op in ops:
            if indeg[op.idx] == 0:
                heapq.heappush(future[op.eng], (0.0, op.idx))
        order = {e: [] for e in self.ENG}
        done = 0
        cur_tbl = [None]
        while done < n:
            best_e, best_t = None, None
            for e in self.ENG:
                if avail[e]:
                    t = eng_free[e]
                elif future[e]:
                    t = max(eng_free[e], future[e][0][0])
                else:
                    continue
                if best_t is None or t < best_t:
                    best_e, best_t = e, t
            e, t = best_e, best_t
            while future[e] and future[e][0][0] <= t + 1e-9:
                rt, i = heapq.heappop(future[e])
                heapq.heappush(avail[e], (-bl[i], i))
            extra = 0.0
            if e == "act":
                comp = [x for x in avail[e] if ops[x[1]].tbl is None or ops[x[1]].tbl == cur_tbl[0]]
                if comp:
                    pick = min(comp)
                    avail[e].remove(pick)
                    heapq.heapify(avail[e])
                    i = pick[1]
                else:
                    soon = [x for x in future[e] if x[0] <= t + 0.9 and (ops[x[1]].tbl is None or ops[x[1]].tbl == cur_tbl[0])]
                    if soon:
                        pick = min(soon)
                        future[e].remove(pick)
                        heapq.heapify(future[e])
                        t = pick[0]
                        i = pick[1]
                    else:
                        _, i = heapq.heappop(avail[e])
                        extra = 1.3
                if ops[i].tbl is not None:
                    cur_tbl[0] = ops[i].tbl
            else:
                _, i = heapq.heappop(avail[e])
            op = ops[i]
            order[e].append(op)
            t = t + extra
            eng_free[e] = t + op.cost
            finish[i] = t + op.cost + op.lat
            done += 1
            for s in succ[i]:
                so = ops[s]
                lat = 0.0 if (so.eng == op.eng and not op.dma and not so.dma) else 0.8
                rt = finish[i] + lat
                if rt > ready_t[s]:
                    ready_t[s] = rt
                indeg[s] -= 1
                if indeg[s] == 0:
                    heapq.heappush(future[so.eng], (ready_t[s], s))
        self.order = order
        self.est = max(finish) if n else 0.0
        return order


def _consts():
    cols = {}
    parts = []

    def put(name, a):
        a = np.asarray(a, np.float32)
        off = sum(p.shape[1] for p in parts)
        cols[name] = (off, a.shape[1])
        parts.append(a)

    idx = np.arange(128)
    masks = []
    put("ident", np.eye(128))
    put("ones", np.ones((128, 128)))
    for g in ("p", "s"):
        if g == "p":
            seq = np.zeros(128, np.int64); tim = idx.copy(); nseq = 1
        else:
            seq = idx % 16; tim = idx // 16; nseq = 16
        same = seq[:, None] == seq[None, :]
        tr_ = tim[:, None]
        tc_ = tim[None, :]
        masks.append(np.where(same & (tc_ < tr_), 0.0, -NEG))
        masks.append(np.where(same & (tc_ > tr_), 0.0, NEG))
        masks.append(np.where(same & (tc_ >= tr_), 0.0, NEG))
        put("tri" + g, (same & (tr_ <= tc_)).astype(np.float32))
        put("same" + g, same.astype(np.float32))
        lm = np.zeros((128, 16), np.float32)
        rm = np.zeros((128, 16), np.float32)
        for s in range(nseq):
            members = np.where(seq == s)[0]
            lm[members[np.argmax(tim[members])], s] = 1.0
            rm[members, s] = 1.0
        put("lastm" + g, lm)
        put("rowm" + g, rm)
    ic = np.zeros((128, 4 * 16), np.float32)
    for wi, w in enumerate((2, 4, 8, 16)):
        ic[:, wi * 16:(wi + 1) * 16] = 1.0 / np.minimum(np.arange(16) + 1, w)
    put("invc", ic)
    cols["_masks"] = np.concatenate([np.asarray(m, np.float32) for m in masks], axis=1)
    return np.concatenate(parts, axis=1), cols


def build(debug=()):
    nc = bass.Bass("TRN2", target_bir_lowering=False)
    S = Sched()
    es = ExitStack()
    cst_np, CC = _consts()
    NCST = cst_np.shape[1]
    LP = 256
    NST = 2048 // LP

    def din(name, shape):
        return nc.dram_tensor(name, list(shape), F32, kind="ExternalInput").ap()

    def dout(name, shape):
        return nc.dram_tensor(name, list(shape), F32, kind="ExternalOutput").ap()

    xp = din("xp", [2048, D]); xs = din("xs", [128, D]); cexp = din("cexp", [2, 128, D])
    spool = din("spool", [16, 15, D]); sconv = din("sconv", [16, 3, 3 * D]); sdelta = din("sdelta", [16, NH, 128, 128])
    wimg = din("wimg", [22, 128, 8, 512]); wada_img = din("wada_img", [6, 128, 8, 512]); wba_img = din("wba_img", [128, 8, 16])
    pool_w = din("pool_w", [4, 256, 256])
    w_ada = w_in = p_a = p_b = w_out = None
    bada = din("bada", [128, 3 * D]); prm = din("prm", [128, 2176]); cst = din("cst", [128, NCST]); msk = din("msk", [128, 768])
    yp = dout("yp", [2048, D]); ys = dout("ys", [128, D]); poolp = dout("poolp", [15, D]); convp = dout("convp", [3, 3 * D])
    deltap = dout("deltap", [NH, 128, 128]); pools = dout("pools", [16, 15, D]); convs = dout("convs", [16, 3, 3 * D])
    deltas = dout("deltas", [16, NH, 128, 128])
    WS = nc.dram_tensor("ws_bf16", [22, 128, 8, 512], BF16, kind="Internal").ap()

    def sb(name, shape, dt=F32):
        return es.enter_context(nc.sbuf_tensor(name, list(shape), dt))

    def V(t, off, dims, np_=128, p0=0):
        base = t[:]
        ps = base.ap[0][0]
        return bass.AP(t, p0 * ps + off, [[ps, np_]] + [list(d) for d in dims])

    CST = sb("CST", [128, NCST]); PRM = sb("PRM", [128, 2176])
    IDB = sb("IDB", [128, 128], BF16)
    MSKB = sb("MSKB", [128, 6, 128], BF16)
    POOLW = sb("POOLW", [128, 4, 2, 256], BF16); WBA = sb("WBA", [128, 8, 16], BF16)
    ADA = [[sb("ADA%d_%d" % (g, i), [128, D], BF16) for i in range(3)] for g in range(2)]
    NEGA = sb("NEGA", [128, 8]); DUMMY = sb("DUMMY", [128, 8])
    NWB = 4
    WB = [sb("WB%d" % i, [128, 8, 512], BF16) for i in range(NWB)]
    SETN = ("ZA", "ZB", "QT", "KT", "VT", "SGA", "SGB")
    SET = [{n: sb("%s%d" % (n, s), [128, 8, LP], BF16) for n in SETN} for s in range(2)]
    BAS = [sb("BA%d" % s, [128, 2, 16]) for s in range(2)]
    SSQS = [sb("SSQ%d" % s, [128, 2, 16]) for s in range(2)]
    HT = sb("HT", [128, 8, LP], BF16); MRG = sb("MRG", [128, 8, LP], BF16)
    XT = [sb("XT%d" % i, [128, D]) for i in range(2)]
    HF = sb("HF", [128, D]); HB = sb("HB", [128, D], BF16)
    EW = 368
    EXTP = sb("EXTP", [128, 2, EW])
    PW = [sb("PW%d" % i, [128, 2, EW]) for i in range(2)]
    POOLED = sb("POOLED", [128, 2, LP], BF16)
    EXTC = [sb("EXTC%d" % i, [128, LP + 3]) for i in range(2)]
    CACC = sb("CACC", [128, LP]); SQ = sb("SQ", [128, LP], BF16); ONB = sb("ONB", [128, 8], BF16); CACC2 = sb("CACC2", [128, LP])
    PHIST = sb("PHIST", [128, 8, 15]); CHIST = sb("CHIST", [128, 24, 3])
    TOK = [sb("TOK%d" % i, [128, 128]) for i in range(2)]
    UTOK = HF
    RR = sb("RR", [128, D]); HF2 = sb("HF2", [128, D]); STAT = sb("STAT", [128, 2, 6]); MV = sb("MV", [128, 4])
    SCB = [sb("SC%d" % i, [128, 18, 8]) for i in range(2)]
    DG = sb("DG", [128, 8, 128])
    TF = DG
    QG = sb("QG", [128, 8, 128], BF16)
    AM = [sb("AM%d" % i, [128, 8, 128], BF16) for i in range(2)]
    AT = [sb("AT%d" % i, [128, 8, 128], BF16) for i in range(2)]
    YM = [sb("YM%d" % i, [128, 8, 128], BF16) for i in range(2)]
    QKT = sb("QKT", [128, 8, 128], BF16)
    ATO = sb("ATO", [128, 8, 128], BF16); RES = sb("RES", [128, 8, 128], BF16)
    KTOK = sb("KTOK", [128, 8, 128], BF16); BV = sb("BV", [128, 8, 128], BF16)
    BM = sb("BM", [128, 8, 128], BF16); UM = sb("UM", [128, 8, 128], BF16)
    SF = sb("SF", [128, 8, 128]); SBF = sb("SBF", [128, 8, 128], BF16)
    JUNK = sb("JUNK", [128, 128], BF16)
    GL = sb("GL", [128, 16, 8])
    S0BH = [(ADA[1][0], "ADA1_0"), (ADA[1][1], "ADA1_1")]
    UBLK = MRG
    PS = [es.enter_context(nc.psum_tensor("PS%d" % i, [128, 1024], F32)) for i in range(4)]

    def psb(i, half):
        return PS[i][:, half * 512:(half + 1) * 512]

    def c_(name, lo=0, hi=None):
        off, n = CC[name]
        hi = n if hi is None else hi
        return CST[:, off + lo:off + hi]

    def coff(name):
        return CC[name][0]

    LNG, LNB, HNW, PSC, CW, ALOG, DTB = 0, 1024, 2048, 2049, 2057, 2153, 2161
    ident = c_("ident"); ones = c_("ones")

    outc = [0]

    TBL = {AF.Exp: 'exp', AF.Ln: 'ln', AF.Silu: 'silu', AF.Sigmoid: 'sig'}

    def fsz(ap):
        n = 1
        for s in ap.shape[1:]:
            n *= s
        return n

    def dma(q, out, in_, r, w, sem):
        nbytes = 4.0 * fsz(out) * out.shape[0]
        S.add(q, lambda e: e.dma_start(out=out, in_=in_), r=r, w=w, dma_sem=sem, cost=0.06, lat=2.0 + nbytes / 150e3)

    def dma_out(out, in_, r, sem):
        outc[0] += 1
        dma("sp", out, in_, r=r, w=["OUT%d" % outc[0]], sem=sem)

    def mm(out, lhsT, rhs, start, stop, r, w):
        npass = 4.0 if lhsT.dtype == F32 else 1.0
        c = max(fsz(out), 48) * npass / 1400.0 + 0.012
        S.add("pe", lambda e: e.matmul(out, lhsT, rhs, start=start, stop=stop), r=r, w=w, cost=c)

    def tr(out, in_, idn, r, w):
        npass = 2.0 if in_.dtype == F32 else 1.0
        c = max(fsz(out), 48) * npass / 1400.0 + 0.012
        S.add("pe", lambda e: e.transpose(out, in_, idn), r=r, w=w, cost=c)

    def act(out, in_, func, r, w, bias=None, scale=None, accum=None):
        kw = {}
        if bias is not None:
            kw["bias"] = bias
        if scale is not None:
            kw["scale"] = scale
        if accum is not None:
            kw["accum_out"] = accum
        tbl = TBL.get(func)
        S.add("act", lambda e: e.activation(out=out, in_=in_, func=func, **kw), r=r, w=w, cost=(200 + fsz(out)) / 1200.0, tbl=tbl)

    def dcost(out, eng="dve"):
        c = (140 + fsz(out)) / 960.0
        return c * 2.0 if eng == "pool" else c

    def tt(out, a, b, op, r, w, eng="dve"):
        S.add(eng, lambda e: e.tensor_tensor(out=out, in0=a, in1=b, op=op), r=r, w=w, cost=dcost(out, eng))

    def ts(out, a, s1, s2, op0, op1, r, w, eng="dve"):
        if op1 is None:
            S.add(eng, lambda e: e.tensor_scalar(out=out, in0=a, scalar1=s1, scalar2=None, op0=op0), r=r, w=w, cost=dcost(out, eng))
        else:
            S.add(eng, lambda e: e.tensor_scalar(out=out, in0=a, scalar1=s1, scalar2=s2, op0=op0, op1=op1), r=r, w=w, cost=dcost(out, eng))

    def stt(out, a, sc, b, op0, op1, r, w, eng="dve"):
        S.add(eng, lambda e: e.scalar_tensor_tensor(out=out, in0=a, scalar=sc, in1=b, op0=op0, op1=op1), r=r, w=w, cost=dcost(out, eng))

    def cp(out, in_, r, w, eng="dve"):
        if eng == "act":
            S.add("act", lambda e: e.copy(out=out, in_=in_), r=r, w=w, cost=(200 + fsz(out)) / 1200.0)
        else:
            S.add(eng, lambda e: e.tensor_copy(out=out, in_=in_), r=r, w=w, cost=dcost(out))

    def ms(out, val, w, eng="dve"):
        S.add(eng, lambda e: e.memset(out, val), r=(), w=w, cost=dcost(out))

    dma("sp", CST[:], cst, r=[], w=["CST"], sem="cst")
    dma("sp", PRM[:], prm, r=[], w=["PRM"], sem="prm")
    dma("pool", POOLW[:], pool_w.rearrange("g (k p) m -> p g k m", p=128), r=[], w=["POOLW"], sem="poolw")
    dma("pool", WBA[:], wba_img, r=[], w=["WBA"], sem="wba")
    cp(IDB[:], ident, r=["CST"], w=["IDB"])
    ms(ONB[:], 1.0, w=["ONB"])
    ms(PW[0][:], 0.0, w=["PW0"])
    ms(PW[1][:], 0.0, w=["PW1"])
    dma("pool", MSKB[:], msk.rearrange("p (m t) -> p m t", m=6), r=[], w=["MSKB"], sem="mskb")
    act(NEGA[:], PRM[:, ALOG:ALOG + 8], AF.Exp, r=["PRM"], w=["NEGA0"])
    ts(NEGA[:], NEGA[:], -1.0, None, ALU.mult, None, r=["NEGA0"], w=["NEGA"])

    WG = {}
    glist = []
    for col0 in list(range(0, 6144, 512)) + [6160, 6672, 7184, 7696]:
        WG[("w_in", col0)] = len(glist); glist.append((w_in, col0))
    for nm, t in (("p_a", p_a), ("p_b", p_b), ("w_out", w_out)):
        for col0 in (0, 512):
            WG[(nm, col0)] = len(glist); glist.append((t, col0))
    wrot = [0]

    def wload(name, src, col0, direct=False):
        i = wrot[0] % NWB
        wrot[0] += 1
        wk = "WB%d" % i
        if direct:
            img = wada_img[col0 // 512] if name == "w_ada" else wimg[WG[(name, col0)]]
            dma("pool", WB[i][:], img, r=[], w=[wk], sem=wk.lower())
            if (name, col0) in WG:
                gidx = WG[(name, col0)]
                dma("sp", WS[gidx], WB[i][:], r=[wk], w=["WS%d" % gidx], sem="wst%d" % i)
        else:
            gidx = WG[(name, col0)]
            dma("sp", WB[i][:], WS[gidx], r=["WS%d" % gidx], w=[wk], sem=wk.lower() + "h")
        return WB[i], wk

    pmrot = [0]

    PMB = [(psb(0, 0), "PS0_0"), (psb(0, 1), "PS0_1"), (psb(1, 1), "PS1_1")]

    def pm_next():
        i = pmrot[0] % 3
        pmrot[0] += 1
        return PMB[i]

    PT = PS[1][:, 0:256]; PTK = "PS1_0"
    PX = PS[1][:, 256:512]; PXK = "PS1_0"

    ORDER = [0, NST] + list(range(1, NST))
    SETI = {st: pos % 2 for pos, st in enumerate(ORDER)}

    def ada(gi):
        AK = ["ADA%d_%d" % (gi, i) for i in range(3)]
        dma("sp", XT[0][:], cexp[gi], r=[], w=["XT0"], sem="xt0")
        act(HB[:], XT[0][:], AF.Silu, r=["XT0"], w=["HB"])
        ptb = PT.bitcast(BF16)
        for hf in range(2):
            for k4 in range(4):
                k = hf * 4 + k4
                tr(ptb[:, k4 * 128:(k4 + 1) * 128], HB[:, k * 128:(k + 1) * 128], IDB[:], r=["HB", "IDB"], w=[PTK])
            cp(HT[:, hf * 4:(hf + 1) * 4, 0:128], ptb[:, 0:512].rearrange("p (k t) -> p k t", k=4), r=[PTK], w=["HT"])
        for grp in range(6):
            wt, wk = wload("w_ada", w_ada, grp * 512, direct=True)
            dma("sp", HF[:, 0:512], bada[:, grp * 512:(grp + 1) * 512], r=[], w=["HF"], sem="hf")
            pm, pmk = pm_next()
            for k in range(8):
                mm(pm, HT[:, k, 0:128], wt[:, k, :], k == 0, k == 7, r=["HT", wk], w=[pmk])
            which = grp // 2
            dst = ADA[gi][which][:, (grp % 2) * 512:(grp % 2 + 1) * 512]
            if which == 0:
                tt(dst, pm, HF[:, 0:512], ALU.add, r=[pmk, "HF"], w=[AK[which]])
            elif which == 1:
                stt(dst, pm, 1.0, HF[:, 0:512], ALU.add, ALU.add, r=[pmk, "HF"], w=[AK[which]])
            else:
                stt(HF[:, 512:1024], pm, 1.0, HF[:, 0:512], ALU.add, ALU.add, r=[pmk, "HF"], w=["HF"])
                ts(dst, HF[:, 512:1024], 0.5, None, ALU.mult, None, r=["HF"], w=[AK[which]])
            yield

    def P(st):
        samp = st == NST
        gi = 1 if samp else 0
        if samp:
            yield from ada(1)
        si = SETI[st]
        T = SET[si]
        K_ = {n: "%s%d" % (n, si) for n in SETN}
        BA, SSQ = BAS[si], SSQS[si]
        bak, ssk = "BA%d" % si, "SSQ%d" % si
        L = 128 if samp else LP
        ntt = L // 128
        TS_ = 16 if samp else 1
        W = 15 * TS_ + L
        first = st == 0
        last = samp or st == NST - 1
        direct = st == 0
        xsrc = xs if samp else xp[st * LP:(st + 1) * LP]
        shift, sc1, g1 = ADA[gi]
        AK = ["ADA%d_%d" % (gi, i) for i in range(3)]

        for t_ in range(ntt):
            xt, xk = XT[t_ % 2], "XT%d" % (t_ % 2)
            dma("sp", xt[:], xsrc[t_ * 128:(t_ + 1) * 128, :], r=[], w=[xk], sem=xk.lower())
            tt(HF[:], xt[:], sc1[:], ALU.mult, r=[xk, AK[1]], w=["HF"])
            tt(HB[:], HF[:], shift[:], ALU.add, r=["HF", AK[0]], w=["HB"])
            ptb = PT.bitcast(BF16)
            for hf in range(2):
                for k4 in range(4):
                    k = hf * 4 + k4
                    tr(ptb[:, k4 * 128:(k4 + 1) * 128], HB[:, k * 128:(k + 1) * 128], IDB[:], r=["HB", "IDB"], w=[PTK])
                cp(HT[:, hf * 4:(hf + 1) * 4, t_ * 128:(t_ + 1) * 128], ptb[:, 0:512].rearrange("p (k t) -> p k t", k=4), r=[PTK], w=["HT"], eng="act")
            yield
        for t_ in range(ntt):
            for k in range(8):
                mm(PX[:, 0:16], HT[:, k, t_ * 128:(t_ + 1) * 128], WBA[:, k, :], k == 0, k == 7, r=["HT", "WBA"], w=[PXK])
            cp(BA[:, t_, :], PX[:, 0:16], r=[PXK], w=[bak], eng="act")
        if samp:
            for half in range(2):
                ni = 8 if half == 0 else 7
                for ii in range(ni):
                    i = half * 8 + ii
                    dma("sp", XT[half][ii * 16:(ii + 1) * 16, :], spool[:, i, :], r=[], w=["XT%d" % half], sem="xt%d" % half)
            dma_out(pools[:, 0:7, :], spool[:, 8:15, :], r=[], sem="o_dd")
        yield

        def proj(src_cols, consumer):
            for grp in range(2):
                wt, wk = wload("w_in", w_in, src_cols + grp * 512, direct=direct)
                for cc in range(4):
                    c = grp * 4 + cc
                    pm, pmk = pm_next()
                    for k in range(8):
                        mm(pm[:, 0:L], wt[:, k, cc * 128:(cc + 1) * 128], HT[:, k, 0:L], k == 0, k == 7, r=["HT", wk], w=[pmk])
                    consumer(c, pm[:, 0:L], pmk)
                    yield

        def cons_act(dst, dkey, func):
            def f(c, pm, pmk):
                act(dst[:, c, 0:L], pm, func, r=[pmk], w=[dkey])
            return f

        ZA, ZB = T["ZA"], T["ZB"]
        yield from proj(1024, cons_act(ZA, K_["ZA"], AF.Silu))
        ext, ek = EXTP, "EXTP"

        def cons_ua(c, pm, pmk):
            j = c % 2
            hist = V(ext, j * EW, [[1, 15 * TS_]])
            new = V(ext, j * EW + 15 * TS_, [[1, L]])
            hk, nk = ek + "h%d" % j, ek + "n%d" % j
            if first:
                ms(hist, 0.0, w=[hk])
            elif samp:
                tr(PT[:, 0:128], XT[0][:, c * 128:(c + 1) * 128], ident, r=["XT0", "CST"], w=[PTK])
                tr(PT[:, 128:240], XT[1][0:112, c * 128:(c + 1) * 128], ident[0:112, 0:112], r=["XT1", "CST"], w=[PTK])
                cp(hist, PT[:, 0:240], r=[PTK], w=[hk])
            else:
                cp(hist, PHIST[:, c, :], r=["PHIST%d" % c], w=[hk])
            cp(new, pm, r=[pmk], w=[nk], eng="act")
            if not last:
                cp(PHIST[:, c, :], V(ext, j * EW + L, [[1, 15]]), r=[nk], w=["PHIST%d" % c])
            else:
                tr(PT[:, 0:128], V(ext, j * EW + 15 * TS_ + L - 128, [[1, 128]]), ident, r=[nk, "CST"], w=[PTK])
                cp(UTOK[:, c * 128:(c + 1) * 128], PT[:, 0:128], r=[PTK], w=["HF"])
            if j == 1:
                gidx = c // 2
                w_ = 2 << gidx
                ekeys = [ek + "h0", ek + "h1", ek + "n0", ek + "n1"]

                def e2(t, off, n):
                    return V(t, off, [[EW, 2], [1, n]])
                cur, curk = ext, ekeys
                sh = 1
                pi = 0
                while sh < w_:
                    dst, dk = PW[pi], "PW%d" % pi
                    o_ = sh * TS_
                    tt(e2(dst, o_, W - o_), e2(cur, o_, W - o_), e2(cur, 0, W - o_), ALU.add, r=curk, w=[dk])
                    cur, curk = dst, [dk]
                    pi ^= 1
                    sh *= 2
                stt(V(POOLED, 0, [[LP, 2], [1, L]]), e2(cur, 15 * TS_, L), 1.0 / w_, e2(ext, 15 * TS_, L), ALU.mult, ALU.subtract,
                    r=curk + ekeys, w=["POOLED"])
                if first:
                    icv = V(CST, coff("invc") + gidx * 16, [[0, 2], [1, 16]])
                    tt(PW[pi][:, :, 0:16], e2(cur, 15, 16), icv, ALU.mult, r=curk + ["CST"], w=["PW%d" % pi])
                    tt(V(POOLED, 0, [[LP, 2], [1, 16]]), PW[pi][:, :, 0:16], e2(ext, 15, 16), ALU.subtract,
                       r=["PW%d" % pi] + ekeys + ["POOLED"], w=["POOLED"])
                for m in range(2):
                    pm2, pm2k = pm_next()
                    for k in range(2):
                        mm(pm2[:, 0:L], POOLW[:, gidx, k, m * 128:(m + 1) * 128], POOLED[:, k, 0:L], k == 0, k == 1,
                           r=["POOLW", "POOLED"], w=[pm2k])
                    cidx = 2 * gidx + m
                    stt(ZA[:, cidx, 0:L], pm2[:, 0:L], PRM[:, PSC + cidx:PSC + cidx + 1], ZA[:, cidx, 0:L], ALU.mult, ALU.mult,
                        r=[pm2k, "PRM", K_["ZA"]], w=[K_["ZA"]])

        yield from proj(0, cons_ua)
        if last:
            if samp:
                for t in range(8):
                    dma_out(pools[:, 7 + t, :], UTOK[t * 16:(t + 1) * 16, :], r=["HF"], sem="o_utok")
            else:
                dma_out(poolp[:, :], UTOK[113:128, :], r=["HF"], sem="o_utok")
        yield from proj(5120, cons_act(ZB, K_["ZB"], AF.Silu))

        def cons_qkv(typ):
            dst = (T["QT"], T["KT"], T["VT"])[typ]
            dkey = (K_["QT"], K_["KT"], K_["VT"])[typ]

            def f(h, pm, pmk):
                c = typ * 8 + h
                ei = c % 2
                extc, eck = EXTC[ei], "EXTC%d" % ei
                hist = V(extc, 0, [[1, 3 * TS_]])
                new = V(extc, 3 * TS_, [[1, L]])
                if first:
                    ms(hist, 0.0, w=[eck + "h"])
                elif samp:
                    if h == 0:
                        for i in range(3):
                            dma("sp", HF[i * 16:(i + 1) * 16, :], sconv[:, i, typ * 1024:(typ + 1) * 1024], r=[], w=["HF"], sem="hf")
                    tr(PT[:, 0:48], HF[0:48, h * 128:(h + 1) * 128], ident[0:48, 0:48], r=["HF", "CST"], w=[PTK])
                    cp(hist, PT[:, 0:48], r=[PTK], w=[eck + "h"])
                else:
                    cp(hist, CHIST[:, c, :], r=["CHIST%d" % c], w=[eck + "h"])
                cp(new, pm, r=[pmk], w=[eck + "n"], eng="act")
                if not last:
                    cp(CHIST[:, c, :], V(extc, L, [[1, 3]]), r=[eck + "n"], w=["CHIST%d" % c])
                else:
                    tk, tkk = TOK[c % 2], "TOK%d" % (c % 2)
                    tr(PT[:, 0:128], V(extc, 3 * TS_ + L - 128, [[1, 128]]), ident, r=[eck + "n", "CST"], w=[PTK])
                    cp(tk[:], PT[:, 0:128], r=[PTK], w=[tkk])
                    if samp:
                        for i in range(3):
                            dma_out(convs[:, i, c * 128:(c + 1) * 128], tk[(5 + i) * 16:(6 + i) * 16, :], r=[tkk], sem="o_" + tkk.lower())
                    else:
                        dma_out(convp[:, c * 128:(c + 1) * 128], tk[125:128, :], r=[tkk], sem="o_" + tkk.lower())
                acc = CACC[:, 0:L]
                for i in range(4):
                    src = V(extc, i * TS_, [[1, L]])
                    cwc = PRM[:, CW + c * 4 + i:CW + c * 4 + i + 1]
                    if i == 0:
                        ts(acc, src, cwc, None, ALU.mult, None, r=[eck + "h", eck + "n", "PRM"], w=["CACC"])
                    else:
                        stt(acc, src, cwc, acc, ALU.mult, ALU.add, r=[eck + "h", eck + "n", "PRM", "CACC"], w=["CACC"])
                act(dst[:, h, 0:L], acc, AF.Silu, r=["CACC"], w=[dkey])
                if typ < 2:
                    act(SQ[:, 0:L], dst[:, h, 0:L], AF.Square, r=[dkey], w=["SQ"])
                    for t_ in range(ntt):
                        col = 32 + t_ * 16 + typ * 8 + h
                        mm(PX[:, col:col + 1], SQ[:, t_ * 128:(t_ + 1) * 128], ONB[:, 0:1], True, True, r=["SQ", "ONB"], w=[PXK])
            return f

        yield from proj(2048, cons_qkv(0))
        yield from proj(3072, cons_qkv(1))
        cp(SSQ[:, 0:ntt, :], PX[:, 32:32 + ntt * 16].rearrange("p (t c) -> p t c", t=ntt), r=[PXK], w=[ssk])
        yield from proj(4096, cons_qkv(2))
        def cons_tanh(dst, dkey):
            def f(c, pm, pmk):
                act(dst[:, c, 0:L], pm, AF.Tanh, r=[pmk], w=[dkey], scale=0.5)
            return f
        yield from proj(6160, cons_tanh(T["SGA"], K_["SGA"]))
        yield from proj(7184, cons_tanh(T["SGB"], K_["SGB"]))

    def DF(st):
        samp = st == NST
        gi = 1 if samp else 0
        si = SETI[st]
        T = SET[si]
        K_ = {n: "%s%d" % (n, si) for n in SETN}
        L = 128 if samp else LP
        ntt = L // 128
        for t_ in range(ntt):
            yield from delta_tile(st, t_)
        xsrc = xs if samp else xp[st * LP:(st + 1) * LP]
        ydst = ys if samp else yp[st * LP:(st + 1) * LP]
        g1 = ADA[gi][2]
        g1k = "ADA%d_2" % gi
        for bi, (wn, wsrc, ysrc, yk, gsrc, gk) in enumerate((("p_a", p_a, T["ZA"], K_["ZA"], T["SGA"], K_["SGA"]),
                                                             ("p_b", p_b, T["ZB"], K_["ZB"], T["SGB"], K_["SGB"]))):
            for grp in range(2):
                wt, wk = wload(wn, wsrc, grp * 512, direct=(st == 0))
                for cc in range(4):
                    c = grp * 4 + cc
                    pm, pmk = pm_next()
                    for k in range(8):
                        mm(pm[:, 0:L], wt[:, k, cc * 128:(cc + 1) * 128], ysrc[:, k, 0:L], k == 0, k == 7, r=[yk, wk], w=[pmk])
                    if bi == 0:
                        stt(MRG[:, c, 0:L], gsrc[:, c, 0:L], 1.0, pm[:, 0:L], ALU.add, ALU.mult, r=[pmk, gk], w=["MRG"])
                    else:
                        stt(CACC2[:, 0:L], gsrc[:, c, 0:L], 1.0, pm[:, 0:L], ALU.add, ALU.mult, r=[pmk, gk], w=["CACC2"])
                        tt(MRG[:, c, 0:L], MRG[:, c, 0:L], CACC2[:, 0:L], ALU.add, r=["MRG", "CACC2"], w=["MRG"])
                    yield
        wts = [wload("w_out", w_out, nh * 512, direct=(st == 0)) for nh in range(2)]
        for t_ in range(ntt):
            dma("sp", RR[:], xsrc[t_ * 128:(t_ + 1) * 128, :], r=[], w=["RR"], sem="rr")
            for nh in range(2):
                wt, wk = wts[nh]
                pm, pmk = pm_next()
                for k in range(8):
                    mm(pm, MRG[:, k, t_ * 128:(t_ + 1) * 128], wt[:, k, :], k == 0, k == 7, r=["MRG", wk], w=[pmk])
                tt(HF2[:, nh * 512:(nh + 1) * 512], pm, g1[:, nh * 512:(nh + 1) * 512], ALU.mult, r=[pmk, g1k], w=["HF2"])
            stt(RR[:], RR[:], float(2 ** 0.25), HF2[:], ALU.mult, ALU.add, r=["RR", "HF2"], w=["RR"])
            yield
            for i in range(2):
                S.add("dve", (lambda i: lambda e: e.bn_stats(out=STAT[:, i, :], in_=RR[:, i * 512:(i + 1) * 512]))(i), r=["RR"], w=["STAT"], cost=0.6)
            S.add("dve", lambda e: e.bn_aggr(out=MV[:, 0:2], in_=STAT[:]), r=["STAT"], w=["MV"])
            ts(MV[:, 2:3], MV[:, 1:2], 1e-5, None, ALU.add, None, r=["MV"], w=["MV2"])
            act(MV[:, 2:3], MV[:, 2:3], AF.Ln, r=["MV2"], w=["MV2"])
            act(MV[:, 2:3], MV[:, 2:3], AF.Exp, r=["MV2"], w=["MV2"], scale=-0.5)
            ts(RR[:], RR[:], MV[:, 0:1], MV[:, 2:3], ALU.subtract, ALU.mult, r=["RR", "MV", "MV2"], w=["RR"])
            tt(RR[:], RR[:], PRM[:, LNG:LNG + 1024], ALU.mult, r=["RR", "PRM"], w=["RR"])
            tt(HF2[:], RR[:], PRM[:, LNB:LNB + 1024], ALU.add, r=["RR", "PRM"], w=["HF2"])
            dma_out(ydst[t_ * 128:(t_ + 1) * 128, :], HF2[:], r=["HF2"], sem="o_hf2")
            yield

    tcount = [0]

    def delta_tile(st, t_):
        samp = st == NST
        g = "s" if samp else "p"
        si = SETI[st]
        T = SET[si]
        QT, KT_, VT, ZB = T["QT"], T["KT"], T["VT"], T["ZB"]
        qk_, kk_, vk_, zbk = "QT%d" % si, "KT%d" % si, "VT%d" % si, "ZB%d" % si
        BA, SSQ = BAS[si], SSQS[si]
        bak, ssk = "BA%d" % si, "SSQ%d" % si
        nl = 2 if samp else 6
        cols = slice(t_ * 128, (t_ + 1) * 128)
        first = st == 0 and t_ == 0
        lastp = st == NST - 1 and t_ == (LP // 128 - 1)
        PA, PB, PC, PD = PS[2], PS[3], PS[2], PS[3]
        E1, E2, E3 = AM[1], AT[1], YM[1]
        e1k, e2k, e3k = "AM1", "AT1", "YM1"
        GQ = BM
        ON = BM
        pkm = {id(PS[2]): 2, id(PS[3]): 3}

        def pk(P_):
            i = 2 if P_ is PS[2] else 3
            return ["PS%d_0" % i, "PS%d_1" % i]

        def p3(P_):
            return P_[:, :].rearrange("p (h t) -> p h t", h=8)

        tcount[0] += 1
        sbi = tcount[0] % 2
        SC = SCB[sbi]

        def k(i):
            return "SC%dr%d" % (sbi, i)

        def scol(i):
            return SC[:, i, :]

        def sbc(i):
            return V(SC, i * 8, [[1, 8], [0, 128]])
        braw = BA[:, t_, 0:8]; araw = BA[:, t_, 8:16]
        act(scol(0), braw, AF.Exp, r=[bak], w=[k(0)], scale=-1.0)
        tt(scol(1), araw, PRM[:, DTB:DTB + 8], ALU.add, r=[bak, "PRM"], w=[k(1)])
        act(scol(1), scol(1), AF.Exp, r=[k(1)], w=[k(1)])
        ts(SC[:, 0:2, :], SC[:, 0:2, :], 1.0, None, ALU.add, None, r=[k(0), k(1)], w=[k(0), k(1)])
        ts(SC[:, 2:4, :], SSQ[:, t_, :].rearrange("p (a b) -> p a b", a=2), 1e-6, None, ALU.add, None, r=[ssk], w=[k(2), k(3)])
        act(SC[:, 0:4, :], SC[:, 0:4, :], AF.Ln, r=[k(0), k(1), k(2), k(3)], w=[k(0), k(1), k(2), k(3)])
        act(scol(16), scol(0), AF.Exp, r=[k(0)], w=[k(16)], scale=-1.0)
        tt(scol(4), scol(1), NEGA[:], ALU.mult, r=[k(1), "NEGA"], w=[k(4)])
        mm(PD[:, 0:8], c_("tri" + g), scol(4), True, True, r=[k(4), "CST"], w=["PS3_0"])
        mm(PD[:, 8:16], c_("same" + g), scol(4), True, True, r=[k(4), "CST"], w=["PS3_0"])
        cp(SC[:, 5:7, :], PD[:, 0:16].rearrange("p (a b) -> p a b", a=2), r=["PS3_0"], w=[k(5), k(6)])
        ts(scol(7), scol(2), -0.5, math.log(128 ** -0.5), ALU.mult, ALU.add, r=[k(2)], w=[k(7)])
        ts(scol(8), scol(3), -0.5, None, ALU.mult, None, r=[k(3)], w=[k(8)])
        tt(scol(17), scol(5), scol(8), ALU.add, r=[k(5), k(8)], w=[k(17)])
        tt(scol(9), scol(17), scol(0), ALU.subtract, r=[k(17), k(0)], w=[k(9)])
        tt(scol(10), scol(5), scol(8), ALU.subtract, r=[k(5), k(8)], w=[k(10)])
        tt(scol(11), scol(5), scol(7), ALU.add, r=[k(5), k(7)], w=[k(11)])
        ts(scol(12), scol(10), -1.0, None, ALU.mult, None, r=[k(10)], w=[k(12)])
        tt(scol(13), scol(6), scol(10), ALU.subtract, r=[k(6), k(10)], w=[k(13)])
        act(scol(13), scol(13), AF.Exp, r=[k(13)], w=[k(13)])
        act(scol(14), scol(9), AF.Exp, r=[k(9)], w=[k(14)])
        ts(scol(14), scol(14), -1.0, None, ALU.mult, None, r=[k(14)], w=[k(14)])
        yield

        identb = V(CST, coff("ident"), [[0, 8], [1, 128]])

        def rowbc(P_, src_i, skey, maskidx):
            if maskidx is not None:
                for hf in range(2):
                    out = P_[:, hf * 512:(hf + 1) * 512].rearrange("p (h t) -> p h t", h=4)
                    mk = V(MSKB, maskidx * 128, [[0, 4], [1, 128]])
                    mm(out, IDB[:], mk, True, False, r=["MSKB", "IDB"], w=[pk(P_)[hf]])
            for h in range(NH):
                out = P_[:, h * 128:(h + 1) * 128]
                mm(out, V(SC, src_i * 8 + h, [[0, 128]]), ident, maskidx is None, maskidx is None or h % 4 == 3,
                   r=[skey, "CST"], w=[pk(P_)[h // 4]])

        mbase = 3 if samp else 0
        rowbc(PA, 10, k(10), mbase + 0)
        for h in range(NH):
            hs = slice(h * 128, (h + 1) * 128)
            act(E1[:, h, :], PA[:, hs], AF.Exp, r=pk(PA) + [k(9)], w=[e1k], bias=SC[:, 9, h:h + 1], scale=-1.0)
        yield
        rowbc(PC, 11, k(11), None)
        rowbc(PD, 11, k(11), mbase + 2)
        act(GQ[:], p3(PC), AF.Exp, r=pk(PC), w=["BM"])
        for h in range(NH):
            hs = slice(h * 128, (h + 1) * 128)
            act(E3[:, h, :], PD[:, hs], AF.Exp, r=pk(PD) + [k(12)], w=[e3k], bias=SC[:, 12, h:h + 1])
        tt(QG[:], QT[:, :, cols], GQ[:], ALU.mult, r=[qk_, "BM"], w=["QG"])
        yield
        for h in range(NH):
            hs = slice(h * 128, (h + 1) * 128)
            mm(PA[:, hs], KT_[:, h, cols], KT_[:, h, cols], True, True, r=[kk_], w=[pk(PA)[h // 4]])
            mm(PB[:, hs], KT_[:, h, cols], QT[:, h, cols], True, True, r=[kk_, qk_], w=[pk(PB)[h // 4]])
        tt(AM[0][:], p3(PA), E1[:], ALU.mult, r=pk(PA) + [e1k], w=["AM0"])
        tt(QKT[:], p3(PB), E3[:], ALU.mult, r=pk(PB) + [e3k], w=["QKT"])
        pbb = PB[:, :].bitcast(BF16)
        for h in range(NH):
            tr(pbb[:, h * 128:(h + 1) * 128], AM[0][:, h, :], IDB[:], r=["AM0", "IDB"], w=["PS3_0"])
        cp(ATO[:], pbb[:, 0:1024].rearrange("p (h t) -> p h t", h=8), r=["PS3_0"], w=["ATO"], eng="act")
        tt(YM[0][:], identb, ATO[:], ALU.subtract, r=["CST", "ATO"], w=["YM0"])
        yield
        pcb = PC[:, :].bitcast(BF16)
        for h in range(NH):
            tr(pcb[:, h * 128:(h + 1) * 128], KT_[:, h, cols], IDB[:], r=[kk_, "IDB"], w=["PS2_0"])
            tr(pcb[:, 1024 + h * 128:1024 + (h + 1) * 128], VT[:, h, cols], IDB[:], r=[vk_, "IDB"], w=["PS2_1"])
        tt(KTOK[:], pcb[:, 0:1024].rearrange("p (h t) -> p h t", h=8), sbc(13), ALU.mult, r=["PS2_0", k(13)], w=["KTOK"])
        tt(BV[:], pcb[:, 1024:2048].rearrange("p (h t) -> p h t", h=8), sbc(16), ALU.mult, r=["PS2_1", k(16)], w=["BV"])
        yield
        cur = 0
        for lvl in range(1, nl + 1):
            nxt = cur ^ 1
            atc, atck = (ATO, "ATO") if lvl == 1 else (AT[cur], "AT%d" % cur)
            for h in range(NH):
                hs = slice(h * 128, (h + 1) * 128)
                mm(PA[:, hs], atc[:, h, :], AM[cur][:, h, :], True, True, r=[atck, "AM%d" % cur], w=[pk(PA)[h // 4]])
                if lvl < nl:
                    mm(PB[:, hs], AM[cur][:, h, :], atc[:, h, :], True, True, r=[atck, "AM%d" % cur], w=[pk(PB)[h // 4]])
            cp(AM[nxt][:], p3(PA), r=pk(PA), w=["AM%d" % nxt], eng="act")
            if lvl < nl:
                cp(AT[nxt][:], p3(PB), r=pk(PB), w=["AT%d" % nxt])
            yield
            for h in range(NH):
                hs = slice(h * 128, (h + 1) * 128)
                mm(PD[:, hs], AM[nxt][:, h, :], YM[cur][:, h, :], True, True, r=["AM%d" % nxt, "YM%d" % cur], w=[pk(PD)[h // 4]])
            tt(YM[nxt][:], p3(PD), YM[cur][:], ALU.add, r=pk(PD) + ["YM%d" % cur], w=["YM%d" % nxt])
            cur = nxt
            yield
        Y, yk = YM[cur], "YM%d" % cur
        if not samp:
            if first:
                ms(SF[:], 0.0, w=["SF"])
                ms(SBF[:], 0.0, w=["SBF"])
            for h in range(NH):
                hs = slice(h * 128, (h + 1) * 128)
                mm(PA[:, hs], KT_[:, h, cols], SBF[:, h, :], True, True, r=[kk_, "SBF"], w=[pk(PA)[h // 4]])
        else:
            for h in range(NH):
                for hf in range(2):
                    dma("pool", V(S0BH[hf][0], 0, [[128, 8], [1, 128]]), sdelta[hf * 8:(hf + 1) * 8, h, :, :].rearrange("s d e -> d s e"),
                        r=[], w=[S0BH[hf][1]], sem="s0b%d" % hf)
                for s in range(16):
                    sbt, sbk = S0BH[s // 8]
                    mm(V(PS[3], h * 128 + s, [[16, 8]]), V(sbt, (s % 8) * 128, [[1, 128]]), V(KT_, h * LP + s, [[16, 8]]), True, True,
                       r=[sbk, kk_], w=[pk(PB)[h // 4]])
                yield
            cp(TF[:], p3(PB), r=pk(PB), w=["DG"])
            for h in range(NH):
                hs = slice(h * 128, (h + 1) * 128)
                tr(PA[:, hs], TF[:, h, :], ident, r=["DG", "CST"], w=[pk(PA)[h // 4]])
        for h in range(NH):
            hs = slice(h * 128, (h + 1) * 128)
            stt(BM[:, h, :], PA[:, hs], SC[:, 14, h:h + 1], BV[:, h, :], ALU.mult, ALU.add, r=pk(PA) + [k(14), "BV"], w=["BM"])
        yield
        for h in range(NH):
            hs = slice(h * 128, (h + 1) * 128)
            mm(PD[:, hs], Y[:, h, :], BM[:, h, :], True, True, r=[yk, "BM"], w=[pk(PD)[h // 4]])
        cp(UM[:], p3(PD), r=pk(PD), w=["UM"], eng="act")
        yield
        if not samp:
            for h in range(NH):
                hs = slice(h * 128, (h + 1) * 128)
                mm(PB[:, hs], ATO[:, h, :], UM[:, h, :], True, True, r=["ATO", "UM"], w=[pk(PB)[h // 4]])
            tt(DG[:], BM[:], UM[:], ALU.subtract, r=["BM", "UM"], w=["DG"])
            tt(RES[:], DG[:], p3(PB), ALU.subtract, r=["DG"] + pk(PB), w=["RES"])
            yield
            for h in range(NH):
                hs = slice(h * 128, (h + 1) * 128)
                mm(PD[:, hs], Y[:, h, :], RES[:, h, :], True, True, r=[yk, "RES"], w=[pk(PD)[h // 4]])
            tt(UM[:], UM[:], p3(PD), ALU.add, r=["UM"] + pk(PD), w=["UM"])
            yield
            for h in range(NH):
                hs = slice(h * 128, (h + 1) * 128)
                mm(PC[:, hs], QG[:, h, :], SBF[:, h, :], True, False, r=["QG", "SBF"], w=[pk(PC)[h // 4]])
                mm(PC[:, hs], QKT[:, h, :], UM[:, h, :], False, True, r=["QKT", "UM"], w=[pk(PC)[h // 4]])
        else:
            for h in range(NH):
                hs = slice(h * 128, (h + 1) * 128)
                for hf in range(2):
                    dma("pool", V(S0BH[hf][0], 0, [[128, 8], [1, 128]]), sdelta[hf * 8:(hf + 1) * 8, h, :, :].rearrange("s d e -> d s e"),
                        r=[], w=[S0BH[hf][1]], sem="s0b%d" % hf)
                mm(PC[:, hs], UM[:, h, :], QKT[:, h, :], True, False, r=["UM", "QKT"], w=[pk(PC)[h // 4]])
                for s in range(16):
                    sbt, sbk = S0BH[s // 8]
                    mm(V(PS[2], h * 128 + s, [[16, 8]]), V(sbt, (s % 8) * 128, [[1, 128]]), V(QG, h * 128 + s, [[16, 8]]), False, s == 15,
                       r=[sbk, "QG"], w=[pk(PC)[h // 4]])
                yield
            cp(TF[:], p3(PC), r=pk(PC), w=["DG"])
            for h in range(NH):
                hs = slice(h * 128, (h + 1) * 128)
                tr(PC[:, hs], TF[:, h, :], ident, r=["DG", "CST"], w=[pk(PC)[h // 4]])
        yield
        ms(scol(15), 0.0, w=[k(15)])
        for h in range(NH):
            hs = slice(h * 128, (h + 1) * 128)
            act(JUNK[:], PC[:, hs], AF.Square, r=pk(PC) + [k(15)], w=["JUNK", k(15)], accum=SC[:, 15, h:h + 1])
        ts(scol(15), scol(15), 1.0 / 128, 1e-6, ALU.mult, ALU.add, r=[k(15)], w=[k(15)])
        act(scol(15), scol(15), AF.Ln, r=[k(15)], w=[k(15)])
        act(scol(15), scol(15), AF.Exp, r=[k(15)], w=[k(15)], scale=-0.5)
        tt(ON[:], p3(PC), sbc(15), ALU.mult, r=pk(PC) + [k(15)], w=["BM"])
        yield
        pab = PA[:, :].bitcast(BF16)
        for h in range(NH):
            tr(pab[:, h * 128:(h + 1) * 128], ON[:, h, :], IDB[:], r=["BM", "IDB"], w=["PS2_0"])
        stt(ZB[:, :, cols], pab[:, 0:1024].rearrange("p (h t) -> p h t", h=8), PRM[:, HNW:HNW + 1], ZB[:, :, cols], ALU.mult, ALU.mult,
            r=["PS2_0", "PRM", zbk], w=[zbk])
        yield
        if not samp:
            tt(DG[:, 0, 0:8], scol(5), V(CST, coff("lastm" + g), [[0, 8]]), ALU.mult, r=[k(5), "CST"], w=["DG"])
            mm(PB[:, 0:8], ones, DG[:, 0, 0:8], True, True, r=["DG", "CST"], w=["PS3_0"])
            act(GL[:, 0, :], PB[:, 0:8], AF.Exp, r=["PS3_0"], w=["GL"])
            for h in range(NH):
                hs = slice(h * 128, (h + 1) * 128)
                mm(PB[:, hs], KTOK[:, h, :], UM[:, h, :], True, True, r=["KTOK", "UM"], w=[pk(PB)[h // 4]])
            for h in range(NH):
                hs = slice(h * 128, (h + 1) * 128)
                stt(SF[:, h, :], SF[:, h, :], GL[:, 0, h:h + 1], PB[:, hs], ALU.mult, ALU.add, r=["SF", "GL"] + pk(PB), w=["SF"])
            cp(SBF[:], SF[:], r=["SF"], w=["SBF"], eng="act")
            if lastp:
                dma_out(deltap.rearrange("h d e -> d h e"), SF[:], r=["SF"], sem="o_sf")
            yield
        else:
            tt(V(DG, 0, [[8, 16], [1, 8]]), V(SC, 5 * 8, [[0, 16], [1, 8]]), V(CST, coff("lastm" + g), [[1, 16], [0, 8]]), ALU.mult,
               r=[k(5), "CST"], w=["DG"])
            mm(PB[:, 0:128], ones, V(DG, 0, [[1, 128]]), True, True, r=["DG", "CST"], w=["PS3_0"])
            act(GL[:], PB[:, 0:128].rearrange("p (s h) -> p s h", h=8), AF.Exp, r=["PS3_0"], w=["GL"])
            RT = [(RR, "RR"), (HF2, "HF2"), (XT[1], "XT1"), (DG, "DG")]
            UBv = V(UBLK, 0, [[128, 16], [1, 128]])
            cnt2 = 0
            for h in range(NH):
                tt(UBv, V(UM, h * 128, [[0, 16], [1, 128]]), V(CST, coff("rowm" + g), [[1, 16], [0, 128]]), ALU.mult,
                   r=["UM", "CST"], w=["MRG"])
                for rnd in range(2):
                    ri = cnt2 % 2
                    cnt2 += 1
                    sf, sfk = RT[ri]
                    sn, snk = RT[2 + ri]
                    Pq = (PA, PB)[ri]
                    dma("pool", V(sf, 0, [[128, 8], [1, 128]]), sdelta[rnd * 8:(rnd + 1) * 8, h, :, :].rearrange("s d e -> d s e"),
                        r=[], w=[sfk], sem="ld_" + sfk.lower())
                    for s8 in range(8):
                        mm(Pq[:, s8 * 128:(s8 + 1) * 128], KTOK[:, h, :], V(UBLK, (rnd * 8 + s8) * 128, [[1, 128]]), True, True,
                           r=["KTOK", "MRG"], w=[pk(Pq)[s8 // 4]])
                    for s8 in range(8):
                        s = rnd * 8 + s8
                        stt(V(sn, s8 * 128, [[1, 128]]), V(sf, s8 * 128, [[1, 128]]), GL[:, s, h:h + 1], Pq[:, s8 * 128:(s8 + 1) * 128],
                            ALU.mult, ALU.add, r=[sfk, "GL"] + pk(Pq), w=[snk])
                    outc[0] += 1
                    dma("pool", deltas[rnd * 8:(rnd + 1) * 8, h, :, :].rearrange("s d e -> d s e"), V(sn, 0, [[128, 8], [1, 128]]),
                        r=[snk], w=["OUT%d" % outc[0]], sem="os_" + snk.lower())
                    yield

    def drain(g):
        for _ in g:
            pass

    def interleave(ga_, gb_, na=1, nb=1):
        a_live = b_live = True
        while a_live or b_live:
            for _ in range(na):
                if a_live:
                    try:
                        next(ga_)
                    except StopIteration:
                        a_live = False
            for _ in range(nb):
                if b_live:
                    try:
                        next(gb_)
                    except StopIteration:
                        b_live = False

    drain(ada(0))
    drain(P(ORDER[0]))
    for pos in range(len(ORDER) - 1):
        interleave(DF(ORDER[pos]), P(ORDER[pos + 1]), 1, 2)
    drain(DF(ORDER[-1]))
    _emit(nc, S, es)
    print('sched est us', round(S.est, 1), 'ops', len(S.ops))
    es.close()
    return nc


def _emit(nc, S, es):
    order = S.schedule()
    csem = {e: es.enter_context(nc.semaphore("c_" + e)) for e in S.ENG}
    dnames = sorted({op.sem for op in S.ops if op.dma})
    dsem = {n: es.enter_context(nc.semaphore("d_" + n)) for n in dnames}
    dcnt = {n: 0 for n in dnames}
    for e in S.ENG:
        c = 0
        for op in order[e]:
            if op.dma:
                dcnt[op.sem] += 16
                op.tick = dcnt[op.sem]
                op.sem = dsem[op.sem]
            elif op.sig:
                c += 1
                op.tick = c
                op.sem = csem[e]
    block = es.enter_context(nc.Block())

    def body(e):
        def f(eng):
            waited = {}
            for op in order[e]:
                need = {}
                for d in op.waits:
                    key = id(d.sem)
                    if need.get(key, (None, 0))[1] < d.tick:
                        need[key] = (d.sem, d.tick)
                for key, (sem, val) in need.items():
                    if waited.get(key, 0) < val:
                        eng.wait_ge(sem, val)
                        waited[key] = val
                ins = op.fn(eng)
                if op.dma:
                    ins.then_inc(op.sem, 16)
                elif op.sig:
                    ins.then_inc(op.sem, 1)
            if e == "sp":
                for n, v in dcnt.items():
                    eng.wait_ge(dsem[n], v)
        return f

    block.tensor(body("pe"))
    block.scalar(body("act"))
    block.vector(body("dve"))
    block.gpsimd(body("pool"))
    block.sync(body("sp"))


_NC_CACHE = {}


def kernel(x_prompt, x_sample, state_pool, state_conv, state_delta, c_prompt, c_sample,
           w_ada, b_ada, w_in, conv_w, a_log, dt_bias, head_norm_w, pool_w, pool_scale,
           p_a, p_b, w_out, ln_g, ln_b, _debug=()):
    f = lambda a: np.ascontiguousarray(np.asarray(a, dtype=np.float32))
    key = tuple(n for n, _ in _debug)
    if key not in _NC_CACHE:
        _NC_CACHE[key] = build(_debug)
    nc = _NC_CACHE[key]
    cst_np, _ = _consts()
    prm = np.zeros((128, 2176), np.float32)
    prm[:, 0:1024] = f(ln_g)[0][None, :]
    prm[:, 1024:2048] = f(ln_b)[0][None, :]
    prm[:, 2048] = f(head_norm_w)[0]
    prm[:, 2049:2057] = f(pool_scale)[0].reshape(8, 128).T
    prm[:, 2057:2153] = f(conv_w)[0].reshape(4, 24, 128).transpose(2, 1, 0).reshape(128, 96)
    prm[:, 2153:2161] = f(a_log)[0][None, :]
    prm[:, 2161:2169] = f(dt_bias)[0][None, :]
    bada = np.ascontiguousarray(np.broadcast_to(f(b_ada)[0][None, :], (128, 3072)))
    msk_np = np.ascontiguousarray(_consts()[1]["_masks"])
    def img(w2d, col0, n=512):
        return np.ascontiguousarray(w2d[:, col0:col0 + n].reshape(8, 128, n).transpose(1, 0, 2))
    w_in2, w_ada2 = f(w_in)[0], f(w_ada)[0]
    groups = [img(w_in2, c0) for c0 in list(range(0, 6144, 512)) + [6160, 6672, 7184, 7696]]
    for w3 in (p_a, p_b, w_out):
        w2 = f(w3)[0]
        groups += [img(w2, 0), img(w2, 512)]
    wimg_np = np.stack(groups)
    wada_np = np.stack([img(w_ada2, g * 512) for g in range(6)])
    wba_np = img(w_in2, 6144, 16)
    shared = dict(msk=msk_np, wimg=wimg_np, wada_img=wada_np, wba_img=wba_np, pool_w=f(pool_w)[0],
                  bada=bada, prm=prm, cst=cst_np)
    xpf, xsf = f(x_prompt), f(x_sample)
    spf, scf, sdf = f(state_pool)[0], f(state_conv)[0], f(state_delta)[0]
    cpf, csf = f(c_prompt), f(c_sample)
    in_maps = []
    for i in range(8):
        cexp = np.empty((2, 128, D), np.float32)
        cexp[0] = cpf[i][None, :]
        cexp[1] = np.tile(csf[16 * i:16 * i + 16], (8, 1))
        m = dict(shared)
        m.update(xp=xpf[i], xs=np.ascontiguousarray(xsf[16 * i:16 * i + 16].transpose(1, 0, 2)).reshape(128, D), cexp=cexp,
                 spool=spf[16 * i:16 * i + 16], sconv=scf[16 * i:16 * i + 16], sdelta=sdf[16 * i:16 * i + 16])
        in_maps.append(m)
    res = run_bass_kernel_spmd(nc, in_maps, core_ids=list(range(8)))
    R = res.results
    y_prompt = np.stack([R[i]["yp"] for i in range(8)])
    y_sample = np.concatenate([R[i]["ys"].reshape(8, 16, D).transpose(1, 0, 2) for i in range(8)])
    pool_prompt = np.stack([R[i]["poolp"] for i in range(8)])[None]
    conv_prompt = np.stack([R[i]["convp"] for i in range(8)])[None]
    delta_prompt = np.stack([R[i]["deltap"] for i in range(8)])[None]
    pool_sample = np.concatenate([R[i]["pools"] for i in range(8)])[None]
    conv_sample = np.concatenate([R[i]["convs"] for i in range(8)])[None]
    delta_sample = np.concatenate([R[i]["deltas"] for i in range(8)])[None]
    if _debug:
        DEBUG.clear()
        DEBUG.update({n: R[0]["dbg_" + n] for n, _ in _debug})
    return (y_prompt, y_sample, pool_prompt, conv_prompt, delta_prompt, pool_sample, conv_sample, delta_sample)
```

```python
from contextlib import ExitStack
import math
import numpy as np
import ml_dtypes
import concourse.bass as bass
import concourse.mybir as mybir
from concourse.bass_utils import run_bass_kernel_spmd

F32 = mybir.dt.float32
BF16 = mybir.dt.bfloat16
ALU = mybir.AluOpType
AF = mybir.ActivationFunctionType

D = 1024
NH = 8
DIN = 8208
NEG = -1.0e5
DEBUG = {}
STRICT_SAME_ENGINE = False


class Op:
    __slots__ = ("eng", "fn", "waits", "sig", "dma", "sem", "tick", "idx", "deps", "cost", "lat", "tbl", "boost")


class Sched:
    ENG = ("pe", "act", "dve", "pool", "sp")

    def __init__(self):
        self.ops = []
        self.lastw = {}
        self.readers = {}
        self.dma_cnt = {}
        self.cur_boost = 0.0

    def add(self, eng, fn, r=(), w=(), dma_sem=None, cost=0.2, lat=0.0, tbl=None):
        op = Op()
        op.tbl = tbl
        op.boost = self.cur_boost
        op.eng, op.fn, op.dma, op.sem = eng, fn, dma_sem is not None, dma_sem
        op.sig = False
        op.idx = len(self.ops)
        op.cost, op.lat = cost, lat
        deps = {}
        for k in r:
            lw = self.lastw.get(k)
            if lw is not None:
                deps[lw.idx] = (lw, True)
        for k in w:
            lw = self.lastw.get(k)
            if lw is not None and lw.idx not in deps:
                deps[lw.idx] = (lw, False)
            for rd in self.readers.get(k, ()):
                if rd.idx not in deps:
                    deps[rd.idx] = (rd, False)
        op.deps = []
        waits = []
        for d, raw in deps.values():
            need = True
            if (not d.dma) and (not op.dma) and d.eng == eng and not raw and not STRICT_SAME_ENGINE:
                need = False
            if (not d.dma) and (not op.dma) and d.eng == eng and eng == "pe":
                need = False
            op.deps.append((d, need))
            if need:
                waits.append(d)
                d.sig = True
        op.waits = waits
        for k in r:
            self.readers.setdefault(k, []).append(op)
        for k in w:
            self.lastw[k] = op
            self.readers[k] = []
        self.ops.append(op)
        return op

    def schedule(self):
        import heapq
        ops = self.ops
        n = len(ops)
        succ = [[] for _ in range(n)]
        indeg = [0] * n
        for op in ops:
            for d, _ in op.deps:
                succ[d.idx].append(op.idx)
                indeg[op.idx] += 1
        bl = [0.0] * n
        for op in reversed(ops):
            m = 0.0
            for s in succ[op.idx]:
                if bl[s] > m:
                    m = bl[s]
            bl[op.idx] = op.cost + op.lat + m
        for op in ops:
            bl[op.idx] += op.boost
        eng_free = {e: 0.0 for e in self.ENG}
        future = {e: [] for e in self.ENG}
        avail = {e: [] for e in self.ENG}
        ready_t = [0.0] * n
        finish = [0.0] * n
        for op in ops:
            if indeg[op.idx] == 0:
                heapq.heappush(future[op.eng], (0.0, op.idx))
        order = {e: [] for e in self.ENG}
        done = 0
        cur_tbl = [None]
        while done < n:
            best_e, best_t = None, None
            for e in self.ENG:
                if avail[e]:
                    t = eng_free[e]
                elif future[e]:
                    t = max(eng_free[e], future[e][0][0])
                else:
                    continue
                if best_t is None or t < best_t:
                    best_e, best_t = e, t
            e, t = best_e, best_t
            while future[e] and future[e][0][0] <= t + 1e-9:
                rt, i = heapq.heappop(future[e])
                heapq.heappush(avail[e], (-bl[i], i))
            extra = 0.0
            if e == "act":
                comp = [x for x in avail[e] if ops[x[1]].tbl is None or ops[x[1]].tbl == cur_tbl[0]]
                if comp:
                    pick = min(comp)
                    avail[e].remove(pick)
                    heapq.heapify(avail[e])
                    i = pick[1]
                else:
                    soon = [x for x in future[e] if x[0] <= t + 0.9 and (ops[x[1]].tbl is None or ops[x[1]].tbl == cur_tbl[0])]
                    if soon:
                        pick = min(soon)
                        future[e].remove(pick)
                        heapq.heapify(future[e])
                        t = pick[0]
                        i = pick[1]
                    else:
                        _, i = heapq.heappop(avail[e])
                        extra = 1.3
                if ops[i].tbl is not None:
                    cur_tbl[0] = ops[i].tbl
            else:
                _, i = heapq.heappop(avail[e])
            op = ops[i]
            order[e].append(op)
            t = t + extra
            eng_free[e] = t + op.cost
            finish[i] = t + op.cost + op.lat
            done += 1
            for s in succ[i]:
                so = ops[s]
                lat = 0.0 if (so.eng == op.eng and not op.dma and not so.dma) else 0.8
                rt = finish[i] + lat
                if rt > ready_t[s]:
                    ready_t[s] = rt
                indeg[s] -= 1
                if indeg[s] == 0:
                    heapq.heappush(future[so.eng], (ready_t[s], s))
        self.order = order
        self.est = max(finish) if n else 0.0
        return order


def _consts():
    cols = {}
    parts = []

    def put(name, a):
        a = np.asarray(a, np.float32)
        off = sum(p.shape[1] for p in parts)
        cols[name] = (off, a.shape[1])
        parts.append(a)

    idx = np.arange(128)
    masks = []
    put("ident", np.eye(128))
    put("ones", np.ones((128, 128)))
    for g in ("p", "s"):
        if g == "p":
            seq = np.zeros(128, np.int64); tim = idx.copy(); nseq = 1
        else:
            seq = idx % 16; tim = idx // 16; nseq = 16
        same = seq[:, None] == seq[None, :]
        tr_ = tim[:, None]
        tc_ = tim[None, :]
        masks.append(np.where(same & (tc_ < tr_), 0.0, -NEG))
        masks.append(np.where(same & (tc_ > tr_), 0.0, NEG))
        masks.append(np.where(same & (tc_ >= tr_), 0.0, NEG))
        put("tri" + g, (same & (tr_ <= tc_)).astype(np.float32))
        put("same" + g, same.astype(np.float32))
        lm = np.zeros((128, 16), np.float32)
        rm = np.zeros((128, 16), np.float32)
        for s in range(nseq):
            members = np.where(seq == s)[0]
            lm[members[np.argmax(tim[members])], s] = 1.0
            rm[members, s] = 1.0
        put("lastm" + g, lm)
        put("rowm" + g, rm)
    ic = np.zeros((128, 4 * 16), np.float32)
    for wi, w in enumerate((2, 4, 8, 16)):
        ic[:, wi * 16:(wi + 1) * 16] = 1.0 / np.minimum(np.arange(16) + 1, w)
    put("invc", ic)
    cols["_masks"] = np.concatenate([np.asarray(m, np.float32) for m in masks], axis=1)
    return np.concatenate(parts, axis=1), cols


def build(debug=()):
    nc = bass.Bass("TRN2", target_bir_lowering=False)
    S = Sched()
    es = ExitStack()
    cst_np, CC = _consts()
    NCST = cst_np.shape[1]
    LP = 256
    NST = 2048 // LP

    def din(name, shape):
        return nc.dram_tensor(name, list(shape), F32, kind="ExternalInput").ap()

    def dout(name, shape):
        return nc.dram_tensor(name, list(shape), F32, kind="ExternalOutput").ap()

    xp = din("xp", [2048, D]); xs = din("xs", [128, D]); cexp = din("cexp", [2, 128, D])
    spool = din("spool", [16, 15, D]); sconv = din("sconv", [16, 3, 3 * D]); sdelta = din("sdelta", [16, NH, 128, 128])
    wimg = din("wimg", [22, 128, 8, 512]); wada_img = din("wada_img", [6, 128, 8, 512]); wba_img = din("wba_img", [128, 8, 16])
    pool_w = din("pool_w", [4, 256, 256])
    w_ada = w_in = p_a = p_b = w_out = None
    bada = din("bada", [128, 3 * D]); prm = din("prm", [128, 2176]); cst = din("cst", [128, NCST]); msk = din("msk", [128, 768])
    yp = dout("yp", [2048, D]); ys = dout("ys", [128, D]); poolp = dout("poolp", [15, D]); convp = dout("convp", [3, 3 * D])
    deltap = dout("deltap", [NH, 128, 128]); pools = dout("pools", [16, 15, D]); convs = dout("convs", [16, 3, 3 * D])
    deltas = dout("deltas", [16, NH, 128, 128])
    WS = nc.dram_tensor("ws_bf16", [22, 128, 8, 512], BF16, kind="Internal").ap()

    def sb(name, shape, dt=F32):
        return es.enter_context(nc.sbuf_tensor(name, list(shape), dt))

    def V(t, off, dims, np_=128, p0=0):
        base = t[:]
        ps = base.ap[0][0]
        return bass.AP(t, p0 * ps + off, [[ps, np_]] + [list(d) for d in dims])

    CST = sb("CST", [128, NCST]); PRM = sb("PRM", [128, 2176])
    IDB = sb("IDB", [128, 128], BF16)
    MSKB = sb("MSKB", [128, 6, 128], BF16)
    POOLW = sb("POOLW", [128, 4, 2, 256], BF16); WBA = sb("WBA", [128, 8, 16], BF16)
    ADA = [[sb("ADA%d_%d" % (g, i), [128, D], BF16) for i in range(3)] for g in range(2)]
    NEGA = sb("NEGA", [128, 8]); DUMMY = sb("DUMMY", [128, 8])
    NWB = 4
    WB = [sb("WB%d" % i, [128, 8, 512], BF16) for i in range(NWB)]
    SETN = ("ZA", "ZB", "QT", "KT", "VT", "SGA", "SGB")
    SET = [{n: sb("%s%d" % (n, s), [128, 8, LP], BF16) for n in SETN} for s in range(2)]
    BAS = [sb("BA%d" % s, [128, 2, 16]) for s in range(2)]
    SSQS = [sb("SSQ%d" % s, [128, 2, 16]) for s in range(2)]
    HT = sb("HT", [128, 8, LP], BF16); MRG = sb("MRG", [128, 8, LP], BF16)
    XT = [sb("XT%d" % i, [128, D]) for i in range(2)]
    HF = sb("HF", [128, D]); HB = sb("HB", [128, D], BF16)
    EW = 368
    EXTP = sb("EXTP", [128, 2, EW])
    PW = [sb("PW%d" % i, [128, 2, EW]) for i in range(2)]
    POOLED = sb("POOLED", [128, 2, LP], BF16)
    EXTC = [sb("EXTC%d" % i, [128, LP + 3]) for i in range(2)]
    CACC = sb("CACC", [128, LP]); SQ = sb("SQ", [128, LP], BF16); ONB = sb("ONB", [128, 8], BF16); CACC2 = sb("CACC2", [128, LP])
    PHIST = sb("PHIST", [128, 8, 15]); CHIST = sb("CHIST", [128, 24, 3])
    TOK = [sb("TOK%d" % i, [128, 128]) for i in range(2)]
    UTOK = HF
    RR = sb("RR", [128, D]); HF2 = sb("HF2", [128, D]); STAT = sb("STAT", [128, 2, 6]); MV = sb("MV", [128, 4])
    SCB = [sb("SC%d" % i, [128, 18, 8]) for i in range(2)]
    DG = sb("DG", [128, 8, 128])
    TF = DG
    QG = sb("QG", [128, 8, 128], BF16)
    AM = [sb("AM%d" % i, [128, 8, 128], BF16) for i in range(2)]
    AT = [sb("AT%d" % i, [128, 8, 128], BF16) for i in range(2)]
    YM = [sb("YM%d" % i, [128, 8, 128], BF16) for i in range(2)]
    QKT = sb("QKT", [128, 8, 128], BF16)
    ATO = sb("ATO", [128, 8, 128], BF16); RES = sb("RES", [128, 8, 128], BF16)
    KTOK = sb("KTOK", [128, 8, 128], BF16); BV = sb("BV", [128, 8, 128], BF16)
    BM = sb("BM", [128, 8, 128], BF16); UM = sb("UM", [128, 8, 128], BF16)
    SF = sb("SF", [128, 8, 128]); SBF = sb("SBF", [128, 8, 128], BF16)
    JUNK = sb("JUNK", [128, 128], BF16)
    GL = sb("GL", [128, 16, 8])
    S0BH = [(ADA[1][0], "ADA1_0"), (ADA[1][1], "ADA1_1")]
    UBLK = MRG
    PS = [es.enter_context(nc.psum_tensor("PS%d" % i, [128, 1024], F32)) for i in range(4)]

    def psb(i, half):
        return PS[i][:, half * 512:(half + 1) * 512]

    def c_(name, lo=0, hi=None):
        off, n = CC[name]
        hi = n if hi is None else hi
        return CST[:, off + lo:off + hi]

    def coff(name):
        return CC[name][0]

    LNG, LNB, HNW, PSC, CW, ALOG, DTB = 0, 1024, 2048, 2049, 2057, 2153, 2161
    ident = c_("ident"); ones = c_("ones")

    outc = [0]

    TBL = {AF.Exp: 'exp', AF.Ln: 'ln', AF.Silu: 'silu', AF.Sigmoid: 'sig'}

    def fsz(ap):
        n = 1
        for s in ap.shape[1:]:
            n *= s
        return n

    def dma(q, out, in_, r, w, sem):
        nbytes = 4.0 * fsz(out) * out.shape[0]
        S.add(q, lambda e: e.dma_start(out=out, in_=in_), r=r, w=w, dma_sem=sem, cost=0.06, lat=2.0 + nbytes / 150e3)

    def dma_out(out, in_, r, sem):
        outc[0] += 1
        dma("sp", out, in_, r=r, w=["OUT%d" % outc[0]], sem=sem)

    def mm(out, lhsT, rhs, start, stop, r, w):
        npass = 4.0 if lhsT.dtype == F32 else 1.0
        c = max(fsz(out), 48) * npass / 1400.0 + 0.012
        S.add("pe", lambda e: e.matmul(out, lhsT, rhs, start=start, stop=stop), r=r, w=w, cost=c)

    def tr(out, in_, idn, r, w):
        npass = 2.0 if in_.dtype == F32 else 1.0
        c = max(fsz(out), 48) * npass / 1400.0 + 0.012
        S.add("pe", lambda e: e.transpose(out, in_, idn), r=r, w=w, cost=c)

    def act(out, in_, func, r, w, bias=None, scale=None, accum=None):
        kw = {}
        if bias is not None:
            kw["bias"] = bias
        if scale is not None:
            kw["scale"] = scale
        if accum is not None:
            kw["accum_out"] = accum
        tbl = TBL.get(func)
        S.add("act", lambda e: e.activation(out=out, in_=in_, func=func, **kw), r=r, w=w, cost=(200 + fsz(out)) / 1200.0, tbl=tbl)

    def dcost(out, eng="dve"):
        c = (140 + fsz(out)) / 960.0
        return c * 2.0 if eng == "pool" else c

    def tt(out, a, b, op, r, w, eng="dve"):
        S.add(eng, lambda e: e.tensor_tensor(out=out, in0=a, in1=b, op=op), r=r, w=w, cost=dcost(out, eng))

    def ts(out, a, s1, s2, op0, op1, r, w, eng="dve"):
        if op1 is None:
            S.add(eng, lambda e: e.tensor_scalar(out=out, in0=a, scalar1=s1, scalar2=None, op0=op0), r=r, w=w, cost=dcost(out, eng))
        else:
            S.add(eng, lambda e: e.tensor_scalar(out=out, in0=a, scalar1=s1, scalar2=s2, op0=op0, op1=op1), r=r, w=w, cost=dcost(out, eng))

    def stt(out, a, sc, b, op0, op1, r, w, eng="dve"):
        S.add(eng, lambda e: e.scalar_tensor_tensor(out=out, in0=a, scalar=sc, in1=b, op0=op0, op1=op1), r=r, w=w, cost=dcost(out, eng))

    def cp(out, in_, r, w, eng="dve"):
        if eng == "act":
            S.add("act", lambda e: e.copy(out=out, in_=in_), r=r, w=w, cost=(200 + fsz(out)) / 1200.0)
        else:
            S.add(eng, lambda e: e.tensor_copy(out=out, in_=in_), r=r, w=w, cost=dcost(out))

    def ms(out, val, w, eng="dve"):
        S.add(eng, lambda e: e.memset(out, val), r=(), w=w, cost=dcost(out))

    dma("sp", CST[:], cst, r=[], w=["CST"], sem="cst")
    dma("sp", PRM[:], prm, r=[], w=["PRM"], sem="prm")
    dma("pool", POOLW[:], pool_w.rearrange("g (k p) m -> p g k m", p=128), r=[], w=["POOLW"], sem="poolw")
    dma("pool", WBA[:], wba_img, r=[], w=["WBA"], sem="wba")
    cp(IDB[:], ident, r=["CST"], w=["IDB"])
    ms(ONB[:], 1.0, w=["ONB"])
    ms(PW[0][:], 0.0, w=["PW0"])
    ms(PW[1][:], 0.0, w=["PW1"])
    dma("pool", MSKB[:], msk.rearrange("p (m t) -> p m t", m=6), r=[], w=["MSKB"], sem="mskb")
    act(NEGA[:], PRM[:, ALOG:ALOG + 8], AF.Exp, r=["PRM"], w=["NEGA0"])
    ts(NEGA[:], NEGA[:], -1.0, None, ALU.mult, None, r=["NEGA0"], w=["NEGA"])

    WG = {}
    glist = []
    for col0 in list(range(0, 6144, 512)) + [6160, 6672, 7184, 7696]:
        WG[("w_in", col0)] = len(glist); glist.append((w_in, col0))
    for nm, t in (("p_a", p_a), ("p_b", p_b), ("w_out", w_out)):
        for col0 in (0, 512):
            WG[(nm, col0)] = len(glist); glist.append((t, col0))
    wrot = [0]

    def wload(name, src, col0, direct=False):
        i = wrot[0] % NWB
        wrot[0] += 1
        wk = "WB%d" % i
        if direct:
            img = wada_img[col0 // 512] if name == "w_ada" else wimg[WG[(name, col0)]]
            dma("pool", WB[i][:], img, r=[], w=[wk], sem=wk.lower())
            if (name, col0) in WG:
                gidx = WG[(name, col0)]
                dma("sp", WS[gidx], WB[i][:], r=[wk], w=["WS%d" % gidx], sem="wst%d" % i)
        else:
            gidx = WG[(name, col0)]
            dma("sp", WB[i][:], WS[gidx], r=["WS%d" % gidx], w=[wk], sem=wk.lower() + "h")
        return WB[i], wk

    pmrot = [0]

    PMB = [(psb(0, 0), "PS0_0"), (psb(0, 1), "PS0_1"), (psb(1, 1), "PS1_1")]

    def pm_next():
        i = pmrot[0] % 3
        pmrot[0] += 1
        return PMB[i]

    PT = PS[1][:, 0:256]; PTK = "PS1_0"
    PX = PS[1][:, 256:512]; PXK = "PS1_0"

    ORDER = [0, NST] + list(range(1, NST))
    SETI = {st: pos % 2 for pos, st in enumerate(ORDER)}

    def ada(gi):
        AK = ["ADA%d_%d" % (gi, i) for i in range(3)]
        dma("sp", XT[0][:], cexp[gi], r=[], w=["XT0"], sem="xt0")
        act(HB[:], XT[0][:], AF.Silu, r=["XT0"], w=["HB"])
        ptb = PT.bitcast(BF16)
        for hf in range(2):
            for k4 in range(4):
                k = hf * 4 + k4
                tr(ptb[:, k4 * 128:(k4 + 1) * 128], HB[:, k * 128:(k + 1) * 128], IDB[:], r=["HB", "IDB"], w=[PTK])
            cp(HT[:, hf * 4:(hf + 1) * 4, 0:128], ptb[:, 0:512].rearrange("p (k t) -> p k t", k=4), r=[PTK], w=["HT"])
        for grp in range(6):
            wt, wk = wload("w_ada", w_ada, grp * 512, direct=True)
            dma("sp", HF[:, 0:512], bada[:, grp * 512:(grp + 1) * 512], r=[], w=["HF"], sem="hf")
            pm, pmk = pm_next()
            for k in range(8):
                mm(pm, HT[:, k, 0:128], wt[:, k, :], k == 0, k == 7, r=["HT", wk], w=[pmk])
            which = grp // 2
            dst = ADA[gi][which][:, (grp % 2) * 512:(grp % 2 + 1) * 512]
            if which == 0:
                tt(dst, pm, HF[:, 0:512], ALU.add, r=[pmk, "HF"], w=[AK[which]])
            elif which == 1:
                stt(dst, pm, 1.0, HF[:, 0:512], ALU.add, ALU.add, r=[pmk, "HF"], w=[AK[which]])
            else:
                stt(HF[:, 512:1024], pm, 1.0, HF[:, 0:512], ALU.add, ALU.add, r=[pmk, "HF"], w=["HF"])
                ts(dst, HF[:, 512:1024], 0.5, None, ALU.mult, None, r=["HF"], w=[AK[which]])
            yield

    def P(st):
        samp = st == NST
        gi = 1 if samp else 0
        if samp:
            yield from ada(1)
        si = SETI[st]
        T = SET[si]
        K_ = {n: "%s%d" % (n, si) for n in SETN}
        BA, SSQ = BAS[si], SSQS[si]
        bak, ssk = "BA%d" % si, "SSQ%d" % si
        L = 128 if samp else LP
        ntt = L // 128
        TS_ = 16 if samp else 1
        W = 15 * TS_ + L
        first = st == 0
        last = samp or st == NST - 1
        direct = st == 0
        xsrc = xs if samp else xp[st * LP:(st + 1) * LP]
        shift, sc1, g1 = ADA[gi]
        AK = ["ADA%d_%d" % (gi, i) for i in range(3)]

        for t_ in range(ntt):
            xt, xk = XT[t_ % 2], "XT%d" % (t_ % 2)
            dma("sp", xt[:], xsrc[t_ * 128:(t_ + 1) * 128, :], r=[], w=[xk], sem=xk.lower())
            tt(HF[:], xt[:], sc1[:], ALU.mult, r=[xk, AK[1]], w=["HF"])
            tt(HB[:], HF[:], shift[:], ALU.add, r=["HF", AK[0]], w=["HB"])
            ptb = PT.bitcast(BF16)
            for hf in range(2):
                for k4 in range(4):
                    k = hf * 4 + k4
                    tr(ptb[:, k4 * 128:(k4 + 1) * 128], HB[:, k * 128:(k + 1) * 128], IDB[:], r=["HB", "IDB"], w=[PTK])
                cp(HT[:, hf * 4:(hf + 1) * 4, t_ * 128:(t_ + 1) * 128], ptb[:, 0:512].rearrange("p (k t) -> p k t", k=4), r=[PTK], w=["HT"], eng="act")
            yield
        for t_ in range(ntt):
            for k in range(8):
                mm(PX[:, 0:16], HT[:, k, t_ * 128:(t_ + 1) * 128], WBA[:, k, :], k == 0, k == 7, r=["HT", "WBA"], w=[PXK])
            cp(BA[:, t_, :], PX[:, 0:16], r=[PXK], w=[bak], eng="act")
        if samp:
            for half in range(2):
                ni = 8 if half == 0 else 7
                for ii in range(ni):
                    i = half * 8 + ii
                    dma("sp", XT[half][ii * 16:(ii + 1) * 16, :], spool[:, i, :], r=[], w=["XT%d" % half], sem="xt%d" % half)
            dma_out(pools[:, 0:7, :], spool[:, 8:15, :], r=[], sem="o_dd")
        yield

        def proj(src_cols, consumer):
            for grp in range(2):
                wt, wk = wload("w_in", w_in, src_cols + grp * 512, direct=direct)
                for cc in range(4):
                    c = grp * 4 + cc
                    pm, pmk = pm_next()
                    for k in range(8):
                        mm(pm[:, 0:L], wt[:, k, cc * 128:(cc + 1) * 128], HT[:, k, 0:L], k == 0, k == 7, r=["HT", wk], w=[pmk])
                    consumer(c, pm[:, 0:L], pmk)
                    yield

        def cons_act(dst, dkey, func):
            def f(c, pm, pmk):
                act(dst[:, c, 0:L], pm, func, r=[pmk], w=[dkey])
            return f

        ZA, ZB = T["ZA"], T["ZB"]
        yield from proj(1024, cons_act(ZA, K_["ZA"], AF.Silu))
        ext, ek = EXTP, "EXTP"

        def cons_ua(c, pm, pmk):
            j = c % 2
            hist = V(ext, j * EW, [[1, 15 * TS_]])
            new = V(ext, j * EW + 15 * TS_, [[1, L]])
            hk, nk = ek + "h%d" % j, ek + "n%d" % j
            if first:
                ms(hist, 0.0, w=[hk])
            elif samp:
                tr(PT[:, 0:128], XT[0][:, c * 128:(c + 1) * 128], ident, r=["XT0", "CST"], w=[PTK])
                tr(PT[:, 128:240], XT[1][0:112, c * 128:(c + 1) * 128], ident[0:112, 0:112], r=["XT1", "CST"], w=[PTK])
                cp(hist, PT[:, 0:240], r=[PTK], w=[hk])
            else:
                cp(hist, PHIST[:, c, :], r=["PHIST%d" % c], w=[hk])
            cp(new, pm, r=[pmk], w=[nk], eng="act")
            if not last:
                cp(PHIST[:, c, :], V(ext, j * EW + L, [[1, 15]]), r=[nk], w=["PHIST%d" % c])
            else:
                tr(PT[:, 0:128], V(ext, j * EW + 15 * TS_ + L - 128, [[1, 128]]), ident, r=[nk, "CST"], w=[PTK])
                cp(UTOK[:, c * 128:(c + 1) * 128], PT[:, 0:128], r=[PTK], w=["HF"])
            if j == 1:
                gidx = c // 2
                w_ = 2 << gidx
                ekeys = [ek + "h0", ek + "h1", ek + "n0", ek + "n1"]

                def e2(t, off, n):
                    return V(t, off, [[EW, 2], [1, n]])
                cur, curk = ext, ekeys
                sh = 1
                pi = 0
                while sh < w_:
                    dst, dk = PW[pi], "PW%d" % pi
                    o_ = sh * TS_
                    tt(e2(dst, o_, W - o_), e2(cur, o_, W - o_), e2(cur, 0, W - o_), ALU.add, r=curk, w=[dk])
                    cur, curk = dst, [dk]
                    pi ^= 1
                    sh *= 2
                stt(V(POOLED, 0, [[LP, 2], [1, L]]), e2(cur, 15 * TS_, L), 1.0 / w_, e2(ext, 15 * TS_, L), ALU.mult, ALU.subtract,
                    r=curk + ekeys, w=["POOLED"])
                if first:
                    icv = V(CST, coff("invc") + gidx * 16, [[0, 2], [1, 16]])
                    tt(PW[pi][:, :, 0:16], e2(cur, 15, 16), icv, ALU.mult, r=curk + ["CST"], w=["PW%d" % pi])
                    tt(V(POOLED, 0, [[LP, 2], [1, 16]]), PW[pi][:, :, 0:16], e2(ext, 15, 16), ALU.subtract,
                       r=["PW%d" % pi] + ekeys + ["POOLED"], w=["POOLED"])
                for m in range(2):
                    pm2, pm2k = pm_next()
                    for k in range(2):
                        mm(pm2[:, 0:L], POOLW[:, gidx, k, m * 128:(m + 1) * 128], POOLED[:, k, 0:L], k == 0, k == 1,
                           r=["POOLW", "POOLED"], w=[pm2k])
                    cidx = 2 * gidx + m
                    stt(ZA[:, cidx, 0:L], pm2[:, 0:L], PRM[:, PSC + cidx:PSC + cidx + 1], ZA[:, cidx, 0:L], ALU.mult, ALU.mult,
                        r=[pm2k, "PRM", K_["ZA"]], w=[K_["ZA"]])

        yield from proj(0, cons_ua)
        if last:
            if samp:
                for t in range(8):
                    dma_out(pools[:, 7 + t, :], UTOK[t * 16:(t + 1) * 16, :], r=["HF"], sem="o_utok")
            else:
                dma_out(poolp[:, :], UTOK[113:128, :], r=["HF"], sem="o_utok")
        yield from proj(5120, cons_act(ZB, K_["ZB"], AF.Silu))

        def cons_qkv(typ):
            dst = (T["QT"], T["KT"], T["VT"])[typ]
            dkey = (K_["QT"], K_["KT"], K_["VT"])[typ]

            def f(h, pm, pmk):
                c = typ * 8 + h
                ei = c % 2
                extc, eck = EXTC[ei], "EXTC%d" % ei
                hist = V(extc, 0, [[1, 3 * TS_]])
                new = V(extc, 3 * TS_, [[1, L]])
                if first:
                    ms(hist, 0.0, w=[eck + "h"])
                elif samp:
                    if h == 0:
                        for i in range(3):
                            dma("sp", HF[i * 16:(i + 1) * 16, :], sconv[:, i, typ * 1024:(typ + 1) * 1024], r=[], w=["HF"], sem="hf")
                    tr(PT[:, 0:48], HF[0:48, h * 128:(h + 1) * 128], ident[0:48, 0:48], r=["HF", "CST"], w=[PTK])
                    cp(hist, PT[:, 0:48], r=[PTK], w=[eck + "h"])
                else:
                    cp(hist, CHIST[:, c, :], r=["CHIST%d" % c], w=[eck + "h"])
                cp(new, pm, r=[pmk], w=[eck + "n"], eng="act")
                if not last:
                    cp(CHIST[:, c, :], V(extc, L, [[1, 3]]), r=[eck + "n"], w=["CHIST%d" % c])
                else:
                    tk, tkk = TOK[c % 2], "TOK%d" % (c % 2)
                    tr(PT[:, 0:128], V(extc, 3 * TS_ + L - 128, [[1, 128]]), ident, r=[eck + "n", "CST"], w=[PTK])
                    cp(tk[:], PT[:, 0:128], r=[PTK], w=[tkk])
                    if samp:
                        for i in range(3):
                            dma_out(convs[:, i, c * 128:(c + 1) * 128], tk[(5 + i) * 16:(6 + i) * 16, :], r=[tkk], sem="o_" + tkk.lower())
                    else:
                        dma_out(convp[:, c * 128:(c + 1) * 128], tk[125:128, :], r=[tkk], sem="o_" + tkk.lower())
                acc = CACC[:, 0:L]
                for i in range(4):
                    src = V(extc, i * TS_, [[1, L]])
                    cwc = PRM[:, CW + c * 4 + i:CW + c * 4 + i + 1]
                    if i == 0:
                        ts(acc, src, cwc, None, ALU.mult, None, r=[eck + "h", eck + "n", "PRM"], w=["CACC"])
                    else:
                        stt(acc, src, cwc, acc, ALU.mult, ALU.add, r=[eck + "h", eck + "n", "PRM", "CACC"], w=["CACC"])
                act(dst[:, h, 0:L], acc, AF.Silu, r=["CACC"], w=[dkey])
                if typ < 2:
                    act(SQ[:, 0:L], dst[:, h, 0:L], AF.Square, r=[dkey], w=["SQ"])
                    for t_ in range(ntt):
                        col = 32 + t_ * 16 + typ * 8 + h
                        mm(PX[:, col:col + 1], SQ[:, t_ * 128:(t_ + 1) * 128], ONB[:, 0:1], True, True, r=["SQ", "ONB"], w=[PXK])
            return f

        yield from proj(2048, cons_qkv(0))
        yield from proj(3072, cons_qkv(1))
        cp(SSQ[:, 0:ntt, :], PX[:, 32:32 + ntt * 16].rearrange("p (t c) -> p t c", t=ntt), r=[PXK], w=[ssk])
        yield from proj(4096, cons_qkv(2))
        def cons_tanh(dst, dkey):
            def f(c, pm, pmk):
                act(dst[:, c, 0:L], pm, AF.Tanh, r=[pmk], w=[dkey], scale=0.5)
            return f
        yield from proj(6160, cons_tanh(T["SGA"], K_["SGA"]))
        yield from proj(7184, cons_tanh(T["SGB"], K_["SGB"]))

    def DF(st):
        samp = st == NST
        gi = 1 if samp else 0
        si = SETI[st]
        T = SET[si]
        K_ = {n: "%s%d" % (n, si) for n in SETN}
        L = 128 if samp else LP
        ntt = L // 128
        for t_ in range(ntt):
            yield from delta_tile(st, t_)
        xsrc = xs if samp else xp[st * LP:(st + 1) * LP]
        ydst = ys if samp else yp[st * LP:(st + 1) * LP]
        g1 = ADA[gi][2]
        g1k = "ADA%d_2" % gi
        for bi, (wn, wsrc, ysrc, yk, gsrc, gk) in enumerate((("p_a", p_a, T["ZA"], K_["ZA"], T["SGA"], K_["SGA"]),
                                                             ("p_b", p_b, T["ZB"], K_["ZB"], T["SGB"], K_["SGB"]))):
            for grp in range(2):
                wt, wk = wload(wn, wsrc, grp * 512, direct=(st == 0))
                for cc in range(4):
                    c = grp * 4 + cc
                    pm, pmk = pm_next()
                    for k in range(8):
                        mm(pm[:, 0:L], wt[:, k, cc * 128:(cc + 1) * 128], ysrc[:, k, 0:L], k == 0, k == 7, r=[yk, wk], w=[pmk])
                    if bi == 0:
                        stt(MRG[:, c, 0:L], gsrc[:, c, 0:L], 1.0, pm[:, 0:L], ALU.add, ALU.mult, r=[pmk, gk], w=["MRG"])
                    else:
                        stt(CACC2[:, 0:L], gsrc[:, c, 0:L], 1.0, pm[:, 0:L], ALU.add, ALU.mult, r=[pmk, gk], w=["CACC2"])
                        tt(MRG[:, c, 0:L], MRG[:, c, 0:L], CACC2[:, 0:L], ALU.add, r=["MRG", "CACC2"], w=["MRG"])
                    yield
        wts = [wload("w_out", w_out, nh * 512, direct=(st == 0)) for nh in range(2)]
        for t_ in range(ntt):
            dma("sp", RR[:], xsrc[t_ * 128:(t_ + 1) * 128, :], r=[], w=["RR"], sem="rr")
            for nh in range(2):
                wt, wk = wts[nh]
                pm, pmk = pm_next()
                for k in range(8):
                    mm(pm, MRG[:, k, t_ * 128:(t_ + 1) * 128], wt[:, k, :], k == 0, k == 7, r=["MRG", wk], w=[pmk])
                tt(HF2[:, nh * 512:(nh + 1) * 512], pm, g1[:, nh * 512:(nh + 1) * 512], ALU.mult, r=[pmk, g1k], w=["HF2"])
            stt(RR[:], RR[:], float(2 ** 0.25), HF2[:], ALU.mult, ALU.add, r=["RR", "HF2"], w=["RR"])
            yield
            for i in range(2):
                S.add("dve", (lambda i: lambda e: e.bn_stats(out=STAT[:, i, :], in_=RR[:, i * 512:(i + 1) * 512]))(i), r=["RR"], w=["STAT"], cost=0.6)
            S.add("dve", lambda e: e.bn_aggr(out=MV[:, 0:2], in_=STAT[:]), r=["STAT"], w=["MV"])
            ts(MV[:, 2:3], MV[:, 1:2], 1e-5, None, ALU.add, None, r=["MV"], w=["MV2"])
            act(MV[:, 2:3], MV[:, 2:3], AF.Ln, r=["MV2"], w=["MV2"])
            act(MV[:, 2:3], MV[:, 2:3], AF.Exp, r=["MV2"], w=["MV2"], scale=-0.5)
            ts(RR[:], RR[:], MV[:, 0:1], MV[:, 2:3], ALU.subtract, ALU.mult, r=["RR", "MV", "MV2"], w=["RR"])
            tt(RR[:], RR[:], PRM[:, LNG:LNG + 1024], ALU.mult, r=["RR", "PRM"], w=["RR"])
            tt(HF2[:], RR[:], PRM[:, LNB:LNB + 1024], ALU.add, r=["RR", "PRM"], w=["HF2"])
            dma_out(ydst[t_ * 128:(t_ + 1) * 128, :], HF2[:], r=["HF2"], sem="o_hf2")
            yield

    tcount = [0]

    def delta_tile(st, t_):
        samp = st == NST
        g = "s" if samp else "p"
        si = SETI[st]
        T = SET[si]
        QT, KT_, VT, ZB = T["QT"], T["KT"], T["VT"], T["ZB"]
        qk_, kk_, vk_, zbk = "QT%d" % si, "KT%d" % si, "VT%d" % si, "ZB%d" % si
        BA, SSQ = BAS[si], SSQS[si]
        bak, ssk = "BA%d" % si, "SSQ%d" % si
        nl = 2 if samp else 6
        cols = slice(t_ * 128, (t_ + 1) * 128)
        first = st == 0 and t_ == 0
        lastp = st == NST - 1 and t_ == (LP // 128 - 1)
        PA, PB, PC, PD = PS[2], PS[3], PS[2], PS[3]
        E1, E2, E3 = AM[1], AT[1], YM[1]
        e1k, e2k, e3k = "AM1", "AT1", "YM1"
        GQ = BM
        ON = BM
        pkm = {id(PS[2]): 2, id(PS[3]): 3}

        def pk(P_):
            i = 2 if P_ is PS[2] else 3
            return ["PS%d_0" % i, "PS%d_1" % i]

        def p3(P_):
            return P_[:, :].rearrange("p (h t) -> p h t", h=8)

        tcount[0] += 1
        sbi = tcount[0] % 2
        SC = SCB[sbi]

        def k(i):
            return "SC%dr%d" % (sbi, i)

        def scol(i):
            return SC[:, i, :]

        def sbc(i):
            return V(SC, i * 8, [[1, 8], [0, 128]])
        braw = BA[:, t_, 0:8]; araw = BA[:, t_, 8:16]
        act(scol(0), braw, AF.Exp, r=[bak], w=[k(0)], scale=-1.0)
        tt(scol(1), araw, PRM[:, DTB:DTB + 8], ALU.add, r=[bak, "PRM"], w=[k(1)])
        act(scol(1), scol(1), AF.Exp, r=[k(1)], w=[k(1)])
        ts(SC[:, 0:2, :], SC[:, 0:2, :], 1.0, None, ALU.add, None, r=[k(0), k(1)], w=[k(0), k(1)])
        ts(SC[:, 2:4, :], SSQ[:, t_, :].rearrange("p (a b) -> p a b", a=2), 1e-6, None, ALU.add, None, r=[ssk], w=[k(2), k(3)])
        act(SC[:, 0:4, :], SC[:, 0:4, :], AF.Ln, r=[k(0), k(1), k(2), k(3)], w=[k(0), k(1), k(2), k(3)])
        act(scol(16), scol(0), AF.Exp, r=[k(0)], w=[k(16)], scale=-1.0)
        tt(scol(4), scol(1), NEGA[:], ALU.mult, r=[k(1), "NEGA"], w=[k(4)])
        mm(PD[:, 0:8], c_("tri" + g), scol(4), True, True, r=[k(4), "CST"], w=["PS3_0"])
        mm(PD[:, 8:16], c_("same" + g), scol(4), True, True, r=[k(4), "CST"], w=["PS3_0"])
        cp(SC[:, 5:7, :], PD[:, 0:16].rearrange("p (a b) -> p a b", a=2), r=["PS3_0"], w=[k(5), k(6)])
        ts(scol(7), scol(2), -0.5, math.log(128 ** -0.5), ALU.mult, ALU.add, r=[k(2)], w=[k(7)])
        ts(scol(8), scol(3), -0.5, None, ALU.mult, None, r=[k(3)], w=[k(8)])
        tt(scol(17), scol(5), scol(8), ALU.add, r=[k(5), k(8)], w=[k(17)])
        tt(scol(9), scol(17), scol(0), ALU.subtract, r=[k(17), k(0)], w=[k(9)])
        tt(scol(10), scol(5), scol(8), ALU.subtract, r=[k(5), k(8)], w=[k(10)])
        tt(scol(11), scol(5), scol(7), ALU.add, r=[k(5), k(7)], w=[k(11)])
        ts(scol(12), scol(10), -1.0, None, ALU.mult, None, r=[k(10)], w=[k(12)])
        tt(scol(13), scol(6), scol(10), ALU.subtract, r=[k(6), k(10)], w=[k(13)])
        act(scol(13), scol(13), AF.Exp, r=[k(13)], w=[k(13)])
        act(scol(14), scol(9), AF.Exp, r=[k(9)], w=[k(14)])
        ts(scol(14), scol(14), -1.0, None, ALU.mult, None, r=[k(14)], w=[k(14)])
        yield

        identb = V(CST, coff("ident"), [[0, 8], [1, 128]])

        def rowbc(P_, src_i, skey, maskidx):
            if maskidx is not None:
                for hf in range(2):
                    out = P_[:, hf * 512:(hf + 1) * 512].rearrange("p (h t) -> p h t", h=4)
                    mk = V(MSKB, maskidx * 128, [[0, 4], [1, 128]])
                    mm(out, IDB[:], mk, True, False, r=["MSKB", "IDB"], w=[pk(P_)[hf]])
            for h in range(NH):
                out = P_[:, h * 128:(h + 1) * 128]
                mm(out, V(SC, src_i * 8 + h, [[0, 128]]), ident, maskidx is None, maskidx is None or h % 4 == 3,
                   r=[skey, "CST"], w=[pk(P_)[h // 4]])

        mbase = 3 if samp else 0
        rowbc(PA, 10, k(10), mbase + 0)
        for h in range(NH):
            hs = slice(h * 128, (h + 1) * 128)
            act(E1[:, h, :], PA[:, hs], AF.Exp, r=pk(PA) + [k(9)], w=[e1k], bias=SC[:, 9, h:h + 1], scale=-1.0)
        yield
        rowbc(PC, 11, k(11), None)
        rowbc(PD, 11, k(11), mbase + 2)
        act(GQ[:], p3(PC), AF.Exp, r=pk(PC), w=["BM"])
        for h in range(NH):
            hs = slice(h * 128, (h + 1) * 128)
            act(E3[:, h, :], PD[:, hs], AF.Exp, r=pk(PD) + [k(12)], w=[e3k], bias=SC[:, 12, h:h + 1])
        tt(QG[:], QT[:, :, cols], GQ[:], ALU.mult, r=[qk_, "BM"], w=["QG"])
        yield
        for h in range(NH):
            hs = slice(h * 128, (h + 1) * 128)
            mm(PA[:, hs], KT_[:, h, cols], KT_[:, h, cols], True, True, r=[kk_], w=[pk(PA)[h // 4]])
            mm(PB[:, hs], KT_[:, h, cols], QT[:, h, cols], True, True, r=[kk_, qk_], w=[pk(PB)[h // 4]])
        tt(AM[0][:], p3(PA), E1[:], ALU.mult, r=pk(PA) + [e1k], w=["AM0"])
        tt(QKT[:], p3(PB), E3[:], ALU.mult, r=pk(PB) + [e3k], w=["QKT"])
        pbb = PB[:, :].bitcast(BF16)
        for h in range(NH):
            tr(pbb[:, h * 128:(h + 1) * 128], AM[0][:, h, :], IDB[:], r=["AM0", "IDB"], w=["PS3_0"])
        cp(ATO[:], pbb[:, 0:1024].rearrange("p (h t) -> p h t", h=8), r=["PS3_0"], w=["ATO"], eng="act")
        tt(YM[0][:], identb, ATO[:], ALU.subtract, r=["CST", "ATO"], w=["YM0"])
        yield
        pcb = PC[:, :].bitcast(BF16)
        for h in range(NH):
            tr(pcb[:, h * 128:(h + 1) * 128], KT_[:, h, cols], IDB[:], r=[kk_, "IDB"], w=["PS2_0"])
            tr(pcb[:, 1024 + h * 128:1024 + (h + 1) * 128], VT[:, h, cols], IDB[:], r=[vk_, "IDB"], w=["PS2_1"])
        tt(KTOK[:], pcb[:, 0:1024].rearrange("p (h t) -> p h t", h=8), sbc(13), ALU.mult, r=["PS2_0", k(13)], w=["KTOK"])
        tt(BV[:], pcb[:, 1024:2048].rearrange("p (h t) -> p h t", h=8), sbc(16), ALU.mult, r=["PS2_1", k(16)], w=["BV"])
        yield
        cur = 0
        for lvl in range(1, nl + 1):
            nxt = cur ^ 1
            atc, atck = (ATO, "ATO") if lvl == 1 else (AT[cur], "AT%d" % cur)
            for h in range(NH):
                hs = slice(h * 128, (h + 1) * 128)
                mm(PA[:, hs], atc[:, h, :], AM[cur][:, h, :], True, True, r=[atck, "AM%d" % cur], w=[pk(PA)[h // 4]])
                if lvl < nl:
                    mm(PB[:, hs], AM[cur][:, h, :], atc[:, h, :], True, True, r=[atck, "AM%d" % cur], w=[pk(PB)[h // 4]])
            cp(AM[nxt][:], p3(PA), r=pk(PA), w=["AM%d" % nxt], eng="act")
            if lvl < nl:
                cp(AT[nxt][:], p3(PB), r=pk(PB), w=["AT%d" % nxt], eng="act")
            yield
            for h in range(NH):
                hs = slice(h * 128, (h + 1) * 128)
                mm(PD[:, hs], AM[nxt][:, h, :], YM[cur][:, h, :], True, True, r=["AM%d" % nxt, "YM%d" % cur], w=[pk(PD)[h // 4]])
            tt(YM[nxt][:], p3(PD), YM[cur][:], ALU.add, r=pk(PD) + ["YM%d" % cur], w=["YM%d" % nxt])
            cur = nxt
            yield
        Y, yk = YM[cur], "YM%d" % cur
        if not samp:
            if first:
                ms(SF[:], 0.0, w=["SF"])
                ms(SBF[:], 0.0, w=["SBF"])
            for h in range(NH):
                hs = slice(h * 128, (h + 1) * 128)
                mm(PA[:, hs], KT_[:, h, cols], SBF[:, h, :], True, True, r=[kk_, "SBF"], w=[pk(PA)[h // 4]])
        else:
            for h in range(NH):
                for hf in range(2):
                    dma("pool", V(S0BH[hf][0], 0, [[128, 8], [1, 128]]), sdelta[hf * 8:(hf + 1) * 8, h, :, :].rearrange("s d e -> d s e"),
                        r=[], w=[S0BH[hf][1]], sem="s0b%d" % hf)
                for s in range(16):
                    sbt, sbk = S0BH[s // 8]
                    mm(V(PS[3], h * 128 + s, [[16, 8]]), V(sbt, (s % 8) * 128, [[1, 128]]), V(KT_, h * LP + s, [[16, 8]]), True, True,
                       r=[sbk, kk_], w=[pk(PB)[h // 4]])
                yield
            cp(TF[:], p3(PB), r=pk(PB), w=["DG"])
            for h in range(NH):
                hs = slice(h * 128, (h + 1) * 128)
                tr(PA[:, hs], TF[:, h, :], ident, r=["DG", "CST"], w=[pk(PA)[h // 4]])
        for h in range(NH):
            hs = slice(h * 128, (h + 1) * 128)
            stt(BM[:, h, :], PA[:, hs], SC[:, 14, h:h + 1], BV[:, h, :], ALU.mult, ALU.add, r=pk(PA) + [k(14), "BV"], w=["BM"])
        yield
        for h in range(NH):
            hs = slice(h * 128, (h + 1) * 128)
            mm(PD[:, hs], Y[:, h, :], BM[:, h, :], True, True, r=[yk, "BM"], w=[pk(PD)[h // 4]])
        cp(UM[:], p3(PD), r=pk(PD), w=["UM"], eng="act")
        yield
        if not samp:
            for h in range(NH):
                hs = slice(h * 128, (h + 1) * 128)
                mm(PB[:, hs], ATO[:, h, :], UM[:, h, :], True, True, r=["ATO", "UM"], w=[pk(PB)[h // 4]])
            tt(DG[:], BM[:], UM[:], ALU.subtract, r=["BM", "UM"], w=["DG"])
            tt(RES[:], DG[:], p3(PB), ALU.subtract, r=["DG"] + pk(PB), w=["RES"])
            yield
            for h in range(NH):
                hs = slice(h * 128, (h + 1) * 128)
                mm(PD[:, hs], Y[:, h, :], RES[:, h, :], True, True, r=[yk, "RES"], w=[pk(PD)[h // 4]])
            tt(UM[:], UM[:], p3(PD), ALU.add, r=["UM"] + pk(PD), w=["UM"])
            yield
            for h in range(NH):
                hs = slice(h * 128, (h + 1) * 128)
                mm(PC[:, hs], QG[:, h, :], SBF[:, h, :], True, False, r=["QG", "SBF"], w=[pk(PC)[h // 4]])
                mm(PC[:, hs], QKT[:, h, :], UM[:, h, :], False, True, r=["QKT", "UM"], w=[pk(PC)[h // 4]])
        else:
            for h in range(NH):
                hs = slice(h * 128, (h + 1) * 128)
                for hf in range(2):
                    dma("pool", V(S0BH[hf][0], 0, [[128, 8], [1, 128]]), sdelta[hf * 8:(hf + 1) * 8, h, :, :].rearrange("s d e -> d s e"),
                        r=[], w=[S0BH[hf][1]], sem="s0b%d" % hf)
                mm(PC[:, hs], UM[:, h, :], QKT[:, h, :], True, False, r=["UM", "QKT"], w=[pk(PC)[h // 4]])
                for s in range(16):
                    sbt, sbk = S0BH[s // 8]
                    mm(V(PS[2], h * 128 + s, [[16, 8]]), V(sbt, (s % 8) * 128, [[1, 128]]), V(QG, h * 128 + s, [[16, 8]]), False, s == 15,
                       r=[sbk, "QG"], w=[pk(PC)[h // 4]])
                yield
            cp(TF[:], p3(PC), r=pk(PC), w=["DG"])
            for h in range(NH):
                hs = slice(h * 128, (h + 1) * 128)
                tr(PC[:, hs], TF[:, h, :], ident, r=["DG", "CST"], w=[pk(PC)[h // 4]])
        yield
        ms(scol(15), 0.0, w=[k(15)])
        for h in range(NH):
            hs = slice(h * 128, (h + 1) * 128)
            act(JUNK[:], PC[:, hs], AF.Square, r=pk(PC) + [k(15)], w=["JUNK", k(15)], accum=SC[:, 15, h:h + 1])
        ts(scol(15), scol(15), 1.0 / 128, 1e-6, ALU.mult, ALU.add, r=[k(15)], w=[k(15)])
        act(scol(15), scol(15), AF.Ln, r=[k(15)], w=[k(15)])
        act(scol(15), scol(15), AF.Exp, r=[k(15)], w=[k(15)], scale=-0.5)
        tt(ON[:], p3(PC), sbc(15), ALU.mult, r=pk(PC) + [k(15)], w=["BM"])
        yield
        pab = PA[:, :].bitcast(BF16)
        for h in range(NH):
            tr(pab[:, h * 128:(h + 1) * 128], ON[:, h, :], IDB[:], r=["BM", "IDB"], w=["PS2_0"])
        stt(ZB[:, :, cols], pab[:, 0:1024].rearrange("p (h t) -> p h t", h=8), PRM[:, HNW:HNW + 1], ZB[:, :, cols], ALU.mult, ALU.mult,
            r=["PS2_0", "PRM", zbk], w=[zbk])
        yield
        if not samp:
            tt(DG[:, 0, 0:8], scol(5), V(CST, coff("lastm" + g), [[0, 8]]), ALU.mult, r=[k(5), "CST"], w=["DG"])
            mm(PB[:, 0:8], ones, DG[:, 0, 0:8], True, True, r=["DG", "CST"], w=["PS3_0"])
            act(GL[:, 0, :], PB[:, 0:8], AF.Exp, r=["PS3_0"], w=["GL"])
            for h in range(NH):
                hs = slice(h * 128, (h + 1) * 128)
                mm(PB[:, hs], KTOK[:, h, :], UM[:, h, :], True, True, r=["KTOK", "UM"], w=[pk(PB)[h // 4]])
            for h in range(NH):
                hs = slice(h * 128, (h + 1) * 128)
                stt(SF[:, h, :], SF[:, h, :], GL[:, 0, h:h + 1], PB[:, hs], ALU.mult, ALU.add, r=["SF", "GL"] + pk(PB), w=["SF"])
            cp(SBF[:], SF[:], r=["SF"], w=["SBF"], eng="act")
            if lastp:
                dma_out(deltap.rearrange("h d e -> d h e"), SF[:], r=["SF"], sem="o_sf")
            yield
        else:
            tt(V(DG, 0, [[8, 16], [1, 8]]), V(SC, 5 * 8, [[0, 16], [1, 8]]), V(CST, coff("lastm" + g), [[1, 16], [0, 8]]), ALU.mult,
               r=[k(5), "CST"], w=["DG"])
            mm(PB[:, 0:128], ones, V(DG, 0, [[1, 128]]), True, True, r=["DG", "CST"], w=["PS3_0"])
            act(GL[:], PB[:, 0:128].rearrange("p (s h) -> p s h", h=8), AF.Exp, r=["PS3_0"], w=["GL"])
            RT = [(RR, "RR"), (HF2, "HF2"), (XT[1], "XT1"), (DG, "DG")]
            UBv = V(UBLK, 0, [[128, 16], [1, 128]])
            cnt2 = 0
            for h in range(NH):
                tt(UBv, V(UM, h * 128, [[0, 16], [1, 128]]), V(CST, coff("rowm" + g), [[1, 16], [0, 128]]), ALU.mult,
                   r=["UM", "CST"], w=["MRG"])
                for rnd in range(2):
                    ri = cnt2 % 2
                    cnt2 += 1
                    sf, sfk = RT[ri]
                    sn, snk = RT[2 + ri]
                    Pq = (PA, PB)[ri]
                    dma("pool", V(sf, 0, [[128, 8], [1, 128]]), sdelta[rnd * 8:(rnd + 1) * 8, h, :, :].rearrange("s d e -> d s e"),
                        r=[], w=[sfk], sem="ld_" + sfk.lower())
                    for s8 in range(8):
                        mm(Pq[:, s8 * 128:(s8 + 1) * 128], KTOK[:, h, :], V(UBLK, (rnd * 8 + s8) * 128, [[1, 128]]), True, True,
                           r=["KTOK", "MRG"], w=[pk(Pq)[s8 // 4]])
                    for s8 in range(8):
                        s = rnd * 8 + s8
                        stt(V(sn, s8 * 128, [[1, 128]]), V(sf, s8 * 128, [[1, 128]]), GL[:, s, h:h + 1], Pq[:, s8 * 128:(s8 + 1) * 128],
                            ALU.mult, ALU.add, r=[sfk, "GL"] + pk(Pq), w=[snk])
                    outc[0] += 1
                    dma("pool", deltas[rnd * 8:(rnd + 1) * 8, h, :, :].rearrange("s d e -> d s e"), V(sn, 0, [[128, 8], [1, 128]]),
                        r=[snk], w=["OUT%d" % outc[0]], sem="os_" + snk.lower())
                    yield

    def drain(g):
        for _ in g:
            pass

    def interleave(ga_, gb_, na=1, nb=1):
        a_live = b_live = True
        while a_live or b_live:
            for _ in range(na):
                if a_live:
                    S.cur_boost = 20.0
                    try:
                        next(ga_)
                    except StopIteration:
                        a_live = False
                    S.cur_boost = 0.0
            for _ in range(nb):
                if b_live:
                    try:
                        next(gb_)
                    except StopIteration:
                        b_live = False

    drain(ada(0))
    drain(P(ORDER[0]))
    for pos in range(len(ORDER) - 1):
        interleave(DF(ORDER[pos]), P(ORDER[pos + 1]), 1, 2)
    drain(DF(ORDER[-1]))
    _emit(nc, S, es)
    print('sched est us', round(S.est, 1), 'ops', len(S.ops))
    es.close()
    return nc


def _emit(nc, S, es):
    order = S.schedule()
    csem = {e: es.enter_context(nc.semaphore("c_" + e)) for e in S.ENG}
    dnames = sorted({op.sem for op in S.ops if op.dma})
    dsem = {n: es.enter_context(nc.semaphore("d_" + n)) for n in dnames}
    dcnt = {n: 0 for n in dnames}
    for e in S.ENG:
        c = 0
        for op in order[e]:
            if op.dma:
                dcnt[op.sem] += 16
                op.tick = dcnt[op.sem]
                op.sem = dsem[op.sem]
            elif op.sig:
                c += 1
                op.tick = c
                op.sem = csem[e]
    block = es.enter_context(nc.Block())

    def body(e):
        def f(eng):
            waited = {}
            for op in order[e]:
                need = {}
                for d in op.waits:
                    key = id(d.sem)
                    if need.get(key, (None, 0))[1] < d.tick:
                        need[key] = (d.sem, d.tick)
                for key, (sem, val) in need.items():
                    if waited.get(key, 0) < val:
                        eng.wait_ge(sem, val)
                        waited[key] = val
                ins = op.fn(eng)
                if op.dma:
                    ins.then_inc(op.sem, 16)
                elif op.sig:
                    ins.then_inc(op.sem, 1)
            if e == "sp":
                for n, v in dcnt.items():
                    eng.wait_ge(dsem[n], v)
        return f

    block.tensor(body("pe"))
    block.scalar(body("act"))
    block.vector(body("dve"))
    block.gpsimd(body("pool"))
    block.sync(body("sp"))


_NC_CACHE = {}


def kernel(x_prompt, x_sample, state_pool, state_conv, state_delta, c_prompt, c_sample,
           w_ada, b_ada, w_in, conv_w, a_log, dt_bias, head_norm_w, pool_w, pool_scale,
           p_a, p_b, w_out, ln_g, ln_b, _debug=()):
    f = lambda a: np.ascontiguousarray(np.asarray(a, dtype=np.float32))
    key = tuple(n for n, _ in _debug)
    if key not in _NC_CACHE:
        _NC_CACHE[key] = build(_debug)
    nc = _NC_CACHE[key]
    cst_np, _ = _consts()
    prm = np.zeros((128, 2176), np.float32)
    prm[:, 0:1024] = f(ln_g)[0][None, :]
    prm[:, 1024:2048] = f(ln_b)[0][None, :]
    prm[:, 2048] = f(head_norm_w)[0]
    prm[:, 2049:2057] = f(pool_scale)[0].reshape(8, 128).T
    prm[:, 2057:2153] = f(conv_w)[0].reshape(4, 24, 128).transpose(2, 1, 0).reshape(128, 96)
    prm[:, 2153:2161] = f(a_log)[0][None, :]
    prm[:, 2161:2169] = f(dt_bias)[0][None, :]
    bada = np.ascontiguousarray(np.broadcast_to(f(b_ada)[0][None, :], (128, 3072)))
    msk_np = np.ascontiguousarray(_consts()[1]["_masks"])
    def img(w2d, col0, n=512):
        return np.ascontiguousarray(w2d[:, col0:col0 + n].reshape(8, 128, n).transpose(1, 0, 2))
    w_in2, w_ada2 = f(w_in)[0], f(w_ada)[0]
    groups = [img(w_in2, c0) for c0 in list(range(0, 6144, 512)) + [6160, 6672, 7184, 7696]]
    for w3 in (p_a, p_b, w_out):
        w2 = f(w3)[0]
        groups += [img(w2, 0), img(w2, 512)]
    wimg_np = np.stack(groups)
    wada_np = np.stack([img(w_ada2, g * 512) for g in range(6)])
    wba_np = img(w_in2, 6144, 16)
    shared = dict(msk=msk_np, wimg=wimg_np, wada_img=wada_np, wba_img=wba_np, pool_w=f(pool_w)[0],
                  bada=bada, prm=prm, cst=cst_np)
    xpf, xsf = f(x_prompt), f(x_sample)
    spf, scf, sdf = f(state_pool)[0], f(state_conv)[0], f(state_delta)[0]
    cpf, csf = f(c_prompt), f(c_sample)
    in_maps = []
    for i in range(8):
        cexp = np.empty((2, 128, D), np.float32)
        cexp[0] = cpf[i][None, :]
        cexp[1] = np.tile(csf[16 * i:16 * i + 16], (8, 1))
        m = dict(shared)
        m.update(xp=xpf[i], xs=np.ascontiguousarray(xsf[16 * i:16 * i + 16].transpose(1, 0, 2)).reshape(128, D), cexp=cexp,
                 spool=spf[16 * i:16 * i + 16], sconv=scf[16 * i:16 * i + 16], sdelta=sdf[16 * i:16 * i + 16])
        in_maps.append(m)
    res = run_bass_kernel_spmd(nc, in_maps, core_ids=list(range(8)))
    R = res.results
    y_prompt = np.stack([R[i]["yp"] for i in range(8)])
    y_sample = np.concatenate([R[i]["ys"].reshape(8, 16, D).transpose(1, 0, 2) for i in range(8)])
    pool_prompt = np.stack([R[i]["poolp"] for i in range(8)])[None]
    conv_prompt = np.stack([R[i]["convp"] for i in range(8)])[None]
    delta_prompt = np.stack([R[i]["deltap"] for i in range(8)])[None]
    pool_sample = np.concatenate([R[i]["pools"] for i in range(8)])[None]
    conv_sample = np.concatenate([R[i]["convs"] for i in range(8)])[None]
    delta_sample = np.concatenate([R[i]["deltas"] for i in range(8)])[None]
    if _debug:
        DEBUG.clear()
        DEBUG.update({n: R[0]["dbg_" + n] for n, _ in _debug})
    return (y_prompt, y_sample, pool_prompt, conv_prompt, delta_prompt, pool_sample, conv_sample, delta_sample)
```

```python
from contextlib import ExitStack
import math
import numpy as np
import ml_dtypes
import concourse.bass as bass
import concourse.mybir as mybir
from concourse.bass_utils import run_bass_kernel_spmd

F32 = mybir.dt.float32
BF16 = mybir.dt.bfloat16
ALU = mybir.AluOpType
AF = mybir.ActivationFunctionType

D = 1024
NH = 8
DIN = 8208
NEG = -1.0e5
DEBUG = {}
STRICT_SAME_ENGINE = False


class Op:
    __slots__ = ("eng", "fn", "waits", "sig", "dma", "sem", "tick", "idx", "deps", "cost", "lat", "tbl", "boost")


class Sched:
    ENG = ("pe", "act", "dve", "pool", "sp")

    def __init__(self):
        self.ops = []
        self.lastw = {}
        self.readers = {}
        self.dma_cnt = {}
        self.cur_boost = 0.0

    def add(self, eng, fn, r=(), w=(), dma_sem=None, cost=0.2, lat=0.0, tbl=None):
        op = Op()
        op.tbl = tbl
        op.boost = self.cur_boost
        op.eng, op.fn, op.dma, op.sem = eng, fn, dma_sem is not None, dma_sem
        op.sig = False
        op.idx = len(self.ops)
        op.cost, op.lat = cost, lat
        deps = {}
        for k in r:
            lw = self.lastw.get(k)
            if lw is not None:
                deps[lw.idx] = (lw, True)
        for k in w:
            lw = self.lastw.get(k)
            if lw is not None and lw.idx not in deps:
                deps[lw.idx] = (lw, False)
            for rd in self.readers.get(k, ()):
                if rd.idx not in deps:
                    deps[rd.idx] = (rd, False)
        op.deps = []
        waits = []
        for d, raw in deps.values():
            need = True
            if (not d.dma) and (not op.dma) and d.eng == eng and not raw and not STRICT_SAME_ENGINE:
                need = False
            if (not d.dma) and (not op.dma) and d.eng == eng and eng == "pe":
                need = False
            op.deps.append((d, need))
            if need:
                waits.append(d)
                d.sig = True
        op.waits = waits
        for k in r:
            self.readers.setdefault(k, []).append(op)
        for k in w:
            self.lastw[k] = op
            self.readers[k] = []
        self.ops.append(op)
        return op

    def schedule(self):
        import heapq
        ops = self.ops
        n = len(ops)
        succ = [[] for _ in range(n)]
        indeg = [0] * n
        for op in ops:
            for d, _ in op.deps:
                succ[d.idx].append(op.idx)
                indeg[op.idx] += 1
        bl = [0.0] * n
        for op in reversed(ops):
            m = 0.0
            for s in succ[op.idx]:
                if bl[s] > m:
                    m = bl[s]
            bl[op.idx] = op.cost + op.lat + m
        for op in ops:
            bl[op.idx] += op.boost
        eng_free = {e: 0.0 for e in self.ENG}
        future = {e: [] for e in self.ENG}
        avail = {e: [] for e in self.ENG}
        ready_t = [0.0] * n
        finish = [0.0] * n
        for op in ops:
            if indeg[op.idx] == 0:
                heapq.heappush(future[op.eng], (0.0, op.idx))
        order = {e: [] for e in self.ENG}
        done = 0
        cur_tbl = [None]
        while done < n:
            best_e, best_t = None, None
            for e in self.ENG:
                if avail[e]:
                    t = eng_free[e]
                elif future[e]:
                    t = max(eng_free[e], future[e][0][0])
                else:
                    continue
                if best_t is None or t < best_t:
                    best_e, best_t = e, t
            e, t = best_e, best_t
            while future[e] and future[e][0][0] <= t + 1e-9:
                rt, i = heapq.heappop(future[e])
                heapq.heappush(avail[e], (-bl[i], i))
            extra = 0.0
            if e == "act":
                comp = [x for x in avail[e] if ops[x[1]].tbl is None or ops[x[1]].tbl == cur_tbl[0]]
                if comp:
                    pick = min(comp)
                    avail[e].remove(pick)
                    heapq.heapify(avail[e])
                    i = pick[1]
                else:
                    soon = [x for x in future[e] if x[0] <= t + 0.9 and (ops[x[1]].tbl is None or ops[x[1]].tbl == cur_tbl[0])]
                    if soon:
                        pick = min(soon)
                        future[e].remove(pick)
                        heapq.heapify(future[e])
                        t = pick[0]
                        i = pick[1]
                    else:
                        _, i = heapq.heappop(avail[e])
                        extra = 1.3
                if ops[i].tbl is not None:
                    cur_tbl[0] = ops[i].tbl
            else:
                _, i = heapq.heappop(avail[e])
            op = ops[i]
            order[e].append(op)
            t = t + extra
            eng_free[e] = t + op.cost
            finish[i] = t + op.cost + op.lat
            done += 1
            for s in succ[i]:
                so = ops[s]
                lat = 0.0 if (so.eng == op.eng and not op.dma and not so.dma) else 0.8
                rt = finish[i] + lat
                if rt > ready_t[s]:
                    ready_t[s] = rt
                indeg[s] -= 1
                if indeg[s] == 0:
                    heapq.heappush(future[so.eng], (ready_t[s], s))
        self.order = order
        self.est = max(finish) if n else 0.0
        return order


def _consts():
    cols = {}
    parts = []

    def put(name, a):
        a = np.asarray(a, np.float32)
        off = sum(p.shape[1] for p in parts)
        cols[name] = (off, a.shape[1])
        parts.append(a)

    idx = np.arange(128)
    masks = []
    put("ident", np.eye(128))
    put("ones", np.ones((128, 128)))
    for g in ("p", "s"):
        if g == "p":
            seq = np.zeros(128, np.int64); tim = idx.copy(); nseq = 1
        else:
            seq = idx % 16; tim = idx // 16; nseq = 16
        same = seq[:, None] == seq[None, :]
        tr_ = tim[:, None]
        tc_ = tim[None, :]
        masks.append(np.where(same & (tc_ < tr_), 0.0, -NEG))
        masks.append(np.where(same & (tc_ > tr_), 0.0, NEG))
        masks.append(np.where(same & (tc_ >= tr_), 0.0, NEG))
        put("tri" + g, (same & (tr_ <= tc_)).astype(np.float32))
        put("same" + g, same.astype(np.float32))
        lm = np.zeros((128, 16), np.float32)
        rm = np.zeros((128, 16), np.float32)
        for s in range(nseq):
            members = np.where(seq == s)[0]
            lm[members[np.argmax(tim[members])], s] = 1.0
            rm[members, s] = 1.0
        put("lastm" + g, lm)
        put("rowm" + g, rm)
    put("bd64", ((idx[:, None] // 64) == (idx[None, :] // 64)).astype(np.float32))
    ic = np.zeros((128, 4 * 16), np.float32)
    for wi, w in enumerate((2, 4, 8, 16)):
        ic[:, wi * 16:(wi + 1) * 16] = 1.0 / np.minimum(np.arange(16) + 1, w)
    put("invc", ic)
    cols["_masks"] = np.concatenate([np.asarray(m, np.float32) for m in masks], axis=1)
    return np.concatenate(parts, axis=1), cols


def build(debug=()):
    nc = bass.Bass("TRN2", target_bir_lowering=False)
    S = Sched()
    es = ExitStack()
    cst_np, CC = _consts()
    NCST = cst_np.shape[1]
    LP = 256
    NST = 2048 // LP

    def din(name, shape):
        return nc.dram_tensor(name, list(shape), F32, kind="ExternalInput").ap()

    def dout(name, shape):
        return nc.dram_tensor(name, list(shape), F32, kind="ExternalOutput").ap()

    xp = din("xp", [2048, D]); xs = din("xs", [128, D]); cexp = din("cexp", [2, 128, D])
    spool = din("spool", [16, 15, D]); sconv = din("sconv", [16, 3, 3 * D]); sdelta = din("sdelta", [16, NH, 128, 128])
    wimg = din("wimg", [22, 128, 8, 512]); wada_img = din("wada_img", [6, 128, 8, 512]); wba_img = din("wba_img", [128, 8, 16])
    pool_w = din("pool_w", [4, 256, 256])
    w_ada = w_in = p_a = p_b = w_out = None
    bada = din("bada", [128, 3 * D]); prm = din("prm", [128, 2176]); cst = din("cst", [128, NCST]); msk = din("msk", [128, 768])
    yp = dout("yp", [2048, D]); ys = dout("ys", [128, D]); poolp = dout("poolp", [15, D]); convp = dout("convp", [3, 3 * D])
    deltap = dout("deltap", [NH, 128, 128]); pools = dout("pools", [16, 15, D]); convs = dout("convs", [16, 3, 3 * D])
    deltas = dout("deltas", [16, NH, 128, 128])
    WS = nc.dram_tensor("ws_bf16", [22, 128, 8, 512], BF16, kind="Internal").ap()

    def sb(name, shape, dt=F32):
        return es.enter_context(nc.sbuf_tensor(name, list(shape), dt))

    def V(t, off, dims, np_=128, p0=0):
        base = t[:]
        ps = base.ap[0][0]
        return bass.AP(t, p0 * ps + off, [[ps, np_]] + [list(d) for d in dims])

    CST = sb("CST", [128, NCST]); PRM = sb("PRM", [128, 2176])
    IDB = sb("IDB", [128, 128], BF16); BDB = sb("BDB", [128, 128], BF16)
    MSKB = sb("MSKB", [128, 6, 128], BF16)
    POOLW = sb("POOLW", [128, 4, 2, 256], BF16); WBA = sb("WBA", [128, 8, 16], BF16)
    ADA = [[sb("ADA%d_%d" % (g, i), [128, D], BF16) for i in range(3)] for g in range(2)]
    NEGA = sb("NEGA", [128, 8]); DUMMY = sb("DUMMY", [128, 8])
    NWB = 4
    WB = [sb("WB%d" % i, [128, 8, 512], BF16) for i in range(NWB)]
    SETN = ("ZA", "ZB", "QT", "KT", "VT", "SGA", "SGB")
    SET = [{n: sb("%s%d" % (n, s), [128, 8, LP], BF16) for n in SETN} for s in range(2)]
    BAS = [sb("BA%d" % s, [128, 2, 16]) for s in range(2)]
    SSQS = [sb("SSQ%d" % s, [128, 2, 16]) for s in range(2)]
    HT = sb("HT", [128, 8, LP], BF16); MRG = sb("MRG", [128, 8, LP], BF16)
    XT = [sb("XT%d" % i, [128, D]) for i in range(2)]
    HF = sb("HF", [128, D]); HB = sb("HB", [128, D], BF16)
    EW = 368
    EXTP = sb("EXTP", [128, 2, EW])
    PW = [sb("PW%d" % i, [128, 2, EW]) for i in range(2)]
    POOLED = sb("POOLED", [128, 2, LP], BF16)
    EXTC = [sb("EXTC%d" % i, [128, LP + 3]) for i in range(2)]
    CACC = sb("CACC", [128, LP]); SQ = sb("SQ", [128, LP], BF16); ONB = sb("ONB", [128, 8], BF16); CACC2 = sb("CACC2", [128, LP])
    PHIST = sb("PHIST", [128, 8, 15]); CHIST = sb("CHIST", [128, 24, 3])
    TOK = [sb("TOK%d" % i, [128, 128]) for i in range(2)]
    UTOK = HF
    RR = sb("RR", [128, D]); HF2 = sb("HF2", [128, D]); STAT = sb("STAT", [128, 2, 6]); MV = sb("MV", [128, 4])
    SCB = [sb("SC%d" % i, [128, 18, 8]) for i in range(2)]
    DG = sb("DG", [128, 8, 128])
    TF = DG
    QG = sb("QG", [128, 8, 128], BF16)
    AM = [sb("AM%d" % i, [128, 8, 128], BF16) for i in range(2)]
    AT = [sb("AT%d" % i, [128, 8, 128], BF16) for i in range(2)]
    YM = [sb("YM%d" % i, [128, 8, 128], BF16) for i in range(2)]
    QKT = sb("QKT", [128, 8, 128], BF16)
    ATO = sb("ATO", [128, 8, 128], BF16); RES = sb("RES", [128, 8, 128], BF16)
    KTOK = sb("KTOK", [128, 8, 128], BF16); BV = sb("BV", [128, 8, 128], BF16)
    BM = sb("BM", [128, 8, 128], BF16); UM = sb("UM", [128, 8, 128], BF16)
    SF = sb("SF", [128, 8, 128]); SBF = sb("SBF", [128, 8, 128], BF16)
    JUNK = sb("JUNK", [128, 128], BF16)
    GL = sb("GL", [128, 16, 8])
    S0BH = [(ADA[1][0], "ADA1_0"), (ADA[1][1], "ADA1_1")]
    UBLK = MRG
    PS = [es.enter_context(nc.psum_tensor("PS%d" % i, [128, 1024], F32)) for i in range(4)]

    def psb(i, half):
        return PS[i][:, half * 512:(half + 1) * 512]

    def c_(name, lo=0, hi=None):
        off, n = CC[name]
        hi = n if hi is None else hi
        return CST[:, off + lo:off + hi]

    def coff(name):
        return CC[name][0]

    LNG, LNB, HNW, PSC, CW, ALOG, DTB = 0, 1024, 2048, 2049, 2057, 2153, 2161
    ident = c_("ident"); ones = c_("ones")

    outc = [0]

    TBL = {AF.Exp: 'exp', AF.Ln: 'ln', AF.Silu: 'silu', AF.Sigmoid: 'sig'}

    def fsz(ap):
        n = 1
        for s in ap.shape[1:]:
            n *= s
        return n

    def dma(q, out, in_, r, w, sem):
        nbytes = 4.0 * fsz(out) * out.shape[0]
        S.add(q, lambda e: e.dma_start(out=out, in_=in_), r=r, w=w, dma_sem=sem, cost=0.06, lat=2.0 + nbytes / 150e3)

    def dma_out(out, in_, r, sem):
        outc[0] += 1
        dma("sp", out, in_, r=r, w=["OUT%d" % outc[0]], sem=sem)

    def mm(out, lhsT, rhs, start, stop, r, w):
        npass = 4.0 if lhsT.dtype == F32 else 1.0
        c = max(fsz(out), 48) * npass / 1400.0 + 0.012
        S.add("pe", lambda e: e.matmul(out, lhsT, rhs, start=start, stop=stop), r=r, w=w, cost=c)

    def tr(out, in_, idn, r, w):
        npass = 2.0 if in_.dtype == F32 else 1.0
        c = max(fsz(out), 48) * npass / 1400.0 + 0.012
        S.add("pe", lambda e: e.transpose(out, in_, idn), r=r, w=w, cost=c)

    def act(out, in_, func, r, w, bias=None, scale=None, accum=None):
        kw = {}
        if bias is not None:
            kw["bias"] = bias
        if scale is not None:
            kw["scale"] = scale
        if accum is not None:
            kw["accum_out"] = accum
        tbl = TBL.get(func)
        S.add("act", lambda e: e.activation(out=out, in_=in_, func=func, **kw), r=r, w=w, cost=(200 + fsz(out)) / 1200.0, tbl=tbl)

    def dcost(out, eng="dve"):
        c = (140 + fsz(out)) / 960.0
        return c * 2.0 if eng == "pool" else c

    def tt(out, a, b, op, r, w, eng="dve"):
        S.add(eng, lambda e: e.tensor_tensor(out=out, in0=a, in1=b, op=op), r=r, w=w, cost=dcost(out, eng))

    def ts(out, a, s1, s2, op0, op1, r, w, eng="dve"):
        if op1 is None:
            S.add(eng, lambda e: e.tensor_scalar(out=out, in0=a, scalar1=s1, scalar2=None, op0=op0), r=r, w=w, cost=dcost(out, eng))
        else:
            S.add(eng, lambda e: e.tensor_scalar(out=out, in0=a, scalar1=s1, scalar2=s2, op0=op0, op1=op1), r=r, w=w, cost=dcost(out, eng))

    def stt(out, a, sc, b, op0, op1, r, w, eng="dve"):
        S.add(eng, lambda e: e.scalar_tensor_tensor(out=out, in0=a, scalar=sc, in1=b, op0=op0, op1=op1), r=r, w=w, cost=dcost(out, eng))

    def cp(out, in_, r, w, eng="dve"):
        if eng == "act":
            S.add("act", lambda e: e.copy(out=out, in_=in_), r=r, w=w, cost=(200 + fsz(out)) / 1200.0)
        else:
            S.add(eng, lambda e: e.tensor_copy(out=out, in_=in_), r=r, w=w, cost=dcost(out))

    def ms(out, val, w, eng="dve"):
        S.add(eng, lambda e: e.memset(out, val), r=(), w=w, cost=dcost(out))

    dma("sp", CST[:], cst, r=[], w=["CST"], sem="cst")
    dma("sp", PRM[:], prm, r=[], w=["PRM"], sem="prm")
    dma("pool", POOLW[:], pool_w.rearrange("g (k p) m -> p g k m", p=128), r=[], w=["POOLW"], sem="poolw")
    dma("pool", WBA[:], wba_img, r=[], w=["WBA"], sem="wba")
    cp(IDB[:], ident, r=["CST"], w=["IDB"])
    ms(ONB[:], 1.0, w=["ONB"])
    cp(BDB[:], c_("bd64"), r=["CST"], w=["BDB"])
    ms(PW[0][:], 0.0, w=["PW0"])
    ms(PW[1][:], 0.0, w=["PW1"])
    dma("pool", MSKB[:], msk.rearrange("p (m t) -> p m t", m=6), r=[], w=["MSKB"], sem="mskb")
    act(NEGA[:], PRM[:, ALOG:ALOG + 8], AF.Exp, r=["PRM"], w=["NEGA0"])
    ts(NEGA[:], NEGA[:], -1.0, None, ALU.mult, None, r=["NEGA0"], w=["NEGA"])

    WG = {}
    glist = []
    for col0 in list(range(0, 6144, 512)) + [6160, 6672, 7184, 7696]:
        WG[("w_in", col0)] = len(glist); glist.append((w_in, col0))
    for nm, t in (("p_a", p_a), ("p_b", p_b), ("w_out", w_out)):
        for col0 in (0, 512):
            WG[(nm, col0)] = len(glist); glist.append((t, col0))
    wrot = [0]

    def wload(name, src, col0, direct=False):
        i = wrot[0] % NWB
        wrot[0] += 1
        wk = "WB%d" % i
        if direct:
            img = wada_img[col0 // 512] if name == "w_ada" else wimg[WG[(name, col0)]]
            dma("pool", WB[i][:], img, r=[], w=[wk], sem=wk.lower())
            if (name, col0) in WG:
                gidx = WG[(name, col0)]
                dma("sp", WS[gidx], WB[i][:], r=[wk], w=["WS%d" % gidx], sem="wst%d" % i)
        else:
            gidx = WG[(name, col0)]
            dma("sp", WB[i][:], WS[gidx], r=["WS%d" % gidx], w=[wk], sem=wk.lower() + "h")
        return WB[i], wk

    pmrot = [0]

    PMB = [(psb(0, 0), "PS0_0"), (psb(0, 1), "PS0_1"), (psb(1, 1), "PS1_1")]

    def pm_next():
        i = pmrot[0] % 3
        pmrot[0] += 1
        return PMB[i]

    PT = PS[1][:, 0:256]; PTK = "PS1_0"
    PX = PS[1][:, 256:512]; PXK = "PS1_0"

    ORDER = [0, NST] + list(range(1, NST))
    SETI = {st: pos % 2 for pos, st in enumerate(ORDER)}

    def ada(gi):
        AK = ["ADA%d_%d" % (gi, i) for i in range(3)]
        dma("sp", XT[0][:], cexp[gi], r=[], w=["XT0"], sem="xt0")
        act(HB[:], XT[0][:], AF.Silu, r=["XT0"], w=["HB"])
        ptb = PT.bitcast(BF16)
        for hf in range(2):
            for k4 in range(4):
                k = hf * 4 + k4
                tr(ptb[:, k4 * 128:(k4 + 1) * 128], HB[:, k * 128:(k + 1) * 128], IDB[:], r=["HB", "IDB"], w=[PTK])
            cp(HT[:, hf * 4:(hf + 1) * 4, 0:128], ptb[:, 0:512].rearrange("p (k t) -> p k t", k=4), r=[PTK], w=["HT"])
        for grp in range(6):
            wt, wk = wload("w_ada", w_ada, grp * 512, direct=True)
            dma("sp", HF[:, 0:512], bada[:, grp * 512:(grp + 1) * 512], r=[], w=["HF"], sem="hf")
            pm, pmk = pm_next()
            for k in range(8):
                mm(pm, HT[:, k, 0:128], wt[:, k, :], k == 0, k == 7, r=["HT", wk], w=[pmk])
            which = grp // 2
            dst = ADA[gi][which][:, (grp % 2) * 512:(grp % 2 + 1) * 512]
            if which == 0:
                tt(dst, pm, HF[:, 0:512], ALU.add, r=[pmk, "HF"], w=[AK[which]])
            elif which == 1:
                stt(dst, pm, 1.0, HF[:, 0:512], ALU.add, ALU.add, r=[pmk, "HF"], w=[AK[which]])
            else:
                stt(HF[:, 512:1024], pm, 1.0, HF[:, 0:512], ALU.add, ALU.add, r=[pmk, "HF"], w=["HF"])
                ts(dst, HF[:, 512:1024], 0.5, None, ALU.mult, None, r=["HF"], w=[AK[which]])
            yield

    def P(st):
        samp = st == NST
        gi = 1 if samp else 0
        if samp:
            yield from ada(1)
        si = SETI[st]
        T = SET[si]
        K_ = {n: "%s%d" % (n, si) for n in SETN}
        BA, SSQ = BAS[si], SSQS[si]
        bak, ssk = "BA%d" % si, "SSQ%d" % si
        L = 128 if samp else LP
        ntt = L // 128
        TS_ = 16 if samp else 1
        W = 15 * TS_ + L
        first = st == 0
        last = samp or st == NST - 1
        direct = st == 0
        xsrc = xs if samp else xp[st * LP:(st + 1) * LP]
        shift, sc1, g1 = ADA[gi]
        AK = ["ADA%d_%d" % (gi, i) for i in range(3)]

        for t_ in range(ntt):
            xt, xk = XT[t_ % 2], "XT%d" % (t_ % 2)
            dma("sp", xt[:], xsrc[t_ * 128:(t_ + 1) * 128, :], r=[], w=[xk], sem=xk.lower())
            tt(HF[:], xt[:], sc1[:], ALU.mult, r=[xk, AK[1]], w=["HF"])
            tt(HB[:], HF[:], shift[:], ALU.add, r=["HF", AK[0]], w=["HB"])
            ptb = PT.bitcast(BF16)
            for hf in range(2):
                for k4 in range(4):
                    k = hf * 4 + k4
                    tr(ptb[:, k4 * 128:(k4 + 1) * 128], HB[:, k * 128:(k + 1) * 128], IDB[:], r=["HB", "IDB"], w=[PTK])
                cp(HT[:, hf * 4:(hf + 1) * 4, t_ * 128:(t_ + 1) * 128], ptb[:, 0:512].rearrange("p (k t) -> p k t", k=4), r=[PTK], w=["HT"], eng="act")
            yield
        for t_ in range(ntt):
            for k in range(8):
                mm(PX[:, 0:16], HT[:, k, t_ * 128:(t_ + 1) * 128], WBA[:, k, :], k == 0, k == 7, r=["HT", "WBA"], w=[PXK])
            cp(BA[:, t_, :], PX[:, 0:16], r=[PXK], w=[bak], eng="act")
        if samp:
            for half in range(2):
                ni = 8 if half == 0 else 7
                for ii in range(ni):
                    i = half * 8 + ii
                    dma("sp", XT[half][ii * 16:(ii + 1) * 16, :], spool[:, i, :], r=[], w=["XT%d" % half], sem="xt%d" % half)
            dma_out(pools[:, 0:7, :], spool[:, 8:15, :], r=[], sem="o_dd")
        yield

        def proj(src_cols, consumer):
            for grp in range(2):
                wt, wk = wload("w_in", w_in, src_cols + grp * 512, direct=direct)
                for cc in range(4):
                    c = grp * 4 + cc
                    pm, pmk = pm_next()
                    for k in range(8):
                        mm(pm[:, 0:L], wt[:, k, cc * 128:(cc + 1) * 128], HT[:, k, 0:L], k == 0, k == 7, r=["HT", wk], w=[pmk])
                    consumer(c, pm[:, 0:L], pmk)
                    yield

        def cons_act(dst, dkey, func):
            def f(c, pm, pmk):
                act(dst[:, c, 0:L], pm, func, r=[pmk], w=[dkey])
            return f

        ZA, ZB = T["ZA"], T["ZB"]
        yield from proj(1024, cons_act(ZA, K_["ZA"], AF.Silu))
        ext, ek = EXTP, "EXTP"

        def cons_ua(c, pm, pmk):
            j = c % 2
            hist = V(ext, j * EW, [[1, 15 * TS_]])
            new = V(ext, j * EW + 15 * TS_, [[1, L]])
            hk, nk = ek + "h%d" % j, ek + "n%d" % j
            if first:
                ms(hist, 0.0, w=[hk])
            elif samp:
                tr(PT[:, 0:128], XT[0][:, c * 128:(c + 1) * 128], ident, r=["XT0", "CST"], w=[PTK])
                tr(PT[:, 128:240], XT[1][0:112, c * 128:(c + 1) * 128], ident[0:112, 0:112], r=["XT1", "CST"], w=[PTK])
                cp(hist, PT[:, 0:240], r=[PTK], w=[hk])
            else:
                cp(hist, PHIST[:, c, :], r=["PHIST%d" % c], w=[hk])
            cp(new, pm, r=[pmk], w=[nk], eng="act")
            if not last:
                cp(PHIST[:, c, :], V(ext, j * EW + L, [[1, 15]]), r=[nk], w=["PHIST%d" % c])
            else:
                tr(PT[:, 0:128], V(ext, j * EW + 15 * TS_ + L - 128, [[1, 128]]), ident, r=[nk, "CST"], w=[PTK])
                cp(UTOK[:, c * 128:(c + 1) * 128], PT[:, 0:128], r=[PTK], w=["HF"])
            if j == 1:
                gidx = c // 2
                w_ = 2 << gidx
                ekeys = [ek + "h0", ek + "h1", ek + "n0", ek + "n1"]

                def e2(t, off, n):
                    return V(t, off, [[EW, 2], [1, n]])
                cur, curk = ext, ekeys
                sh = 1
                pi = 0
                while sh < w_:
                    dst, dk = PW[pi], "PW%d" % pi
                    o_ = sh * TS_
                    tt(e2(dst, o_, W - o_), e2(cur, o_, W - o_), e2(cur, 0, W - o_), ALU.add, r=curk, w=[dk])
                    cur, curk = dst, [dk]
                    pi ^= 1
                    sh *= 2
                stt(V(POOLED, 0, [[LP, 2], [1, L]]), e2(cur, 15 * TS_, L), 1.0 / w_, e2(ext, 15 * TS_, L), ALU.mult, ALU.subtract,
                    r=curk + ekeys, w=["POOLED"])
                if first:
                    icv = V(CST, coff("invc") + gidx * 16, [[0, 2], [1, 16]])
                    tt(PW[pi][:, :, 0:16], e2(cur, 15, 16), icv, ALU.mult, r=curk + ["CST"], w=["PW%d" % pi])
                    tt(V(POOLED, 0, [[LP, 2], [1, 16]]), PW[pi][:, :, 0:16], e2(ext, 15, 16), ALU.subtract,
                       r=["PW%d" % pi] + ekeys + ["POOLED"], w=["POOLED"])
                for m in range(2):
                    pm2, pm2k = pm_next()
                    for k in range(2):
                        mm(pm2[:, 0:L], POOLW[:, gidx, k, m * 128:(m + 1) * 128], POOLED[:, k, 0:L], k == 0, k == 1,
                           r=["POOLW", "POOLED"], w=[pm2k])
                    cidx = 2 * gidx + m
                    stt(ZA[:, cidx, 0:L], pm2[:, 0:L], PRM[:, PSC + cidx:PSC + cidx + 1], ZA[:, cidx, 0:L], ALU.mult, ALU.mult,
                        r=[pm2k, "PRM", K_["ZA"]], w=[K_["ZA"]])

        yield from proj(0, cons_ua)
        if last:
            if samp:
                for t in range(8):
                    dma_out(pools[:, 7 + t, :], UTOK[t * 16:(t + 1) * 16, :], r=["HF"], sem="o_utok")
            else:
                dma_out(poolp[:, :], UTOK[113:128, :], r=["HF"], sem="o_utok")
        yield from proj(5120, cons_act(ZB, K_["ZB"], AF.Silu))

        def cons_qkv(typ):
            dst = (T["QT"], T["KT"], T["VT"])[typ]
            dkey = (K_["QT"], K_["KT"], K_["VT"])[typ]

            def f(h, pm, pmk):
                c = typ * 8 + h
                ei = c % 2
                extc, eck = EXTC[ei], "EXTC%d" % ei
                hist = V(extc, 0, [[1, 3 * TS_]])
                new = V(extc, 3 * TS_, [[1, L]])
                if first:
                    ms(hist, 0.0, w=[eck + "h"])
                elif samp:
                    if h == 0:
                        for i in range(3):
                            dma("sp", HF[i * 16:(i + 1) * 16, :], sconv[:, i, typ * 1024:(typ + 1) * 1024], r=[], w=["HF"], sem="hf")
                    tr(PT[:, 0:48], HF[0:48, h * 128:(h + 1) * 128], ident[0:48, 0:48], r=["HF", "CST"], w=[PTK])
                    cp(hist, PT[:, 0:48], r=[PTK], w=[eck + "h"])
                else:
                    cp(hist, CHIST[:, c, :], r=["CHIST%d" % c], w=[eck + "h"])
                cp(new, pm, r=[pmk], w=[eck + "n"], eng="act")
                if not last:
                    cp(CHIST[:, c, :], V(extc, L, [[1, 3]]), r=[eck + "n"], w=["CHIST%d" % c])
                else:
                    tk, tkk = TOK[c % 2], "TOK%d" % (c % 2)
                    tr(PT[:, 0:128], V(extc, 3 * TS_ + L - 128, [[1, 128]]), ident, r=[eck + "n", "CST"], w=[PTK])
                    cp(tk[:], PT[:, 0:128], r=[PTK], w=[tkk])
                    if samp:
                        for i in range(3):
                            dma_out(convs[:, i, c * 128:(c + 1) * 128], tk[(5 + i) * 16:(6 + i) * 16, :], r=[tkk], sem="o_" + tkk.lower())
                    else:
                        dma_out(convp[:, c * 128:(c + 1) * 128], tk[125:128, :], r=[tkk], sem="o_" + tkk.lower())
                acc = CACC[:, 0:L]
                for i in range(4):
                    src = V(extc, i * TS_, [[1, L]])
                    cwc = PRM[:, CW + c * 4 + i:CW + c * 4 + i + 1]
                    if i == 0:
                        ts(acc, src, cwc, None, ALU.mult, None, r=[eck + "h", eck + "n", "PRM"], w=["CACC"])
                    else:
                        stt(acc, src, cwc, acc, ALU.mult, ALU.add, r=[eck + "h", eck + "n", "PRM", "CACC"], w=["CACC"])
                act(dst[:, h, 0:L], acc, AF.Silu, r=["CACC"], w=[dkey])
                if typ < 2:
                    act(SQ[:, 0:L], dst[:, h, 0:L], AF.Square, r=[dkey], w=["SQ"])
                    for t_ in range(ntt):
                        col = 32 + t_ * 16 + typ * 8 + h
                        mm(PX[:, col:col + 1], SQ[:, t_ * 128:(t_ + 1) * 128], ONB[:, 0:1], True, True, r=["SQ", "ONB"], w=[PXK])
            return f

        yield from proj(2048, cons_qkv(0))
        yield from proj(3072, cons_qkv(1))
        cp(SSQ[:, 0:ntt, :], PX[:, 32:32 + ntt * 16].rearrange("p (t c) -> p t c", t=ntt), r=[PXK], w=[ssk])
        yield from proj(4096, cons_qkv(2))
        def cons_tanh(dst, dkey):
            def f(c, pm, pmk):
                act(dst[:, c, 0:L], pm, AF.Tanh, r=[pmk], w=[dkey], scale=0.5)
            return f
        yield from proj(6160, cons_tanh(T["SGA"], K_["SGA"]))
        yield from proj(7184, cons_tanh(T["SGB"], K_["SGB"]))

    def DF(st):
        samp = st == NST
        gi = 1 if samp else 0
        si = SETI[st]
        T = SET[si]
        K_ = {n: "%s%d" % (n, si) for n in SETN}
        L = 128 if samp else LP
        ntt = L // 128
        for t_ in range(ntt):
            yield from delta_tile(st, t_)
        xsrc = xs if samp else xp[st * LP:(st + 1) * LP]
        ydst = ys if samp else yp[st * LP:(st + 1) * LP]
        g1 = ADA[gi][2]
        g1k = "ADA%d_2" % gi
        for bi, (wn, wsrc, ysrc, yk, gsrc, gk) in enumerate((("p_a", p_a, T["ZA"], K_["ZA"], T["SGA"], K_["SGA"]),
                                                             ("p_b", p_b, T["ZB"], K_["ZB"], T["SGB"], K_["SGB"]))):
            for grp in range(2):
                wt, wk = wload(wn, wsrc, grp * 512, direct=(st == 0))
                for cc in range(4):
                    c = grp * 4 + cc
                    pm, pmk = pm_next()
                    for k in range(8):
                        mm(pm[:, 0:L], wt[:, k, cc * 128:(cc + 1) * 128], ysrc[:, k, 0:L], k == 0, k == 7, r=[yk, wk], w=[pmk])
                    if bi == 0:
                        stt(MRG[:, c, 0:L], gsrc[:, c, 0:L], 1.0, pm[:, 0:L], ALU.add, ALU.mult, r=[pmk, gk], w=["MRG"])
                    else:
                        stt(CACC2[:, 0:L], gsrc[:, c, 0:L], 1.0, pm[:, 0:L], ALU.add, ALU.mult, r=[pmk, gk], w=["CACC2"])
                        tt(MRG[:, c, 0:L], MRG[:, c, 0:L], CACC2[:, 0:L], ALU.add, r=["MRG", "CACC2"], w=["MRG"])
                    yield
        wts = [wload("w_out", w_out, nh * 512, direct=(st == 0)) for nh in range(2)]
        for t_ in range(ntt):
            dma("sp", RR[:], xsrc[t_ * 128:(t_ + 1) * 128, :], r=[], w=["RR"], sem="rr")
            for nh in range(2):
                wt, wk = wts[nh]
                pm, pmk = pm_next()
                for k in range(8):
                    mm(pm, MRG[:, k, t_ * 128:(t_ + 1) * 128], wt[:, k, :], k == 0, k == 7, r=["MRG", wk], w=[pmk])
                tt(HF2[:, nh * 512:(nh + 1) * 512], pm, g1[:, nh * 512:(nh + 1) * 512], ALU.mult, r=[pmk, g1k], w=["HF2"])
            stt(RR[:], RR[:], float(2 ** 0.25), HF2[:], ALU.mult, ALU.add, r=["RR", "HF2"], w=["RR"])
            yield
            for i in range(2):
                S.add("dve", (lambda i: lambda e: e.bn_stats(out=STAT[:, i, :], in_=RR[:, i * 512:(i + 1) * 512]))(i), r=["RR"], w=["STAT"], cost=0.6)
            S.add("dve", lambda e: e.bn_aggr(out=MV[:, 0:2], in_=STAT[:]), r=["STAT"], w=["MV"])
            ts(MV[:, 2:3], MV[:, 1:2], 1e-5, None, ALU.add, None, r=["MV"], w=["MV2"])
            act(MV[:, 2:3], MV[:, 2:3], AF.Ln, r=["MV2"], w=["MV2"])
            act(MV[:, 2:3], MV[:, 2:3], AF.Exp, r=["MV2"], w=["MV2"], scale=-0.5)
            ts(RR[:], RR[:], MV[:, 0:1], MV[:, 2:3], ALU.subtract, ALU.mult, r=["RR", "MV", "MV2"], w=["RR"])
            tt(RR[:], RR[:], PRM[:, LNG:LNG + 1024], ALU.mult, r=["RR", "PRM"], w=["RR"])
            tt(HF2[:], RR[:], PRM[:, LNB:LNB + 1024], ALU.add, r=["RR", "PRM"], w=["HF2"])
            dma_out(ydst[t_ * 128:(t_ + 1) * 128, :], HF2[:], r=["HF2"], sem="o_hf2")
            yield

    tcount = [0]

    def delta_tile(st, t_):
        samp = st == NST
        g = "s" if samp else "p"
        si = SETI[st]
        T = SET[si]
        QT, KT_, VT, ZB = T["QT"], T["KT"], T["VT"], T["ZB"]
        qk_, kk_, vk_, zbk = "QT%d" % si, "KT%d" % si, "VT%d" % si, "ZB%d" % si
        BA, SSQ = BAS[si], SSQS[si]
        bak, ssk = "BA%d" % si, "SSQ%d" % si
        nl = 2 if samp else 5
        cols = slice(t_ * 128, (t_ + 1) * 128)
        first = st == 0 and t_ == 0
        lastp = st == NST - 1 and t_ == (LP // 128 - 1)
        PA, PB, PC, PD = PS[2], PS[3], PS[2], PS[3]
        E1, E2, E3 = AM[1], AT[1], YM[1]
        e1k, e2k, e3k = "AM1", "AT1", "YM1"
        GQ = BM
        ON = BM
        pkm = {id(PS[2]): 2, id(PS[3]): 3}

        def pk(P_):
            i = 2 if P_ is PS[2] else 3
            return ["PS%d_0" % i, "PS%d_1" % i]

        def p3(P_):
            return P_[:, :].rearrange("p (h t) -> p h t", h=8)

        tcount[0] += 1
        sbi = tcount[0] % 2
        SC = SCB[sbi]

        def k(i):
            return "SC%dr%d" % (sbi, i)

        def scol(i):
            return SC[:, i, :]

        def sbc(i):
            return V(SC, i * 8, [[1, 8], [0, 128]])
        braw = BA[:, t_, 0:8]; araw = BA[:, t_, 8:16]
        act(scol(0), braw, AF.Exp, r=[bak], w=[k(0)], scale=-1.0)
        tt(scol(1), araw, PRM[:, DTB:DTB + 8], ALU.add, r=[bak, "PRM"], w=[k(1)])
        act(scol(1), scol(1), AF.Exp, r=[k(1)], w=[k(1)])
        ts(SC[:, 0:2, :], SC[:, 0:2, :], 1.0, None, ALU.add, None, r=[k(0), k(1)], w=[k(0), k(1)])
        ts(SC[:, 2:4, :], SSQ[:, t_, :].rearrange("p (a b) -> p a b", a=2), 1e-6, None, ALU.add, None, r=[ssk], w=[k(2), k(3)])
        act(SC[:, 0:4, :], SC[:, 0:4, :], AF.Ln, r=[k(0), k(1), k(2), k(3)], w=[k(0), k(1), k(2), k(3)])
        act(scol(16), scol(0), AF.Exp, r=[k(0)], w=[k(16)], scale=-1.0)
        tt(scol(4), scol(1), NEGA[:], ALU.mult, r=[k(1), "NEGA"], w=[k(4)])
        mm(PD[:, 0:8], c_("tri" + g), scol(4), True, True, r=[k(4), "CST"], w=["PS3_0"])
        mm(PD[:, 8:16], c_("same" + g), scol(4), True, True, r=[k(4), "CST"], w=["PS3_0"])
        cp(SC[:, 5:7, :], PD[:, 0:16].rearrange("p (a b) -> p a b", a=2), r=["PS3_0"], w=[k(5), k(6)])
        ts(scol(7), scol(2), -0.5, math.log(128 ** -0.5), ALU.mult, ALU.add, r=[k(2)], w=[k(7)])
        ts(scol(8), scol(3), -0.5, None, ALU.mult, None, r=[k(3)], w=[k(8)])
        tt(scol(17), scol(5), scol(8), ALU.add, r=[k(5), k(8)], w=[k(17)])
        tt(scol(9), scol(17), scol(0), ALU.subtract, r=[k(17), k(0)], w=[k(9)])
        tt(scol(10), scol(5), scol(8), ALU.subtract, r=[k(5), k(8)], w=[k(10)])
        tt(scol(11), scol(5), scol(7), ALU.add, r=[k(5), k(7)], w=[k(11)])
        ts(scol(12), scol(10), -1.0, None, ALU.mult, None, r=[k(10)], w=[k(12)])
        tt(scol(13), scol(6), scol(10), ALU.subtract, r=[k(6), k(10)], w=[k(13)])
        act(scol(13), scol(13), AF.Exp, r=[k(13)], w=[k(13)])
        act(scol(14), scol(9), AF.Exp, r=[k(9)], w=[k(14)])
        ts(scol(14), scol(14), -1.0, None, ALU.mult, None, r=[k(14)], w=[k(14)])
        yield

        identb = V(CST, coff("ident"), [[0, 8], [1, 128]])

        def rowbc(P_, src_i, skey, maskidx):
            if maskidx is not None:
                for hf in range(2):
                    out = P_[:, hf * 512:(hf + 1) * 512].rearrange("p (h t) -> p h t", h=4)
                    mk = V(MSKB, maskidx * 128, [[0, 4], [1, 128]])
                    mm(out, IDB[:], mk, True, False, r=["MSKB", "IDB"], w=[pk(P_)[hf]])
            for h in range(NH):
                out = P_[:, h * 128:(h + 1) * 128]
                mm(out, V(SC, src_i * 8 + h, [[0, 128]]), ident, maskidx is None, maskidx is None or h % 4 == 3,
                   r=[skey, "CST"], w=[pk(P_)[h // 4]])

        mbase = 3 if samp else 0
        rowbc(PA, 10, k(10), mbase + 0)
        for h in range(NH):
            hs = slice(h * 128, (h + 1) * 128)
            act(E1[:, h, :], PA[:, hs], AF.Exp, r=pk(PA) + [k(9)], w=[e1k], bias=SC[:, 9, h:h + 1], scale=-1.0)
        yield
        rowbc(PC, 11, k(11), None)
        rowbc(PD, 11, k(11), mbase + 2)
        act(GQ[:], p3(PC), AF.Exp, r=pk(PC), w=["BM"])
        for h in range(NH):
            hs = slice(h * 128, (h + 1) * 128)
            act(E3[:, h, :], PD[:, hs], AF.Exp, r=pk(PD) + [k(12)], w=[e3k], bias=SC[:, 12, h:h + 1])
        tt(QG[:], QT[:, :, cols], GQ[:], ALU.mult, r=[qk_, "BM"], w=["QG"])
        yield
        for h in range(NH):
            hs = slice(h * 128, (h + 1) * 128)
            mm(PA[:, hs], KT_[:, h, cols], KT_[:, h, cols], True, True, r=[kk_], w=[pk(PA)[h // 4]])
            mm(PB[:, hs], KT_[:, h, cols], QT[:, h, cols], True, True, r=[kk_, qk_], w=[pk(PB)[h // 4]])
        tt(AM[0][:], p3(PA), E1[:], ALU.mult, r=pk(PA) + [e1k], w=["AM0"])
        tt(QKT[:], p3(PB), E3[:], ALU.mult, r=pk(PB) + [e3k], w=["QKT"])
        pbb = PB[:, :].bitcast(BF16)
        for h in range(NH):
            tr(pbb[:, h * 128:(h + 1) * 128], AM[0][:, h, :], IDB[:], r=["AM0", "IDB"], w=["PS3_0"])
        cp(ATO[:], pbb[:, 0:1024].rearrange("p (h t) -> p h t", h=8), r=["PS3_0"], w=["ATO"], eng="act")
        if samp:
            am1, am1k, at1, at1k = AM[0], "AM0", ATO, "ATO"
        else:
            bdb = V(BDB, 0, [[0, 8], [1, 128]])
            tt(RES[:], AM[0][:], bdb, ALU.mult, r=["AM0", "BDB"], w=["RES"])
            tt(AT[0][:], ATO[:], bdb, ALU.mult, r=["ATO", "BDB"], w=["AT0"])
            am1, am1k, at1, at1k = RES, "RES", AT[0], "AT0"
        tt(YM[0][:], identb, at1[:], ALU.subtract, r=["CST", at1k], w=["YM0"])
        yield
        pcb = PC[:, :].bitcast(BF16)
        for h in range(NH):
            tr(pcb[:, h * 128:(h + 1) * 128], KT_[:, h, cols], IDB[:], r=[kk_, "IDB"], w=["PS2_0"])
            tr(pcb[:, 1024 + h * 128:1024 + (h + 1) * 128], VT[:, h, cols], IDB[:], r=[vk_, "IDB"], w=["PS2_1"])
        tt(KTOK[:], pcb[:, 0:1024].rearrange("p (h t) -> p h t", h=8), sbc(13), ALU.mult, r=["PS2_0", k(13)], w=["KTOK"])
        tt(BV[:], pcb[:, 1024:2048].rearrange("p (h t) -> p h t", h=8), sbc(16), ALU.mult, r=["PS2_1", k(16)], w=["BV"])
        yield
        cur = 0
        for lvl in range(1, nl + 1):
            nxt = cur ^ 1
            atc, atck = (at1, at1k) if lvl == 1 else (AT[cur], "AT%d" % cur)
            amc, amck = (am1, am1k) if lvl == 1 else (AM[cur], "AM%d" % cur)
            for h in range(NH):
                hs = slice(h * 128, (h + 1) * 128)
                mm(PA[:, hs], atc[:, h, :], amc[:, h, :], True, True, r=[atck, amck], w=[pk(PA)[h // 4]])
                if lvl < nl:
                    mm(PB[:, hs], amc[:, h, :], atc[:, h, :], True, True, r=[atck, amck], w=[pk(PB)[h // 4]])
            cp(AM[nxt][:], p3(PA), r=pk(PA), w=["AM%d" % nxt], eng="act")
            if lvl < nl:
                cp(AT[nxt][:], p3(PB), r=pk(PB), w=["AT%d" % nxt], eng="act")
            yield
            for h in range(NH):
                hs = slice(h * 128, (h + 1) * 128)
                mm(PD[:, hs], AM[nxt][:, h, :], YM[cur][:, h, :], True, True, r=["AM%d" % nxt, "YM%d" % cur], w=[pk(PD)[h // 4]])
            tt(YM[nxt][:], p3(PD), YM[cur][:], ALU.add, r=pk(PD) + ["YM%d" % cur], w=["YM%d" % nxt])
            cur = nxt
            yield
        Y, yk = YM[cur], "YM%d" % cur
        if not samp:
            if first:
                ms(SF[:], 0.0, w=["SF"])
                ms(SBF[:], 0.0, w=["SBF"])
            for h in range(NH):
                hs = slice(h * 128, (h + 1) * 128)
                mm(PA[:, hs], KT_[:, h, cols], SBF[:, h, :], True, True, r=[kk_, "SBF"], w=[pk(PA)[h // 4]])
        else:
            for h in range(NH):
                for hf in range(2):
                    dma("pool", V(S0BH[hf][0], 0, [[128, 8], [1, 128]]), sdelta[hf * 8:(hf + 1) * 8, h, :, :].rearrange("s d e -> d s e"),
                        r=[], w=[S0BH[hf][1]], sem="s0b%d" % hf)
                for s in range(16):
                    sbt, sbk = S0BH[s // 8]
                    mm(V(PS[3], h * 128 + s, [[16, 8]]), V(sbt, (s % 8) * 128, [[1, 128]]), V(KT_, h * LP + s, [[16, 8]]), True, True,
                       r=[sbk, kk_], w=[pk(PB)[h // 4]])
                yield
            cp(TF[:], p3(PB), r=pk(PB), w=["DG"])
            for h in range(NH):
                hs = slice(h * 128, (h + 1) * 128)
                tr(PA[:, hs], TF[:, h, :], ident, r=["DG", "CST"], w=[pk(PA)[h // 4]])
        for h in range(NH):
            hs = slice(h * 128, (h + 1) * 128)
            stt(BM[:, h, :], PA[:, hs], SC[:, 14, h:h + 1], BV[:, h, :], ALU.mult, ALU.add, r=pk(PA) + [k(14), "BV"], w=["BM"])
        yield
        for h in range(NH):
            hs = slice(h * 128, (h + 1) * 128)
            mm(PD[:, hs], Y[:, h, :], BM[:, h, :], True, True, r=[yk, "BM"], w=[pk(PD)[h // 4]])
        cp(UM[:], p3(PD), r=pk(PD), w=["UM"], eng="act")
        yield
        if not samp:
            for h in range(NH):
                hs = slice(h * 128, (h + 1) * 128)
                mm(PB[:, hs], ATO[:, h, :], UM[:, h, :], True, True, r=["ATO", "UM"], w=[pk(PB)[h // 4]])
            tt(DG[:], BM[:], UM[:], ALU.subtract, r=["BM", "UM"], w=["DG"])
            tt(RES[:], DG[:], p3(PB), ALU.subtract, r=["DG"] + pk(PB), w=["RES"])
            yield
            for h in range(NH):
                hs = slice(h * 128, (h + 1) * 128)
                mm(PD[:, hs], Y[:, h, :], RES[:, h, :], True, True, r=[yk, "RES"], w=[pk(PD)[h // 4]])
            tt(UM[:], UM[:], p3(PD), ALU.add, r=["UM"] + pk(PD), w=["UM"])
            yield
            for h in range(NH):
                hs = slice(h * 128, (h + 1) * 128)
                mm(PC[:, hs], QG[:, h, :], SBF[:, h, :], True, False, r=["QG", "SBF"], w=[pk(PC)[h // 4]])
                mm(PC[:, hs], QKT[:, h, :], UM[:, h, :], False, True, r=["QKT", "UM"], w=[pk(PC)[h // 4]])
        else:
            for h in range(NH):
                hs = slice(h * 128, (h + 1) * 128)
                for hf in range(2):
                    dma("pool", V(S0BH[hf][0], 0, [[128, 8], [1, 128]]), sdelta[hf * 8:(hf + 1) * 8, h, :, :].rearrange("s d e -> d s e"),
                        r=[], w=[S0BH[hf][1]], sem="s0b%d" % hf)
                mm(PC[:, hs], UM[:, h, :], QKT[:, h, :], True, False, r=["UM", "QKT"], w=[pk(PC)[h // 4]])
                for s in range(16):
                    sbt, sbk = S0BH[s // 8]
                    mm(V(PS[2], h * 128 + s, [[16, 8]]), V(sbt, (s % 8) * 128, [[1, 128]]), V(QG, h * 128 + s, [[16, 8]]), False, s == 15,
                       r=[sbk, "QG"], w=[pk(PC)[h // 4]])
                yield
            cp(TF[:], p3(PC), r=pk(PC), w=["DG"])
            for h in range(NH):
                hs = slice(h * 128, (h + 1) * 128)
                tr(PC[:, hs], TF[:, h, :], ident, r=["DG", "CST"], w=[pk(PC)[h // 4]])
        yield
        ms(scol(15), 0.0, w=[k(15)])
        for h in range(NH):
            hs = slice(h * 128, (h + 1) * 128)
            act(JUNK[:], PC[:, hs], AF.Square, r=pk(PC) + [k(15)], w=["JUNK", k(15)], accum=SC[:, 15, h:h + 1])
        ts(scol(15), scol(15), 1.0 / 128, 1e-6, ALU.mult, ALU.add, r=[k(15)], w=[k(15)])
        act(scol(15), scol(15), AF.Ln, r=[k(15)], w=[k(15)])
        act(scol(15), scol(15), AF.Exp, r=[k(15)], w=[k(15)], scale=-0.5)
        tt(ON[:], p3(PC), sbc(15), ALU.mult, r=pk(PC) + [k(15)], w=["BM"])
        yield
        pab = PA[:, :].bitcast(BF16)
        for h in range(NH):
            tr(pab[:, h * 128:(h + 1) * 128], ON[:, h, :], IDB[:], r=["BM", "IDB"], w=["PS2_0"])
        stt(ZB[:, :, cols], pab[:, 0:1024].rearrange("p (h t) -> p h t", h=8), PRM[:, HNW:HNW + 1], ZB[:, :, cols], ALU.mult, ALU.mult,
            r=["PS2_0", "PRM", zbk], w=[zbk])
        yield
        if not samp:
            tt(DG[:, 0, 0:8], scol(5), V(CST, coff("lastm" + g), [[0, 8]]), ALU.mult, r=[k(5), "CST"], w=["DG"])
            mm(PB[:, 0:8], ones, DG[:, 0, 0:8], True, True, r=["DG", "CST"], w=["PS3_0"])
            act(GL[:, 0, :], PB[:, 0:8], AF.Exp, r=["PS3_0"], w=["GL"])
            for h in range(NH):
                hs = slice(h * 128, (h + 1) * 128)
                mm(PB[:, hs], KTOK[:, h, :], UM[:, h, :], True, True, r=["KTOK", "UM"], w=[pk(PB)[h // 4]])
            for h in range(NH):
                hs = slice(h * 128, (h + 1) * 128)
                stt(SF[:, h, :], SF[:, h, :], GL[:, 0, h:h + 1], PB[:, hs], ALU.mult, ALU.add, r=["SF", "GL"] + pk(PB), w=["SF"])
            cp(SBF[:], SF[:], r=["SF"], w=["SBF"], eng="act")
            if lastp:
                dma_out(deltap.rearrange("h d e -> d h e"), SF[:], r=["SF"], sem="o_sf")
            yield
        else:
            tt(V(DG, 0, [[8, 16], [1, 8]]), V(SC, 5 * 8, [[0, 16], [1, 8]]), V(CST, coff("lastm" + g), [[1, 16], [0, 8]]), ALU.mult,
               r=[k(5), "CST"], w=["DG"])
            mm(PB[:, 0:128], ones, V(DG, 0, [[1, 128]]), True, True, r=["DG", "CST"], w=["PS3_0"])
            act(GL[:], PB[:, 0:128].rearrange("p (s h) -> p s h", h=8), AF.Exp, r=["PS3_0"], w=["GL"])
            RT = [(RR, "RR"), (HF2, "HF2"), (XT[1], "XT1"), (DG, "DG")]
            UBv = V(UBLK, 0, [[128, 16], [1, 128]])
            cnt2 = 0
            for h in range(NH):
                tt(UBv, V(UM, h * 128, [[0, 16], [1, 128]]), V(CST, coff("rowm" + g), [[1, 16], [0, 128]]), ALU.mult,
                   r=["UM", "CST"], w=["MRG"])
                for rnd in range(2):
                    ri = cnt2 % 2
                    cnt2 += 1
                    sf, sfk = RT[ri]
                    sn, snk = RT[2 + ri]
                    Pq = (PA, PB)[ri]
                    dma("pool", V(sf, 0, [[128, 8], [1, 128]]), sdelta[rnd * 8:(rnd + 1) * 8, h, :, :].rearrange("s d e -> d s e"),
                        r=[], w=[sfk], sem="ld_" + sfk.lower())
                    for s8 in range(8):
                        mm(Pq[:, s8 * 128:(s8 + 1) * 128], KTOK[:, h, :], V(UBLK, (rnd * 8 + s8) * 128, [[1, 128]]), True, True,
                           r=["KTOK", "MRG"], w=[pk(Pq)[s8 // 4]])
                    for s8 in range(8):
                        s = rnd * 8 + s8
                        stt(V(sn, s8 * 128, [[1, 128]]), V(sf, s8 * 128, [[1, 128]]), GL[:, s, h:h + 1], Pq[:, s8 * 128:(s8 + 1) * 128],
                            ALU.mult, ALU.add, r=[sfk, "GL"] + pk(Pq), w=[snk])
                    outc[0] += 1
                    dma("pool", deltas[rnd * 8:(rnd + 1) * 8, h, :, :].rearrange("s d e -> d s e"), V(sn, 0, [[128, 8], [1, 128]]),
                        r=[snk], w=["OUT%d" % outc[0]], sem="os_" + snk.lower())
                    yield

    def drain(g):
        for _ in g:
            pass

    def interleave(ga_, gb_, na=1, nb=1):
        a_live = b_live = True
        while a_live or b_live:
            for _ in range(na):
                if a_live:
                    S.cur_boost = 20.0
                    try:
                        next(ga_)
                    except StopIteration:
                        a_live = False
                    S.cur_boost = 0.0
            for _ in range(nb):
                if b_live:
                    try:
                        next(gb_)
                    except StopIteration:
                        b_live = False

    drain(ada(0))
    drain(P(ORDER[0]))
    for pos in range(len(ORDER) - 1):
        interleave(DF(ORDER[pos]), P(ORDER[pos + 1]), 1, 2)
    drain(DF(ORDER[-1]))
    _emit(nc, S, es)
    print('sched est us', round(S.est, 1), 'ops', len(S.ops))
    es.close()
    return nc


def _emit(nc, S, es):
    order = S.schedule()
    csem = {e: es.enter_context(nc.semaphore("c_" + e)) for e in S.ENG}
    dnames = sorted({op.sem for op in S.ops if op.dma})
    dsem = {n: es.enter_context(nc.semaphore("d_" + n)) for n in dnames}
    dcnt = {n: 0 for n in dnames}
    for e in S.ENG:
        c = 0
        for op in order[e]:
            if op.dma:
                dcnt[op.sem] += 16
                op.tick = dcnt[op.sem]
                op.sem = dsem[op.sem]
            elif op.sig:
                c += 1
                op.tick = c
                op.sem = csem[e]
    block = es.enter_context(nc.Block())

    def body(e):
        def f(eng):
            waited = {}
            for op in order[e]:
                need = {}
                for d in op.waits:
                    key = id(d.sem)
                    if need.get(key, (None, 0))[1] < d.tick:
                        need[key] = (d.sem, d.tick)
                for key, (sem, val) in need.items():
                    if waited.get(key, 0) < val:
                        eng.wait_ge(sem, val)
                        waited[key] = val
                ins = op.fn(eng)
                if op.dma:
                    ins.then_inc(op.sem, 16)
                elif op.sig:
                    ins.then_inc(op.sem, 1)
            if e == "sp":
                for n, v in dcnt.items():
                    eng.wait_ge(dsem[n], v)
        return f

    block.tensor(body("pe"))
    block.scalar(body("act"))
    block.vector(body("dve"))
    block.gpsimd(body("pool"))
    block.sync(body("sp"))


_NC_CACHE = {}


def kernel(x_prompt, x_sample, state_pool, state_conv, state_delta, c_prompt, c_sample,
           w_ada, b_ada, w_in, conv_w, a_log, dt_bias, head_norm_w, pool_w, pool_scale,
           p_a, p_b, w_out, ln_g, ln_b, _debug=()):
    f = lambda a: np.ascontiguousarray(np.asarray(a, dtype=np.float32))
    key = tuple(n for n, _ in _debug)
    if key not in _NC_CACHE:
        _NC_CACHE[key] = build(_debug)
    nc = _NC_CACHE[key]
    cst_np, _ = _consts()
    prm = np.zeros((128, 2176), np.float32)
    prm[:, 0:1024] = f(ln_g)[0][None, :]
    prm[:, 1024:2048] = f(ln_b)[0][None, :]
    prm[:, 2048] = f(head_norm_w)[0]
    prm[:, 2049:2057] = f(pool_scale)[0].reshape(8, 128).T
    prm[:, 2057:2153] = f(conv_w)[0].reshape(4, 24, 128).transpose(2, 1, 0).reshape(128, 96)
    prm[:, 2153:2161] = f(a_log)[0][None, :]
    prm[:, 2161:2169] = f(dt_bias)[0][None, :]
    bada = np.ascontiguousarray(np.broadcast_to(f(b_ada)[0][None, :], (128, 3072)))
    msk_np = np.ascontiguousarray(_consts()[1]["_masks"])
    def img(w2d, col0, n=512):
        return np.ascontiguousarray(w2d[:, col0:col0 + n].reshape(8, 128, n).transpose(1, 0, 2))
    w_in2, w_ada2 = f(w_in)[0], f(w_ada)[0]
    groups = [img(w_in2, c0) for c0 in list(range(0, 6144, 512)) + [6160, 6672, 7184, 7696]]
    for w3 in (p_a, p_b, w_out):
        w2 = f(w3)[0]
        groups += [img(w2, 0), img(w2, 512)]
    wimg_np = np.stack(groups)
    wada_np = np.stack([img(w_ada2, g * 512) for g in range(6)])
    wba_np = img(w_in2, 6144, 16)
    shared = dict(msk=msk_np, wimg=wimg_np, wada_img=wada_np, wba_img=wba_np, pool_w=f(pool_w)[0],
                  bada=bada, prm=prm, cst=cst_np)
    xpf, xsf = f(x_prompt), f(x_sample)
    spf, scf, sdf = f(state_pool)[0], f(state_conv)[0], f(state_delta)[0]
    cpf, csf = f(c_prompt), f(c_sample)
    in_maps = []
    for i in range(8):
        cexp = np.empty((2, 128, D), np.float32)
        cexp[0] = cpf[i][None, :]
        cexp[1] = np.tile(csf[16 * i:16 * i + 16], (8, 1))
        m = dict(shared)
        m.update(xp=xpf[i], xs=np.ascontiguousarray(xsf[16 * i:16 * i + 16].transpose(1, 0, 2)).reshape(128, D), cexp=cexp,
                 spool=spf[16 * i:16 * i + 16], sconv=scf[16 * i:16 * i + 16], sdelta=sdf[16 * i:16 * i + 16])
        in_maps.append(m)
    res = run_bass_kernel_spmd(nc, in_maps, core_ids=list(range(8)))
    R = res.results
    y_prompt = np.stack([R[i]["yp"] for i in range(8)])
    y_sample = np.concatenate([R[i]["ys"].reshape(8, 16, D).transpose(1, 0, 2) for i in range(8)])
    pool_prompt = np.stack([R[i]["poolp"] for i in range(8)])[None]
    conv_prompt = np.stack([R[i]["convp"] for i in range(8)])[None]
    delta_prompt = np.stack([R[i]["deltap"] for i in range(8)])[None]
    pool_sample = np.concatenate([R[i]["pools"] for i in range(8)])[None]
    conv_sample = np.concatenate([R[i]["convs"] for i in range(8)])[None]
    delta_sample = np.concatenate([R[i]["deltas"] for i in range(8)])[None]
    if _debug:
        DEBUG.clear()
        DEBUG.update({n: R[0]["dbg_" + n] for n, _ in _debug})
    return (y_prompt, y_sample, pool_prompt, conv_prompt, delta_prompt, pool_sample, conv_sample, delta_sample)
```

```python
from contextlib import ExitStack
import math
import numpy as np
import ml_dtypes
import concourse.bass as bass
import concourse.mybir as mybir
from concourse.bass_utils import run_bass_kernel_spmd

F32 = mybir.dt.float32
BF16 = mybir.dt.bfloat16
ALU = mybir.AluOpType
AF = mybir.ActivationFunctionType

D = 1024
NH = 8
DIN = 8208
NEG = -1.0e5
DEBUG = {}
STRICT_SAME_ENGINE = False


class Op:
    __slots__ = ("eng", "fn", "waits", "sig", "dma", "sem", "tick", "idx", "deps", "cost", "lat", "tbl", "boost")


class Sched:
    ENG = ("pe", "act", "dve", "pool", "sp")

    def __init__(self):
        self.ops = []
        self.lastw = {}
        self.readers = {}
        self.dma_cnt = {}
        self.cur_boost = 0.0

    def add(self, eng, fn, r=(), w=(), dma_sem=None, cost=0.2, lat=0.0, tbl=None):
        op = Op()
        op.tbl = tbl
        op.boost = self.cur_boost
        op.eng, op.fn, op.dma, op.sem = eng, fn, dma_sem is not None, dma_sem
        op.sig = False
        op.idx = len(self.ops)
        op.cost, op.lat = cost, lat
        deps = {}
        for k in r:
            lw = self.lastw.get(k)
            if lw is not None:
                deps[lw.idx] = (lw, True)
        for k in w:
            lw = self.lastw.get(k)
            if lw is not None and lw.idx not in deps:
                deps[lw.idx] = (lw, False)
            for rd in self.readers.get(k, ()):
                if rd.idx not in deps:
                    deps[rd.idx] = (rd, False)
        op.deps = []
        waits = []
        for d, raw in deps.values():
            need = True
            if (not d.dma) and (not op.dma) and d.eng == eng and not raw and not STRICT_SAME_ENGINE:
                need = False
            if (not d.dma) and (not op.dma) and d.eng == eng and eng == "pe":
                need = False
            op.deps.append((d, need))
            if need:
                waits.append(d)
                d.sig = True
        op.waits = waits
        for k in r:
            self.readers.setdefault(k, []).append(op)
        for k in w:
            self.lastw[k] = op
            self.readers[k] = []
        self.ops.append(op)
        return op

    def schedule(self):
        import heapq
        ops = self.ops
        n = len(ops)
        succ = [[] for _ in range(n)]
        indeg = [0] * n
        for op in ops:
            for d, _ in op.deps:
                succ[d.idx].append(op.idx)
                indeg[op.idx] += 1
        bl = [0.0] * n
        for op in reversed(ops):
            m = 0.0
            for s in succ[op.idx]:
                if bl[s] > m:
                    m = bl[s]
            bl[op.idx] = op.cost + op.lat + m
        for op in ops:
            bl[op.idx] += op.boost
        eng_free = {e: 0.0 for e in self.ENG}
        future = {e: [] for e in self.ENG}
        avail = {e: [] for e in self.ENG}
        ready_t = [0.0] * n
        finish = [0.0] * n
        for op in ops:
            if indeg[op.idx] == 0:
                heapq.heappush(future[op.eng], (0.0, op.idx))
        order = {e: [] for e in self.ENG}
        done = 0
        cur_tbl = [None]
        while done < n:
            best_e, best_t = None, None
            for e in self.ENG:
                if avail[e]:
                    t = eng_free[e]
                elif future[e]:
                    t = max(eng_free[e], future[e][0][0])
                else:
                    continue
                if best_t is None or t < best_t:
                    best_e, best_t = e, t
            e, t = best_e, best_t
            while future[e] and future[e][0][0] <= t + 1e-9:
                rt, i = heapq.heappop(future[e])
                heapq.heappush(avail[e], (-bl[i], i))
            extra = 0.0
            if e == "act":
                comp = [x for x in avail[e] if ops[x[1]].tbl is None or ops[x[1]].tbl == cur_tbl[0]]
                if comp:
                    pick = min(comp)
                    avail[e].remove(pick)
                    heapq.heapify(avail[e])
                    i = pick[1]
                else:
                    soon = [x for x in future[e] if x[0] <= t + 0.9 and (ops[x[1]].tbl is None or ops[x[1]].tbl == cur_tbl[0])]
                    if soon:
                        pick = min(soon)
                        future[e].remove(pick)
                        heapq.heapify(future[e])
                        t = pick[0]
                        i = pick[1]
                    else:
                        _, i = heapq.heappop(avail[e])
                        extra = 1.3
                if ops[i].tbl is not None:
                    cur_tbl[0] = ops[i].tbl
            else:
                _, i = heapq.heappop(avail[e])
            op = ops[i]
            order[e].append(op)
            t = t + extra
            eng_free[e] = t + op.cost
            finish[i] = t + op.cost + op.lat
            done += 1
            for s in succ[i]:
                so = ops[s]
                lat = 0.0 if (so.eng == op.eng and not op.dma and not so.dma) else 0.8
                rt = finish[i] + lat
                if rt > ready_t[s]:
                    ready_t[s] = rt
                indeg[s] -= 1
                if indeg[s] == 0:
                    heapq.heappush(future[so.eng], (ready_t[s], s))
        self.order = order
        self.est = max(finish) if n else 0.0
        return order


def _consts():
    cols = {}
    parts = []

    def put(name, a):
        a = np.asarray(a, np.float32)
        off = sum(p.shape[1] for p in parts)
        cols[name] = (off, a.shape[1])
        parts.append(a)

    idx = np.arange(128)
    masks = []
    put("ident", np.eye(128))
    put("ones", np.ones((128, 128)))
    for g in ("p", "s"):
        if g == "p":
            seq = np.zeros(128, np.int64); tim = idx.copy(); nseq = 1
        else:
            seq = idx % 16; tim = idx // 16; nseq = 16
        same = seq[:, None] == seq[None, :]
        tr_ = tim[:, None]
        tc_ = tim[None, :]
        masks.append(np.where(same & (tc_ < tr_), 0.0, -NEG))
        masks.append(np.where(same & (tc_ > tr_), 0.0, NEG))
        masks.append(np.where(same & (tc_ >= tr_), 0.0, NEG))
        put("tri" + g, (same & (tr_ <= tc_)).astype(np.float32))
        put("same" + g, same.astype(np.float32))
        lm = np.zeros((128, 16), np.float32)
        rm = np.zeros((128, 16), np.float32)
        for s in range(nseq):
            members = np.where(seq == s)[0]
            lm[members[np.argmax(tim[members])], s] = 1.0
            rm[members, s] = 1.0
        put("lastm" + g, lm)
        put("rowm" + g, rm)
    ic = np.zeros((128, 4 * 16), np.float32)
    for wi, w in enumerate((2, 4, 8, 16)):
        ic[:, wi * 16:(wi + 1) * 16] = 1.0 / np.minimum(np.arange(16) + 1, w)
    put("invc", ic)
    cols["_masks"] = np.concatenate([np.asarray(m, np.float32) for m in masks], axis=1)
    return np.concatenate(parts, axis=1), cols


def build(debug=()):
    nc = bass.Bass("TRN2", target_bir_lowering=False)
    S = Sched()
    es = ExitStack()
    cst_np, CC = _consts()
    NCST = cst_np.shape[1]
    LP = 256
    NST = 2048 // LP

    def din(name, shape):
        return nc.dram_tensor(name, list(shape), F32, kind="ExternalInput").ap()

    def dout(name, shape):
        return nc.dram_tensor(name, list(shape), F32, kind="ExternalOutput").ap()

    xp = din("xp", [2048, D]); xs = din("xs", [128, D]); cexp = din("cexp", [2, 128, D])
    spool = din("spool", [16, 15, D]); sconv = din("sconv", [16, 3, 3 * D]); sdelta = din("sdelta", [16, NH, 128, 128])
    wimg = din("wimg", [22, 128, 8, 512]); wada_img = din("wada_img", [6, 128, 8, 512]); wba_img = din("wba_img", [128, 8, 16])
    pool_w = din("pool_w", [4, 256, 256])
    w_ada = w_in = p_a = p_b = w_out = None
    bada = din("bada", [128, 3 * D]); prm = din("prm", [128, 2176]); cst = din("cst", [128, NCST]); msk = din("msk", [128, 768])
    yp = dout("yp", [2048, D]); ys = dout("ys", [128, D]); poolp = dout("poolp", [15, D]); convp = dout("convp", [3, 3 * D])
    deltap = dout("deltap", [NH, 128, 128]); pools = dout("pools", [16, 15, D]); convs = dout("convs", [16, 3, 3 * D])
    deltas = dout("deltas", [16, NH, 128, 128])
    WS = nc.dram_tensor("ws_bf16", [22, 128, 8, 512], BF16, kind="Internal").ap()

    def sb(name, shape, dt=F32):
        return es.enter_context(nc.sbuf_tensor(name, list(shape), dt))

    def V(t, off, dims, np_=128, p0=0):
        base = t[:]
        ps = base.ap[0][0]
        return bass.AP(t, p0 * ps + off, [[ps, np_]] + [list(d) for d in dims])

    CST = sb("CST", [128, NCST]); PRM = sb("PRM", [128, 2176])
    IDB = sb("IDB", [128, 128], BF16)
    MSKB = sb("MSKB", [128, 6, 128], BF16)
    POOLW = sb("POOLW", [128, 4, 2, 256], BF16); WBA = sb("WBA", [128, 8, 16], BF16)
    ADA = [[sb("ADA%d_%d" % (g, i), [128, D], BF16) for i in range(3)] for g in range(2)]
    NEGA = sb("NEGA", [128, 8]); DUMMY = sb("DUMMY", [128, 8])
    NWB = 4
    WB = [sb("WB%d" % i, [128, 8, 512], BF16) for i in range(NWB)]
    SETN = ("ZA", "ZB", "QT", "KT", "VT", "SGA", "SGB")
    SET = [{n: sb("%s%d" % (n, s), [128, 8, LP], BF16) for n in SETN} for s in range(2)]
    BAS = [sb("BA%d" % s, [128, 2, 16]) for s in range(2)]
    SSQS = [sb("SSQ%d" % s, [128, 2, 16]) for s in range(2)]
    HT = sb("HT", [128, 8, LP], BF16); MRG = sb("MRG", [128, 8, LP], BF16)
    XT = [sb("XT%d" % i, [128, D]) for i in range(2)]
    HF = sb("HF", [128, D]); HB = sb("HB", [128, D], BF16)
    EW = 368
    EXTP = sb("EXTP", [128, 2, EW])
    PW = [sb("PW%d" % i, [128, 2, EW]) for i in range(2)]
    POOLED = sb("POOLED", [128, 2, LP], BF16)
    EXTC = [sb("EXTC%d" % i, [128, LP + 3]) for i in range(2)]
    CACC = sb("CACC", [128, LP]); SQ = sb("SQ", [128, LP], BF16); ONB = sb("ONB", [128, 8], BF16); CACC2 = sb("CACC2", [128, LP])
    PHIST = sb("PHIST", [128, 8, 15]); CHIST = sb("CHIST", [128, 24, 3])
    TOK = [sb("TOK%d" % i, [128, 128]) for i in range(2)]
    UTOK = HF
    RR = sb("RR", [128, D]); HF2 = sb("HF2", [128, D]); STAT = sb("STAT", [128, 2, 6]); MV = sb("MV", [128, 4])
    SCB = [sb("SC%d" % i, [128, 18, 8]) for i in range(2)]
    DG = sb("DG", [128, 8, 128])
    TF = DG
    QG = sb("QG", [128, 8, 128], BF16)
    AM = [sb("AM%d" % i, [128, 8, 128], BF16) for i in range(2)]
    AT = [sb("AT%d" % i, [128, 8, 128], BF16) for i in range(2)]
    YM = [sb("YM%d" % i, [128, 8, 128], BF16) for i in range(2)]
    QKT = sb("QKT", [128, 8, 128], BF16)
    ATO = sb("ATO", [128, 8, 128], BF16); RES = sb("RES", [128, 8, 128], BF16)
    KTOK = sb("KTOK", [128, 8, 128], BF16); BV = sb("BV", [128, 8, 128], BF16)
    BM = sb("BM", [128, 8, 128], BF16); UM = sb("UM", [128, 8, 128], BF16)
    SF = sb("SF", [128, 8, 128]); SBF = sb("SBF", [128, 8, 128], BF16)
    JUNK = sb("JUNK", [128, 128], BF16)
    GL = sb("GL", [128, 16, 8])
    S0BH = [(ADA[1][0], "ADA1_0"), (ADA[1][1], "ADA1_1")]
    UBLK = MRG
    PS = [es.enter_context(nc.psum_tensor("PS%d" % i, [128, 1024], F32)) for i in range(4)]

    def psb(i, half):
        return PS[i][:, half * 512:(half + 1) * 512]

    def c_(name, lo=0, hi=None):
        off, n = CC[name]
        hi = n if hi is None else hi
        return CST[:, off + lo:off + hi]

    def coff(name):
        return CC[name][0]

    LNG, LNB, HNW, PSC, CW, ALOG, DTB = 0, 1024, 2048, 2049, 2057, 2153, 2161
    ident = c_("ident"); ones = c_("ones")

    outc = [0]

    TBL = {AF.Exp: 'exp', AF.Ln: 'ln', AF.Silu: 'silu', AF.Sigmoid: 'sig'}

    def fsz(ap):
        n = 1
        for s in ap.shape[1:]:
            n *= s
        return n

    def dma(q, out, in_, r, w, sem):
        nbytes = 4.0 * fsz(out) * out.shape[0]
        S.add(q, lambda e: e.dma_start(out=out, in_=in_), r=r, w=w, dma_sem=sem, cost=0.06, lat=2.0 + nbytes / 150e3)

    def dma_out(out, in_, r, sem):
        outc[0] += 1
        dma("sp", out, in_, r=r, w=["OUT%d" % outc[0]], sem=sem)

    def mm(out, lhsT, rhs, start, stop, r, w):
        npass = 4.0 if lhsT.dtype == F32 else 1.0
        c = max(fsz(out), 48) * npass / 1400.0 + 0.012
        S.add("pe", lambda e: e.matmul(out, lhsT, rhs, start=start, stop=stop), r=r, w=w, cost=c)

    def tr(out, in_, idn, r, w):
        npass = 2.0 if in_.dtype == F32 else 1.0
        c = max(fsz(out), 48) * npass / 1400.0 + 0.012
        S.add("pe", lambda e: e.transpose(out, in_, idn), r=r, w=w, cost=c)

    def act(out, in_, func, r, w, bias=None, scale=None, accum=None):
        kw = {}
        if bias is not None:
            kw["bias"] = bias
        if scale is not None:
            kw["scale"] = scale
        if accum is not None:
            kw["accum_out"] = accum
        tbl = TBL.get(func)
        S.add("act", lambda e: e.activation(out=out, in_=in_, func=func, **kw), r=r, w=w, cost=(200 + fsz(out)) / 1200.0, tbl=tbl)

    def dcost(out, eng="dve"):
        c = (140 + fsz(out)) / 960.0
        return c * 2.0 if eng == "pool" else c

    def tt(out, a, b, op, r, w, eng="dve"):
        S.add(eng, lambda e: e.tensor_tensor(out=out, in0=a, in1=b, op=op), r=r, w=w, cost=dcost(out, eng))

    def ts(out, a, s1, s2, op0, op1, r, w, eng="dve"):
        if op1 is None:
            S.add(eng, lambda e: e.tensor_scalar(out=out, in0=a, scalar1=s1, scalar2=None, op0=op0), r=r, w=w, cost=dcost(out, eng))
        else:
            S.add(eng, lambda e: e.tensor_scalar(out=out, in0=a, scalar1=s1, scalar2=s2, op0=op0, op1=op1), r=r, w=w, cost=dcost(out, eng))

    def stt(out, a, sc, b, op0, op1, r, w, eng="dve"):
        S.add(eng, lambda e: e.scalar_tensor_tensor(out=out, in0=a, scalar=sc, in1=b, op0=op0, op1=op1), r=r, w=w, cost=dcost(out, eng))

    def cp(out, in_, r, w, eng="dve"):
        if eng == "act":
            S.add("act", lambda e: e.copy(out=out, in_=in_), r=r, w=w, cost=(200 + fsz(out)) / 1200.0)
        else:
            S.add(eng, lambda e: e.tensor_copy(out=out, in_=in_), r=r, w=w, cost=dcost(out))

    def ms(out, val, w, eng="dve"):
        S.add(eng, lambda e: e.memset(out, val), r=(), w=w, cost=dcost(out))

    dma("sp", CST[:], cst, r=[], w=["CST"], sem="cst")
    dma("sp", PRM[:], prm, r=[], w=["PRM"], sem="prm")
    dma("pool", POOLW[:], pool_w.rearrange("g (k p) m -> p g k m", p=128), r=[], w=["POOLW"], sem="poolw")
    dma("pool", WBA[:], wba_img, r=[], w=["WBA"], sem="wba")
    cp(IDB[:], ident, r=["CST"], w=["IDB"])
    ms(ONB[:], 1.0, w=["ONB"])
    ms(PW[0][:], 0.0, w=["PW0"])
    ms(PW[1][:], 0.0, w=["PW1"])
    dma("pool", MSKB[:], msk.rearrange("p (m t) -> p m t", m=6), r=[], w=["MSKB"], sem="mskb")
    act(NEGA[:], PRM[:, ALOG:ALOG + 8], AF.Exp, r=["PRM"], w=["NEGA0"])
    ts(NEGA[:], NEGA[:], -1.0, None, ALU.mult, None, r=["NEGA0"], w=["NEGA"])

    WG = {}
    glist = []
    for col0 in list(range(0, 6144, 512)) + [6160, 6672, 7184, 7696]:
        WG[("w_in", col0)] = len(glist); glist.append((w_in, col0))
    for nm, t in (("p_a", p_a), ("p_b", p_b), ("w_out", w_out)):
        for col0 in (0, 512):
            WG[(nm, col0)] = len(glist); glist.append((t, col0))
    wrot = [0]

    def wload(name, src, col0, direct=False):
        i = wrot[0] % NWB
        wrot[0] += 1
        wk = "WB%d" % i
        if direct:
            img = wada_img[col0 // 512] if name == "w_ada" else wimg[WG[(name, col0)]]
            dma("pool", WB[i][:], img, r=[], w=[wk], sem=wk.lower())
            if (name, col0) in WG:
                gidx = WG[(name, col0)]
                dma("sp", WS[gidx], WB[i][:], r=[wk], w=["WS%d" % gidx], sem="wst%d" % i)
        else:
            gidx = WG[(name, col0)]
            dma("sp", WB[i][:], WS[gidx], r=["WS%d" % gidx], w=[wk], sem=wk.lower() + "h")
        return WB[i], wk

    pmrot = [0]

    PMB = [(psb(0, 0), "PS0_0"), (psb(0, 1), "PS0_1"), (psb(1, 1), "PS1_1")]

    def pm_next():
        i = pmrot[0] % 3
        pmrot[0] += 1
        return PMB[i]

    PT = PS[1][:, 0:256]; PTK = "PS1_0"
    PX = PS[1][:, 256:512]; PXK = "PS1_0"

    ORDER = [0, NST] + list(range(1, NST))
    SETI = {st: pos % 2 for pos, st in enumerate(ORDER)}

    def ada(gi):
        AK = ["ADA%d_%d" % (gi, i) for i in range(3)]
        dma("sp", XT[0][:], cexp[gi], r=[], w=["XT0"], sem="xt0")
        act(HB[:], XT[0][:], AF.Silu, r=["XT0"], w=["HB"])
        ptb = PT.bitcast(BF16)
        for hf in range(2):
            for k4 in range(4):
                k = hf * 4 + k4
                tr(ptb[:, k4 * 128:(k4 + 1) * 128], HB[:, k * 128:(k + 1) * 128], IDB[:], r=["HB", "IDB"], w=[PTK])
            cp(HT[:, hf * 4:(hf + 1) * 4, 0:128], ptb[:, 0:512].rearrange("p (k t) -> p k t", k=4), r=[PTK], w=["HT"])
        for grp in range(6):
            wt, wk = wload("w_ada", w_ada, grp * 512, direct=True)
            dma("sp", HF[:, 0:512], bada[:, grp * 512:(grp + 1) * 512], r=[], w=["HF"], sem="hf")
            pm, pmk = pm_next()
            for k in range(8):
                mm(pm, HT[:, k, 0:128], wt[:, k, :], k == 0, k == 7, r=["HT", wk], w=[pmk])
            which = grp // 2
            dst = ADA[gi][which][:, (grp % 2) * 512:(grp % 2 + 1) * 512]
            if which == 0:
                tt(dst, pm, HF[:, 0:512], ALU.add, r=[pmk, "HF"], w=[AK[which]])
            elif which == 1:
                stt(dst, pm, 1.0, HF[:, 0:512], ALU.add, ALU.add, r=[pmk, "HF"], w=[AK[which]])
            else:
                stt(HF[:, 512:1024], pm, 1.0, HF[:, 0:512], ALU.add, ALU.add, r=[pmk, "HF"], w=["HF"])
                ts(dst, HF[:, 512:1024], 0.5, None, ALU.mult, None, r=["HF"], w=[AK[which]])
            yield

    def P(st):
        samp = st == NST
        gi = 1 if samp else 0
        if samp:
            yield from ada(1)
        si = SETI[st]
        T = SET[si]
        K_ = {n: "%s%d" % (n, si) for n in SETN}
        BA, SSQ = BAS[si], SSQS[si]
        bak, ssk = "BA%d" % si, "SSQ%d" % si
        L = 128 if samp else LP
        ntt = L // 128
        TS_ = 16 if samp else 1
        W = 15 * TS_ + L
        first = st == 0
        last = samp or st == NST - 1
        direct = st == 0
        xsrc = xs if samp else xp[st * LP:(st + 1) * LP]
        shift, sc1, g1 = ADA[gi]
        AK = ["ADA%d_%d" % (gi, i) for i in range(3)]

        for t_ in range(ntt):
            xt, xk = XT[t_ % 2], "XT%d" % (t_ % 2)
            dma("sp", xt[:], xsrc[t_ * 128:(t_ + 1) * 128, :], r=[], w=[xk], sem=xk.lower())
            tt(HF[:], xt[:], sc1[:], ALU.mult, r=[xk, AK[1]], w=["HF"])
            tt(HB[:], HF[:], shift[:], ALU.add, r=["HF", AK[0]], w=["HB"])
            ptb = PT.bitcast(BF16)
            for hf in range(2):
                for k4 in range(4):
                    k = hf * 4 + k4
                    tr(ptb[:, k4 * 128:(k4 + 1) * 128], HB[:, k * 128:(k + 1) * 128], IDB[:], r=["HB", "IDB"], w=[PTK])
                cp(HT[:, hf * 4:(hf + 1) * 4, t_ * 128:(t_ + 1) * 128], ptb[:, 0:512].rearrange("p (k t) -> p k t", k=4), r=[PTK], w=["HT"], eng="act")
            yield
        for t_ in range(ntt):
            for k in range(8):
                mm(PX[:, 0:16], HT[:, k, t_ * 128:(t_ + 1) * 128], WBA[:, k, :], k == 0, k == 7, r=["HT", "WBA"], w=[PXK])
            cp(BA[:, t_, :], PX[:, 0:16], r=[PXK], w=[bak], eng="act")
        if samp:
            for half in range(2):
                ni = 8 if half == 0 else 7
                for ii in range(ni):
                    i = half * 8 + ii
                    dma("sp", XT[half][ii * 16:(ii + 1) * 16, :], spool[:, i, :], r=[], w=["XT%d" % half], sem="xt%d" % half)
            dma_out(pools[:, 0:7, :], spool[:, 8:15, :], r=[], sem="o_dd")
        yield

        def proj(src_cols, consumer):
            for grp in range(2):
                wt, wk = wload("w_in", w_in, src_cols + grp * 512, direct=direct)
                for cc in range(4):
                    c = grp * 4 + cc
                    pm, pmk = pm_next()
                    for k in range(8):
                        mm(pm[:, 0:L], wt[:, k, cc * 128:(cc + 1) * 128], HT[:, k, 0:L], k == 0, k == 7, r=["HT", wk], w=[pmk])
                    consumer(c, pm[:, 0:L], pmk)
                    yield

        def cons_act(dst, dkey, func):
            def f(c, pm, pmk):
                act(dst[:, c, 0:L], pm, func, r=[pmk], w=[dkey])
            return f

        ZA, ZB = T["ZA"], T["ZB"]
        yield from proj(1024, cons_act(ZA, K_["ZA"], AF.Silu))
        ext, ek = EXTP, "EXTP"

        def cons_ua(c, pm, pmk):
            j = c % 2
            hist = V(ext, j * EW, [[1, 15 * TS_]])
            new = V(ext, j * EW + 15 * TS_, [[1, L]])
            hk, nk = ek + "h%d" % j, ek + "n%d" % j
            if first:
                ms(hist, 0.0, w=[hk])
            elif samp:
                tr(PT[:, 0:128], XT[0][:, c * 128:(c + 1) * 128], ident, r=["XT0", "CST"], w=[PTK])
                tr(PT[:, 128:240], XT[1][0:112, c * 128:(c + 1) * 128], ident[0:112, 0:112], r=["XT1", "CST"], w=[PTK])
                cp(hist, PT[:, 0:240], r=[PTK], w=[hk])
            else:
                cp(hist, PHIST[:, c, :], r=["PHIST%d" % c], w=[hk])
            cp(new, pm, r=[pmk], w=[nk], eng="act")
            if not last:
                cp(PHIST[:, c, :], V(ext, j * EW + L, [[1, 15]]), r=[nk], w=["PHIST%d" % c])
            else:
                tr(PT[:, 0:128], V(ext, j * EW + 15 * TS_ + L - 128, [[1, 128]]), ident, r=[nk, "CST"], w=[PTK])
                cp(UTOK[:, c * 128:(c + 1) * 128], PT[:, 0:128], r=[PTK], w=["HF"])
            if j == 1:
                gidx = c // 2
                w_ = 2 << gidx
                ekeys = [ek + "h0", ek + "h1", ek + "n0", ek + "n1"]

                def e2(t, off, n):
                    return V(t, off, [[EW, 2], [1, n]])
                cur, curk = ext, ekeys
                sh = 1
                pi = 0
                while sh < w_:
                    dst, dk = PW[pi], "PW%d" % pi
                    o_ = sh * TS_
                    tt(e2(dst, o_, W - o_), e2(cur, o_, W - o_), e2(cur, 0, W - o_), ALU.add, r=curk, w=[dk])
                    cur, curk = dst, [dk]
                    pi ^= 1
                    sh *= 2
                stt(V(POOLED, 0, [[LP, 2], [1, L]]), e2(cur, 15 * TS_, L), 1.0 / w_, e2(ext, 15 * TS_, L), ALU.mult, ALU.subtract,
                    r=curk + ekeys, w=["POOLED"])
                if first:
                    icv = V(CST, coff("invc") + gidx * 16, [[0, 2], [1, 16]])
                    tt(PW[pi][:, :, 0:16], e2(cur, 15, 16), icv, ALU.mult, r=curk + ["CST"], w=["PW%d" % pi])
                    tt(V(POOLED, 0, [[LP, 2], [1, 16]]), PW[pi][:, :, 0:16], e2(ext, 15, 16), ALU.subtract,
                       r=["PW%d" % pi] + ekeys + ["POOLED"], w=["POOLED"])
                for m in range(2):
                    pm2, pm2k = pm_next()
                    for k in range(2):
                        mm(pm2[:, 0:L], POOLW[:, gidx, k, m * 128:(m + 1) * 128], POOLED[:, k, 0:L], k == 0, k == 1,
                           r=["POOLW", "POOLED"], w=[pm2k])
                    cidx = 2 * gidx + m
                    stt(ZA[:, cidx, 0:L], pm2[:, 0:L], PRM[:, PSC + cidx:PSC + cidx + 1], ZA[:, cidx, 0:L], ALU.mult, ALU.mult,
                        r=[pm2k, "PRM", K_["ZA"]], w=[K_["ZA"]])

        yield from proj(0, cons_ua)
        if last:
            if samp:
                for t in range(8):
                    dma_out(pools[:, 7 + t, :], UTOK[t * 16:(t + 1) * 16, :], r=["HF"], sem="o_utok")
            else:
                dma_out(poolp[:, :], UTOK[113:128, :], r=["HF"], sem="o_utok")
        yield from proj(5120, cons_act(ZB, K_["ZB"], AF.Silu))

        def cons_qkv(typ):
            dst = (T["QT"], T["KT"], T["VT"])[typ]
            dkey = (K_["QT"], K_["KT"], K_["VT"])[typ]

            def f(h, pm, pmk):
                c = typ * 8 + h
                ei = c % 2
                extc, eck = EXTC[ei], "EXTC%d" % ei
                hist = V(extc, 0, [[1, 3 * TS_]])
                new = V(extc, 3 * TS_, [[1, L]])
                if first:
                    ms(hist, 0.0, w=[eck + "h"])
                elif samp:
                    if h == 0:
                        for i in range(3):
                            dma("sp", HF[i * 16:(i + 1) * 16, :], sconv[:, i, typ * 1024:(typ + 1) * 1024], r=[], w=["HF"], sem="hf")
                    tr(PT[:, 0:48], HF[0:48, h * 128:(h + 1) * 128], ident[0:48, 0:48], r=["HF", "CST"], w=[PTK])
                    cp(hist, PT[:, 0:48], r=[PTK], w=[eck + "h"])
                else:
                    cp(hist, CHIST[:, c, :], r=["CHIST%d" % c], w=[eck + "h"])
                cp(new, pm, r=[pmk], w=[eck + "n"], eng="act")
                if not last:
                    cp(CHIST[:, c, :], V(extc, L, [[1, 3]]), r=[eck + "n"], w=["CHIST%d" % c])
                else:
                    tk, tkk = TOK[c % 2], "TOK%d" % (c % 2)
                    tr(PT[:, 0:128], V(extc, 3 * TS_ + L - 128, [[1, 128]]), ident, r=[eck + "n", "CST"], w=[PTK])
                    cp(tk[:], PT[:, 0:128], r=[PTK], w=[tkk])
                    if samp:
                        for i in range(3):
                            dma_out(convs[:, i, c * 128:(c + 1) * 128], tk[(5 + i) * 16:(6 + i) * 16, :], r=[tkk], sem="o_" + tkk.lower())
                    else:
                        dma_out(convp[:, c * 128:(c + 1) * 128], tk[125:128, :], r=[tkk], sem="o_" + tkk.lower())
                acc = CACC[:, 0:L]
                for i in range(4):
                    src = V(extc, i * TS_, [[1, L]])
                    cwc = PRM[:, CW + c * 4 + i:CW + c * 4 + i + 1]
                    if i == 0:
                        ts(acc, src, cwc, None, ALU.mult, None, r=[eck + "h", eck + "n", "PRM"], w=["CACC"])
                    else:
                        stt(acc, src, cwc, acc, ALU.mult, ALU.add, r=[eck + "h", eck + "n", "PRM", "CACC"], w=["CACC"])
                act(dst[:, h, 0:L], acc, AF.Silu, r=["CACC"], w=[dkey])
                if typ < 2:
                    act(SQ[:, 0:L], dst[:, h, 0:L], AF.Square, r=[dkey], w=["SQ"])
                    for t_ in range(ntt):
                        col = 32 + t_ * 16 + typ * 8 + h
                        mm(PX[:, col:col + 1], SQ[:, t_ * 128:(t_ + 1) * 128], ONB[:, 0:1], True, True, r=["SQ", "ONB"], w=[PXK])
            return f

        yield from proj(2048, cons_qkv(0))
        yield from proj(3072, cons_qkv(1))
        cp(SSQ[:, 0:ntt, :], PX[:, 32:32 + ntt * 16].rearrange("p (t c) -> p t c", t=ntt), r=[PXK], w=[ssk])
        yield from proj(4096, cons_qkv(2))
        def cons_tanh(dst, dkey):
            def f(c, pm, pmk):
                act(dst[:, c, 0:L], pm, AF.Tanh, r=[pmk], w=[dkey], scale=0.5)
            return f
        yield from proj(6160, cons_tanh(T["SGA"], K_["SGA"]))
        yield from proj(7184, cons_tanh(T["SGB"], K_["SGB"]))

    def DF(st):
        samp = st == NST
        gi = 1 if samp else 0
        si = SETI[st]
        T = SET[si]
        K_ = {n: "%s%d" % (n, si) for n in SETN}
        L = 128 if samp else LP
        ntt = L // 128
        bscale[0] = 1.0
        for t_ in range(ntt):
            yield from delta_tile(st, t_)
        bscale[0] = 0.5
        xsrc = xs if samp else xp[st * LP:(st + 1) * LP]
        ydst = ys if samp else yp[st * LP:(st + 1) * LP]
        g1 = ADA[gi][2]
        g1k = "ADA%d_2" % gi
        for bi, (wn, wsrc, ysrc, yk, gsrc, gk) in enumerate((("p_a", p_a, T["ZA"], K_["ZA"], T["SGA"], K_["SGA"]),
                                                             ("p_b", p_b, T["ZB"], K_["ZB"], T["SGB"], K_["SGB"]))):
            for grp in range(2):
                wt, wk = wload(wn, wsrc, grp * 512, direct=(st == 0))
                for cc in range(4):
                    c = grp * 4 + cc
                    pm, pmk = pm_next()
                    for k in range(8):
                        mm(pm[:, 0:L], wt[:, k, cc * 128:(cc + 1) * 128], ysrc[:, k, 0:L], k == 0, k == 7, r=[yk, wk], w=[pmk])
                    if bi == 0:
                        stt(MRG[:, c, 0:L], gsrc[:, c, 0:L], 1.0, pm[:, 0:L], ALU.add, ALU.mult, r=[pmk, gk], w=["MRG"])
                    else:
                        stt(CACC2[:, 0:L], gsrc[:, c, 0:L], 1.0, pm[:, 0:L], ALU.add, ALU.mult, r=[pmk, gk], w=["CACC2"])
                        tt(MRG[:, c, 0:L], MRG[:, c, 0:L], CACC2[:, 0:L], ALU.add, r=["MRG", "CACC2"], w=["MRG"])
                    yield
        wts = [wload("w_out", w_out, nh * 512, direct=(st == 0)) for nh in range(2)]
        for t_ in range(ntt):
            dma("sp", RR[:], xsrc[t_ * 128:(t_ + 1) * 128, :], r=[], w=["RR"], sem="rr")
            for nh in range(2):
                wt, wk = wts[nh]
                pm, pmk = pm_next()
                for k in range(8):
                    mm(pm, MRG[:, k, t_ * 128:(t_ + 1) * 128], wt[:, k, :], k == 0, k == 7, r=["MRG", wk], w=[pmk])
                tt(HF2[:, nh * 512:(nh + 1) * 512], pm, g1[:, nh * 512:(nh + 1) * 512], ALU.mult, r=[pmk, g1k], w=["HF2"])
            stt(RR[:], RR[:], float(2 ** 0.25), HF2[:], ALU.mult, ALU.add, r=["RR", "HF2"], w=["RR"])
            yield
            for i in range(2):
                S.add("dve", (lambda i: lambda e: e.bn_stats(out=STAT[:, i, :], in_=RR[:, i * 512:(i + 1) * 512]))(i), r=["RR"], w=["STAT"], cost=0.6)
            S.add("dve", lambda e: e.bn_aggr(out=MV[:, 0:2], in_=STAT[:]), r=["STAT"], w=["MV"])
            ts(MV[:, 2:3], MV[:, 1:2], 1e-5, None, ALU.add, None, r=["MV"], w=["MV2"])
            act(MV[:, 2:3], MV[:, 2:3], AF.Ln, r=["MV2"], w=["MV2"])
            act(MV[:, 2:3], MV[:, 2:3], AF.Exp, r=["MV2"], w=["MV2"], scale=-0.5)
            ts(RR[:], RR[:], MV[:, 0:1], MV[:, 2:3], ALU.subtract, ALU.mult, r=["RR", "MV", "MV2"], w=["RR"])
            tt(RR[:], RR[:], PRM[:, LNG:LNG + 1024], ALU.mult, r=["RR", "PRM"], w=["RR"])
            tt(HF2[:], RR[:], PRM[:, LNB:LNB + 1024], ALU.add, r=["RR", "PRM"], w=["HF2"])
            dma_out(ydst[t_ * 128:(t_ + 1) * 128, :], HF2[:], r=["HF2"], sem="o_hf2")
            yield

    tcount = [0]

    def delta_tile(st, t_):
        samp = st == NST
        g = "s" if samp else "p"
        si = SETI[st]
        T = SET[si]
        QT, KT_, VT, ZB = T["QT"], T["KT"], T["VT"], T["ZB"]
        qk_, kk_, vk_, zbk = "QT%d" % si, "KT%d" % si, "VT%d" % si, "ZB%d" % si
        BA, SSQ = BAS[si], SSQS[si]
        bak, ssk = "BA%d" % si, "SSQ%d" % si
        nl = 2 if samp else 6
        cols = slice(t_ * 128, (t_ + 1) * 128)
        first = st == 0 and t_ == 0
        lastp = st == NST - 1 and t_ == (LP // 128 - 1)
        PA, PB, PC, PD = PS[2], PS[3], PS[2], PS[3]
        E1, E2, E3 = AM[1], AT[1], YM[1]
        e1k, e2k, e3k = "AM1", "AT1", "YM1"
        GQ = BM
        ON = BM
        pkm = {id(PS[2]): 2, id(PS[3]): 3}

        def pk(P_):
            i = 2 if P_ is PS[2] else 3
            return ["PS%d_0" % i, "PS%d_1" % i]

        def p3(P_):
            return P_[:, :].rearrange("p (h t) -> p h t", h=8)

        tcount[0] += 1
        sbi = tcount[0] % 2
        SC = SCB[sbi]

        def k(i):
            return "SC%dr%d" % (sbi, i)

        def scol(i):
            return SC[:, i, :]

        def sbc(i):
            return V(SC, i * 8, [[1, 8], [0, 128]])
        braw = BA[:, t_, 0:8]; araw = BA[:, t_, 8:16]
        act(scol(0), braw, AF.Exp, r=[bak], w=[k(0)], scale=-1.0)
        tt(scol(1), araw, PRM[:, DTB:DTB + 8], ALU.add, r=[bak, "PRM"], w=[k(1)])
        act(scol(1), scol(1), AF.Exp, r=[k(1)], w=[k(1)])
        ts(SC[:, 0:2, :], SC[:, 0:2, :], 1.0, None, ALU.add, None, r=[k(0), k(1)], w=[k(0), k(1)])
        ts(SC[:, 2:4, :], SSQ[:, t_, :].rearrange("p (a b) -> p a b", a=2), 1e-6, None, ALU.add, None, r=[ssk], w=[k(2), k(3)])
        act(SC[:, 0:4, :], SC[:, 0:4, :], AF.Ln, r=[k(0), k(1), k(2), k(3)], w=[k(0), k(1), k(2), k(3)])
        act(scol(16), scol(0), AF.Exp, r=[k(0)], w=[k(16)], scale=-1.0)
        tt(scol(4), scol(1), NEGA[:], ALU.mult, r=[k(1), "NEGA"], w=[k(4)])
        mm(PD[:, 0:8], c_("tri" + g), scol(4), True, True, r=[k(4), "CST"], w=["PS3_0"])
        mm(PD[:, 8:16], c_("same" + g), scol(4), True, True, r=[k(4), "CST"], w=["PS3_0"])
        cp(SC[:, 5:7, :], PD[:, 0:16].rearrange("p (a b) -> p a b", a=2), r=["PS3_0"], w=[k(5), k(6)])
        ts(scol(7), scol(2), -0.5, math.log(128 ** -0.5), ALU.mult, ALU.add, r=[k(2)], w=[k(7)])
        ts(scol(8), scol(3), -0.5, None, ALU.mult, None, r=[k(3)], w=[k(8)])
        tt(scol(17), scol(5), scol(8), ALU.add, r=[k(5), k(8)], w=[k(17)])
        tt(scol(9), scol(17), scol(0), ALU.subtract, r=[k(17), k(0)], w=[k(9)])
        tt(scol(10), scol(5), scol(8), ALU.subtract, r=[k(5), k(8)], w=[k(10)])
        tt(scol(11), scol(5), scol(7), ALU.add, r=[k(5), k(7)], w=[k(11)])
        ts(scol(12), scol(10), -1.0, None, ALU.mult, None, r=[k(10)], w=[k(12)])
        tt(scol(13), scol(6), scol(10), ALU.subtract, r=[k(6), k(10)], w=[k(13)])
        act(scol(13), scol(13), AF.Exp, r=[k(13)], w=[k(13)])
        act(scol(14), scol(9), AF.Exp, r=[k(9)], w=[k(14)])
        ts(scol(14), scol(14), -1.0, None, ALU.mult, None, r=[k(14)], w=[k(14)])
        yield

        identb = V(CST, coff("ident"), [[0, 8], [1, 128]])

        def rowbc(P_, src_i, skey, maskidx):
            if maskidx is not None:
                for hf in range(2):
                    out = P_[:, hf * 512:(hf + 1) * 512].rearrange("p (h t) -> p h t", h=4)
                    mk = V(MSKB, maskidx * 128, [[0, 4], [1, 128]])
                    mm(out, IDB[:], mk, True, False, r=["MSKB", "IDB"], w=[pk(P_)[hf]])
            for h in range(NH):
                out = P_[:, h * 128:(h + 1) * 128]
                mm(out, V(SC, src_i * 8 + h, [[0, 128]]), ident, maskidx is None, maskidx is None or h % 4 == 3,
                   r=[skey, "CST"], w=[pk(P_)[h // 4]])

        mbase = 3 if samp else 0
        rowbc(PA, 10, k(10), mbase + 0)
        for h in range(NH):
            hs = slice(h * 128, (h + 1) * 128)
            act(E1[:, h, :], PA[:, hs], AF.Exp, r=pk(PA) + [k(9)], w=[e1k], bias=SC[:, 9, h:h + 1], scale=-1.0)
        yield
        rowbc(PC, 11, k(11), None)
        rowbc(PD, 11, k(11), mbase + 2)
        act(GQ[:], p3(PC), AF.Exp, r=pk(PC), w=["BM"])
        for h in range(NH):
            hs = slice(h * 128, (h + 1) * 128)
            act(E3[:, h, :], PD[:, hs], AF.Exp, r=pk(PD) + [k(12)], w=[e3k], bias=SC[:, 12, h:h + 1])
        tt(QG[:], QT[:, :, cols], GQ[:], ALU.mult, r=[qk_, "BM"], w=["QG"])
        yield
        for h in range(NH):
            hs = slice(h * 128, (h + 1) * 128)
            mm(PA[:, hs], KT_[:, h, cols], KT_[:, h, cols], True, True, r=[kk_], w=[pk(PA)[h // 4]])
            mm(PB[:, hs], KT_[:, h, cols], QT[:, h, cols], True, True, r=[kk_, qk_], w=[pk(PB)[h // 4]])
        tt(AM[0][:], p3(PA), E1[:], ALU.mult, r=pk(PA) + [e1k], w=["AM0"])
        tt(QKT[:], p3(PB), E3[:], ALU.mult, r=pk(PB) + [e3k], w=["QKT"])
        pbb = PB[:, :].bitcast(BF16)
        for h in range(NH):
            tr(pbb[:, h * 128:(h + 1) * 128], AM[0][:, h, :], IDB[:], r=["AM0", "IDB"], w=["PS3_0"])
        cp(ATO[:], pbb[:, 0:1024].rearrange("p (h t) -> p h t", h=8), r=["PS3_0"], w=["ATO"], eng="act")
        tt(YM[0][:], identb, ATO[:], ALU.subtract, r=["CST", "ATO"], w=["YM0"])
        yield
        pcb = PC[:, :].bitcast(BF16)
        for h in range(NH):
            tr(pcb[:, h * 128:(h + 1) * 128], KT_[:, h, cols], IDB[:], r=[kk_, "IDB"], w=["PS2_0"])
            tr(pcb[:, 1024 + h * 128:1024 + (h + 1) * 128], VT[:, h, cols], IDB[:], r=[vk_, "IDB"], w=["PS2_1"])
        tt(KTOK[:], pcb[:, 0:1024].rearrange("p (h t) -> p h t", h=8), sbc(13), ALU.mult, r=["PS2_0", k(13)], w=["KTOK"])
        tt(BV[:], pcb[:, 1024:2048].rearrange("p (h t) -> p h t", h=8), sbc(16), ALU.mult, r=["PS2_1", k(16)], w=["BV"])
        yield
        cur = 0
        for lvl in range(1, nl + 1):
            nxt = cur ^ 1
            atc, atck = (ATO, "ATO") if lvl == 1 else (AT[cur], "AT%d" % cur)
            for h in range(NH):
                hs = slice(h * 128, (h + 1) * 128)
                mm(PA[:, hs], atc[:, h, :], AM[cur][:, h, :], True, True, r=[atck, "AM%d" % cur], w=[pk(PA)[h // 4]])
                if lvl < nl:
                    mm(PB[:, hs], AM[cur][:, h, :], atc[:, h, :], True, True, r=[atck, "AM%d" % cur], w=[pk(PB)[h // 4]])
            cp(AM[nxt][:], p3(PA), r=pk(PA), w=["AM%d" % nxt], eng="act")
            if lvl < nl:
                cp(AT[nxt][:], p3(PB), r=pk(PB), w=["AT%d" % nxt], eng="act")
            yield
            for h in range(NH):
                hs = slice(h * 128, (h + 1) * 128)
                mm(PD[:, hs], AM[nxt][:, h, :], YM[cur][:, h, :], True, True, r=["AM%d" % nxt, "YM%d" % cur], w=[pk(PD)[h // 4]])
            tt(YM[nxt][:], p3(PD), YM[cur][:], ALU.add, r=pk(PD) + ["YM%d" % cur], w=["YM%d" % nxt])
            cur = nxt
            yield
        Y, yk = YM[cur], "YM%d" % cur
        if not samp:
            if first:
                ms(SF[:], 0.0, w=["SF"])
                ms(SBF[:], 0.0, w=["SBF"])
            for h in range(NH):
                hs = slice(h * 128, (h + 1) * 128)
                mm(PA[:, hs], KT_[:, h, cols], SBF[:, h, :], True, True, r=[kk_, "SBF"], w=[pk(PA)[h // 4]])
        else:
            for h in range(NH):
                for hf in range(2):
                    dma("pool", V(S0BH[hf][0], 0, [[128, 8], [1, 128]]), sdelta[hf * 8:(hf + 1) * 8, h, :, :].rearrange("s d e -> d s e"),
                        r=[], w=[S0BH[hf][1]], sem="s0b%d" % hf)
                for s in range(16):
                    sbt, sbk = S0BH[s // 8]
                    mm(V(PS[3], h * 128 + s, [[16, 8]]), V(sbt, (s % 8) * 128, [[1, 128]]), V(KT_, h * LP + s, [[16, 8]]), True, True,
                       r=[sbk, kk_], w=[pk(PB)[h // 4]])
                yield
            cp(TF[:], p3(PB), r=pk(PB), w=["DG"])
            for h in range(NH):
                hs = slice(h * 128, (h + 1) * 128)
                tr(PA[:, hs], TF[:, h, :], ident, r=["DG", "CST"], w=[pk(PA)[h // 4]])
        for h in range(NH):
            hs = slice(h * 128, (h + 1) * 128)
            stt(BM[:, h, :], PA[:, hs], SC[:, 14, h:h + 1], BV[:, h, :], ALU.mult, ALU.add, r=pk(PA) + [k(14), "BV"], w=["BM"])
        yield
        for h in range(NH):
            hs = slice(h * 128, (h + 1) * 128)
            mm(PD[:, hs], Y[:, h, :], BM[:, h, :], True, True, r=[yk, "BM"], w=[pk(PD)[h // 4]])
        cp(UM[:], p3(PD), r=pk(PD), w=["UM"], eng="act")
        yield
        if not samp:
            for h in range(NH):
                hs = slice(h * 128, (h + 1) * 128)
                mm(PB[:, hs], ATO[:, h, :], UM[:, h, :], True, True, r=["ATO", "UM"], w=[pk(PB)[h // 4]])
            tt(DG[:], BM[:], UM[:], ALU.subtract, r=["BM", "UM"], w=["DG"])
            tt(RES[:], DG[:], p3(PB), ALU.subtract, r=["DG"] + pk(PB), w=["RES"])
            yield
            for h in range(NH):
                hs = slice(h * 128, (h + 1) * 128)
                mm(PD[:, hs], Y[:, h, :], RES[:, h, :], True, True, r=[yk, "RES"], w=[pk(PD)[h // 4]])
            tt(UM[:], UM[:], p3(PD), ALU.add, r=["UM"] + pk(PD), w=["UM"])
            yield
            for h in range(NH):
                hs = slice(h * 128, (h + 1) * 128)
                mm(PC[:, hs], QG[:, h, :], SBF[:, h, :], True, False, r=["QG", "SBF"], w=[pk(PC)[h // 4]])
                mm(PC[:, hs], QKT[:, h, :], UM[:, h, :], False, True, r=["QKT", "UM"], w=[pk(PC)[h // 4]])
        else:
            for h in range(NH):
                hs = slice(h * 128, (h + 1) * 128)
                for hf in range(2):
                    dma("pool", V(S0BH[hf][0], 0, [[128, 8], [1, 128]]), sdelta[hf * 8:(hf + 1) * 8, h, :, :].rearrange("s d e -> d s e"),
                        r=[], w=[S0BH[hf][1]], sem="s0b%d" % hf)
                mm(PC[:, hs], UM[:, h, :], QKT[:, h, :], True, False, r=["UM", "QKT"], w=[pk(PC)[h // 4]])
                for s in range(16):
                    sbt, sbk = S0BH[s // 8]
                    mm(V(PS[2], h * 128 + s, [[16, 8]]), V(sbt, (s % 8) * 128, [[1, 128]]), V(QG, h * 128 + s, [[16, 8]]), False, s == 15,
                       r=[sbk, "QG"], w=[pk(PC)[h // 4]])
                yield
            cp(TF[:], p3(PC), r=pk(PC), w=["DG"])
            for h in range(NH):
                hs = slice(h * 128, (h + 1) * 128)
                tr(PC[:, hs], TF[:, h, :], ident, r=["DG", "CST"], w=[pk(PC)[h // 4]])
        yield
        ms(scol(15), 0.0, w=[k(15)])
        for h in range(NH):
            hs = slice(h * 128, (h + 1) * 128)
            act(JUNK[:], PC[:, hs], AF.Square, r=pk(PC) + [k(15)], w=["JUNK", k(15)], accum=SC[:, 15, h:h + 1])
        ts(scol(15), scol(15), 1.0 / 128, 1e-6, ALU.mult, ALU.add, r=[k(15)], w=[k(15)])
        act(scol(15), scol(15), AF.Ln, r=[k(15)], w=[k(15)])
        act(scol(15), scol(15), AF.Exp, r=[k(15)], w=[k(15)], scale=-0.5)
        tt(ON[:], p3(PC), sbc(15), ALU.mult, r=pk(PC) + [k(15)], w=["BM"])
        yield
        pab = PA[:, :].bitcast(BF16)
        for h in range(NH):
            tr(pab[:, h * 128:(h + 1) * 128], ON[:, h, :], IDB[:], r=["BM", "IDB"], w=["PS2_0"])
        stt(ZB[:, :, cols], pab[:, 0:1024].rearrange("p (h t) -> p h t", h=8), PRM[:, HNW:HNW + 1], ZB[:, :, cols], ALU.mult, ALU.mult,
            r=["PS2_0", "PRM", zbk], w=[zbk])
        yield
        if not samp:
            tt(DG[:, 0, 0:8], scol(5), V(CST, coff("lastm" + g), [[0, 8]]), ALU.mult, r=[k(5), "CST"], w=["DG"])
            mm(PB[:, 0:8], ones, DG[:, 0, 0:8], True, True, r=["DG", "CST"], w=["PS3_0"])
            act(GL[:, 0, :], PB[:, 0:8], AF.Exp, r=["PS3_0"], w=["GL"])
            for h in range(NH):
                hs = slice(h * 128, (h + 1) * 128)
                mm(PB[:, hs], KTOK[:, h, :], UM[:, h, :], True, True, r=["KTOK", "UM"], w=[pk(PB)[h // 4]])
            for h in range(NH):
                hs = slice(h * 128, (h + 1) * 128)
                stt(SF[:, h, :], SF[:, h, :], GL[:, 0, h:h + 1], PB[:, hs], ALU.mult, ALU.add, r=["SF", "GL"] + pk(PB), w=["SF"])
            cp(SBF[:], SF[:], r=["SF"], w=["SBF"], eng="act")
            if lastp:
                dma_out(deltap.rearrange("h d e -> d h e"), SF[:], r=["SF"], sem="o_sf")
            yield
        else:
            tt(V(DG, 0, [[8, 16], [1, 8]]), V(SC, 5 * 8, [[0, 16], [1, 8]]), V(CST, coff("lastm" + g), [[1, 16], [0, 8]]), ALU.mult,
               r=[k(5), "CST"], w=["DG"])
            mm(PB[:, 0:128], ones, V(DG, 0, [[1, 128]]), True, True, r=["DG", "CST"], w=["PS3_0"])
            act(GL[:], PB[:, 0:128].rearrange("p (s h) -> p s h", h=8), AF.Exp, r=["PS3_0"], w=["GL"])
            RT = [(RR, "RR"), (HF2, "HF2"), (XT[1], "XT1"), (DG, "DG")]
            UBv = V(UBLK, 0, [[128, 16], [1, 128]])
            cnt2 = 0
            for h in range(NH):
                tt(UBv, V(UM, h * 128, [[0, 16], [1, 128]]), V(CST, coff("rowm" + g), [[1, 16], [0, 128]]), ALU.mult,
                   r=["UM", "CST"], w=["MRG"])
                for rnd in range(2):
                    ri = cnt2 % 2
                    cnt2 += 1
                    sf, sfk = RT[ri]
                    sn, snk = RT[2 + ri]
                    Pq = (PA, PB)[ri]
                    dma("pool", V(sf, 0, [[128, 8], [1, 128]]), sdelta[rnd * 8:(rnd + 1) * 8, h, :, :].rearrange("s d e -> d s e"),
                        r=[], w=[sfk], sem="ld_" + sfk.lower())
                    for s8 in range(8):
                        mm(Pq[:, s8 * 128:(s8 + 1) * 128], KTOK[:, h, :], V(UBLK, (rnd * 8 + s8) * 128, [[1, 128]]), True, True,
                           r=["KTOK", "MRG"], w=[pk(Pq)[s8 // 4]])
                    for s8 in range(8):
                        s = rnd * 8 + s8
                        stt(V(sn, s8 * 128, [[1, 128]]), V(sf, s8 * 128, [[1, 128]]), GL[:, s, h:h + 1], Pq[:, s8 * 128:(s8 + 1) * 128],
                            ALU.mult, ALU.add, r=[sfk, "GL"] + pk(Pq), w=[snk])
                    outc[0] += 1
                    dma("pool", deltas[rnd * 8:(rnd + 1) * 8, h, :, :].rearrange("s d e -> d s e"), V(sn, 0, [[128, 8], [1, 128]]),
                        r=[snk], w=["OUT%d" % outc[0]], sem="os_" + snk.lower())
                    yield

    bscale = [1.0]

    def drain(g):
        for _ in g:
            pass

    def interleave(ga_, gb_, na=1, nb=1):
        a_live = b_live = True
        while a_live or b_live:
            for _ in range(na):
                if a_live:
                    S.cur_boost = 20.0 * bscale[0]
                    try:
                        next(ga_)
                    except StopIteration:
                        a_live = False
                    S.cur_boost = 0.0
            for _ in range(nb):
                if b_live:
                    try:
                        next(gb_)
                    except StopIteration:
                        b_live = False

    drain(ada(0))
    drain(P(ORDER[0]))
    for pos in range(len(ORDER) - 1):
        interleave(DF(ORDER[pos]), P(ORDER[pos + 1]), 1, 2)
    drain(DF(ORDER[-1]))
    _emit(nc, S, es)
    print('sched est us', round(S.est, 1), 'ops', len(S.ops))
    es.close()
    return nc


def _emit(nc, S, es):
    order = S.schedule()
    csem = {e: es.enter_context(nc.semaphore("c_" + e)) for e in S.ENG}
    dnames = sorted({op.sem for op in S.ops if op.dma})
    dsem = {n: es.enter_context(nc.semaphore("d_" + n)) for n in dnames}
    dcnt = {n: 0 for n in dnames}
    for e in S.ENG:
        c = 0
        for op in order[e]:
            if op.dma:
                dcnt[op.sem] += 16
                op.tick = dcnt[op.sem]
                op.sem = dsem[op.sem]
            elif op.sig:
                c += 1
                op.tick = c
                op.sem = csem[e]
    block = es.enter_context(nc.Block())

    def body(e):
        def f(eng):
            waited = {}
            for op in order[e]:
                need = {}
                for d in op.waits:
                    key = id(d.sem)
                    if need.get(key, (None, 0))[1] < d.tick:
                        need[key] = (d.sem, d.tick)
                for key, (sem, val) in need.items():
                    if waited.get(key, 0) < val:
                        eng.wait_ge(sem, val)
                        waited[key] = val
                ins = op.fn(eng)
                if op.dma:
                    ins.then_inc(op.sem, 16)
                elif op.sig:
                    ins.then_inc(op.sem, 1)
            if e == "sp":
                for n, v in dcnt.items():
                    eng.wait_ge(dsem[n], v)
        return f

    block.tensor(body("pe"))
    block.scalar(body("act"))
    block.vector(body("dve"))
    block.gpsimd(body("pool"))
    block.sync(body("sp"))


_NC_CACHE = {}


def kernel(x_prompt, x_sample, state_pool, state_conv, state_delta, c_prompt, c_sample,
           w_ada, b_ada, w_in, conv_w, a_log, dt_bias, head_norm_w, pool_w, pool_scale,
           p_a, p_b, w_out, ln_g, ln_b, _debug=()):
    f = lambda a: np.ascontiguousarray(np.asarray(a, dtype=np.float32))
    key = tuple(n for n, _ in _debug)
    if key not in _NC_CACHE:
        _NC_CACHE[key] = build(_debug)
    nc = _NC_CACHE[key]
    cst_np, _ = _consts()
    prm = np.zeros((128, 2176), np.float32)
    prm[:, 0:1024] = f(ln_g)[0][None, :]
    prm[:, 1024:2048] = f(ln_b)[0][None, :]
    prm[:, 2048] = f(head_norm_w)[0]
    prm[:, 2049:2057] = f(pool_scale)[0].reshape(8, 128).T
    prm[:, 2057:2153] = f(conv_w)[0].reshape(4, 24, 128).transpose(2, 1, 0).reshape(128, 96)
    prm[:, 2153:2161] = f(a_log)[0][None, :]
    prm[:, 2161:2169] = f(dt_bias)[0][None, :]
    bada = np.ascontiguousarray(np.broadcast_to(f(b_ada)[0][None, :], (128, 3072)))
    msk_np = np.ascontiguousarray(_consts()[1]["_masks"])
    def img(w2d, col0, n=512):
        return np.ascontiguousarray(w2d[:, col0:col0 + n].reshape(8, 128, n).transpose(1, 0, 2))
    w_in2, w_ada2 = f(w_in)[0], f(w_ada)[0]
    groups = [img(w_in2, c0) for c0 in list(range(0, 6144, 512)) + [6160, 6672, 7184, 7696]]
    for w3 in (p_a, p_b, w_out):
        w2 = f(w3)[0]
        groups += [img(w2, 0), img(w2, 512)]
    wimg_np = np.stack(groups)
    wada_np = np.stack([img(w_ada2, g * 512) for g in range(6)])
    wba_np = img(w_in2, 6144, 16)
    shared = dict(msk=msk_np, wimg=wimg_np, wada_img=wada_np, wba_img=wba_np, pool_w=f(pool_w)[0],
                  bada=bada, prm=prm, cst=cst_np)
    xpf, xsf = f(x_prompt), f(x_sample)
    spf, scf, sdf = f(state_pool)[0], f(state_conv)[0], f(state_delta)[0]
    cpf, csf = f(c_prompt), f(c_sample)
    in_maps = []
    for i in range(8):
        cexp = np.empty((2, 128, D), np.float32)
        cexp[0] = cpf[i][None, :]
        cexp[1] = np.tile(csf[16 * i:16 * i + 16], (8, 1))
        m = dict(shared)
        m.update(xp=xpf[i], xs=np.ascontiguousarray(xsf[16 * i:16 * i + 16].transpose(1, 0, 2)).reshape(128, D), cexp=cexp,
                 spool=spf[16 * i:16 * i + 16], sconv=scf[16 * i:16 * i + 16], sdelta=sdf[16 * i:16 * i + 16])
        in_maps.append(m)
    res = run_bass_kernel_spmd(nc, in_maps, core_ids=list(range(8)))
    R = res.results
    y_prompt = np.stack([R[i]["yp"] for i in range(8)])
    y_sample = np.concatenate([R[i]["ys"].reshape(8, 16, D).transpose(1, 0, 2) for i in range(8)])
    pool_prompt = np.stack([R[i]["poolp"] for i in range(8)])[None]
    conv_prompt = np.stack([R[i]["convp"] for i in range(8)])[None]
    delta_prompt = np.stack([R[i]["deltap"] for i in range(8)])[None]
    pool_sample = np.concatenate([R[i]["pools"] for i in range(8)])[None]
    conv_sample = np.concatenate([R[i]["convs"] for i in range(8)])[None]
    delta_sample = np.concatenate([R[i]["deltas"] for i in range(8)])[None]
    if _debug:
        DEBUG.clear()
        DEBUG.update({n: R[0]["dbg_" + n] for n, _ in _debug})
    return (y_prompt, y_sample, pool_prompt, conv_prompt, delta_prompt, pool_sample, conv_sample, delta_sample)
```
